# Optimizing a Trainium2 kernel written in Bass

```python
import jax
import jax.numpy as jnp
from jax import lax
import numpy as np

D_MODEL = 1024
BATCH = 32
SEQ = 256
DEPTH = 2
DEC_BATCH = 2
DEC_SEQ = 2048
PAST_LEN = 256

GRID_W = 64
Q_BLOCK = 128
ROPE_THETA = 10000.0
EPS = 1e-6
GATE_FLOOR = 1e-30
HG_HEADS = 4
HG_DK = 64
HG_DV = 64
HG_CHUNK = 16
MLA_HEADS = 6
MLA_NOPE = 64
MLA_ROPE = 32
MLA_V = 64
MLA_Q_RANK = 256
MLA_KV_RANK = 128
GQA_HEADS = 6
GQA_KV_HEADS = 2
GQA_GROUP = GQA_HEADS // GQA_KV_HEADS
GQA_HD = 64
HG_W = HG_HEADS * HG_DV
MLA_W = MLA_HEADS * MLA_V
GQA_W = GQA_HEADS * GQA_HD
MIX_W = HG_W + MLA_W + GQA_W
D_FF = -(-8 * D_MODEL // (3 * 256)) * 256
MLA_SCALE = (MLA_NOPE + MLA_ROPE) ** -0.5
GQA_SCALE = GQA_HD ** -0.5
ALPHA = (2 * DEPTH) ** 0.25
BETA = (8 * DEPTH) ** -0.25
IN_SIZES = (HG_HEADS * HG_DK, HG_HEADS * HG_DK, HG_HEADS * HG_DK, HG_HEADS * HG_DV, HG_HEADS * HG_DV,
            MLA_Q_RANK, MLA_KV_RANK, MLA_ROPE,
            GQA_HEADS * GQA_HD, GQA_KV_HEADS * GQA_HD, GQA_KV_HEADS * GQA_HD)
IN_OFFSETS = tuple(sum(IN_SIZES[:i + 1]) for i in range(len(IN_SIZES) - 1))
N_IN = sum(IN_SIZES)

kernel_name = "hymba_dit_prefix_denoise_step"

F32 = jnp.float32


def _rms(x, g):
    xf = x.astype(F32)
    y = xf * lax.rsqrt(jnp.mean(xf * xf, axis=-1, keepdims=True) + EPS)
    return (y * g.astype(F32)).astype(x.dtype)


def _layernorm(x, g, b):
    xf = x.astype(F32)
    xc = xf - jnp.mean(xf, axis=-1, keepdims=True)
    var = jnp.mean(xc * xc, axis=-1, keepdims=True)
    return (xc * lax.rsqrt(var + EPS) * g.astype(F32) + b.astype(F32)).astype(x.dtype)


def _grid_positions(n):
    rows = n // GRID_W
    row = jnp.repeat(jnp.arange(rows, dtype=F32), GRID_W)
    col = jnp.tile(jnp.arange(GRID_W, dtype=F32), rows)
    return row, col


def _rope_1d(x, pos):
    half = x.shape[-1] // 2
    freqs = ROPE_THETA ** (-jnp.arange(half, dtype=F32) / half)
    ang = pos[:, None] * freqs[None, :]
    cos = jnp.cos(ang)[None, :, None, :]
    sin = jnp.sin(ang)[None, :, None, :]
    xf = x.astype(F32)
    x1, x2 = xf[..., :half], xf[..., half:]
    return jnp.concatenate([x1 * cos - x2 * sin, x1 * sin + x2 * cos], axis=-1).astype(x.dtype)


def _rope_2d(x, row, col):
    r = x.shape[-1] // 2
    return jnp.concatenate([_rope_1d(x[..., :r], row), _rope_1d(x[..., r:], col)], axis=-1)


def _attention(q, k, v, scale):
    b, sq, kh, g, dk = q.shape
    nb = sq // Q_BLOCK
    qb = q.reshape(b, nb, Q_BLOCK, kh, g, dk).swapaxes(0, 1)

    def one_block(qi):
        s = jnp.einsum('bqhgd,bkhd->bhgqk', qi, k, preferred_element_type=F32) * scale
        p = jax.nn.softmax(s, axis=-1)
        return jnp.einsum('bhgqk,bkhe->bqhge', p.astype(v.dtype), v)

    out = lax.map(one_block, qb)
    return out.swapaxes(0, 1).reshape(b, sq, kh, g, v.shape[-1])


def _hgrn_scan(q, logf, k, i, s0):
    b, s, h, dk = q.shape
    dv = i.shape[-1]
    n = s // HG_CHUNK

    def chunk(t):
        return t.reshape(b, n, HG_CHUNK, h, t.shape[-1]).transpose(1, 0, 3, 2, 4)

    qc, lc, kc, ic = chunk(q), chunk(logf), chunk(k), chunk(i)
    bc = jnp.cumsum(lc, axis=3)
    causal = jnp.tril(jnp.ones((HG_CHUNK, HG_CHUNK), dtype=bool))[:, :, None]
    diff = bc[..., :, None, :] - bc[..., None, :, :]
    decay = jnp.where(causal, jnp.exp(jnp.where(causal, diff, 0.0)), 0.0)
    a = jnp.einsum('nbhtk,nbhtsk,nbhsk->nbhts', qc, decay, kc)
    intra = jnp.einsum('nbhts,nbhsv->nbhtv', a, ic)
    q_dec = qc * jnp.exp(bc)
    k_dec = kc * jnp.exp(bc[..., -1:, :] - bc)
    chunk_decay = jnp.exp(bc[..., -1, :])

    def step(S, xs):
        qd, kd, iv, cd = xs
        inter = jnp.einsum('bhtk,bhkv->bhtv', qd, S)
        S = S * cd[..., None] + jnp.einsum('bhsk,bhsv->bhkv', kd, iv)
        return S, inter

    s_final, inter = lax.scan(step, s0, (q_dec, k_dec, ic, chunk_decay))
    o = (intra + inter).transpose(1, 0, 3, 2, 4).reshape(b, s, h, dv)
    return o, s_final


def _hgrn_mixer(hq, hff, hfb, hi, hg, lb, s0f, s0b, norm_g):
    b, s, _ = hq.shape
    q = jax.nn.silu(hq.astype(F32)).reshape(b, s, HG_HEADS, HG_DK) * (HG_DK ** -0.5)
    i = hi.astype(F32).reshape(b, s, HG_HEADS, HG_DV)

    def gates(z, lbd):
        z = z.astype(F32).reshape(b, s, HG_HEADS, HG_DK)
        lbd = lbd.reshape(HG_HEADS, HG_DK)
        f = lbd + (1.0 - lbd) * jax.nn.sigmoid(z)
        return jnp.log(jnp.maximum(f, GATE_FLOOR)), (1.0 - lbd) * jax.nn.sigmoid(-z)

    lf, kf = gates(hff, lb[0])
    lbw, kb = gates(hfb, lb[1])
    o_f, s_f = _hgrn_scan(q, lf, kf, i, s0f)
    flip = lambda t: jnp.flip(t, axis=1)
    o_b, s_b = _hgrn_scan(flip(q), flip(lbw), flip(kb), flip(i), s0b)
    o = _rms(o_f + flip(o_b), norm_g) * jax.nn.silu(hg.astype(F32).reshape(b, s, HG_HEADS, HG_DV))
    return o.reshape(b, s, HG_W).astype(hq.dtype), s_f, s_b


def _mla_queries(mq, q_norm, w_uq):
    b, s, _ = mq.shape
    q = jnp.einsum('bsr,rn->bsn', _rms(mq, q_norm), w_uq).reshape(b, s, MLA_HEADS, MLA_NOPE + MLA_ROPE)
    return q[..., :MLA_NOPE], q[..., MLA_NOPE:]


def _mla_keys_values(ckv, kpe, w_ukv):
    b, s, _ = ckv.shape
    kv = jnp.einsum('bsr,rn->bsn', ckv, w_ukv).reshape(b, s, MLA_HEADS, MLA_NOPE + MLA_V)
    kpe_h = jnp.broadcast_to(kpe[:, :, None, :], (b, s, MLA_HEADS, MLA_ROPE))
    return jnp.concatenate([kv[..., :MLA_NOPE], kpe_h], axis=-1), kv[..., MLA_NOPE:]


def _mixer(h, P, l, lb, cache):
    b, s, _ = h.shape
    z = jnp.einsum('bsd,dn->bsn', h, P['w_in'][l])
    hq, hff, hfb, hi, hg, mq, mkv, mkr, gq, gk, gv = jnp.split(z, IN_OFFSETS, axis=-1)
    is_ctx = cache is None
    if is_ctx:
        s0f = jnp.zeros((b, HG_HEADS, HG_DK, HG_DV), F32)
        s0b = s0f
    else:
        ckv_c, kpe_c, gk_c, gv_c, st_c = cache
        s0f, s0b = st_c[:, 0].astype(F32), st_c[:, 1].astype(F32)
    hg_out, s_f, s_b = _hgrn_mixer(hq, hff, hfb, hi, hg, lb, s0f, s0b, P['hg_norm'][l])

    q_nope, q_pe = _mla_queries(mq, P['mla_q_norm'][l], P['mla_w_uq'][l])
    ckv = _rms(mkv, P['mla_kv_norm'][l])
    kpe = mkr
    gq = _rms(gq.reshape(b, s, GQA_HEADS, GQA_HD), P['gqa_q_norm'][l])
    gk = _rms(gk.reshape(b, s, GQA_KV_HEADS, GQA_HD), P['gqa_k_norm'][l])
    gv = gv.reshape(b, s, GQA_KV_HEADS, GQA_HD)
    w_ukv = P['mla_w_ukv'][l]
    if is_ctx:
        mk, mv = _mla_keys_values(ckv, kpe, w_ukv)
        ak, av = gk, gv
        new = (ckv, kpe, gk, gv, jnp.stack([s_f, s_b], axis=1).astype(h.dtype))
    else:
        row, col = _grid_positions(s)
        q_pe = _rope_2d(q_pe, row, col)
        kpe_rot = _rope_2d(kpe[:, :, None, :], row, col)[:, :, 0, :]
        mk_l, mv_l = _mla_keys_values(ckv, kpe_rot, w_ukv)
        mk_c, mv_c = _mla_keys_values(ckv_c, kpe_c, w_ukv)
        mk = jnp.concatenate([mk_c, mk_l], axis=1)
        mv = jnp.concatenate([mv_c, mv_l], axis=1)
        gq = _rope_2d(gq, row, col)
        ak = jnp.concatenate([gk_c, _rope_2d(gk, row, col)], axis=1)
        av = jnp.concatenate([gv_c, gv], axis=1)
        new = None
    mla_q = jnp.concatenate([q_nope, q_pe], axis=-1)[:, :, :, None, :]
    mla_out = _attention(mla_q, mk, mv, MLA_SCALE).reshape(b, s, MLA_W)
    gqa_out = _attention(gq.reshape(b, s, GQA_KV_HEADS, GQA_GROUP, GQA_HD), ak, av, GQA_SCALE).reshape(b, s, GQA_W)
    mixed = jnp.concatenate([hg_out, mla_out, gqa_out], axis=-1)
    return jnp.einsum('bsn,nd->bsd', mixed, P['w_out'][l]), new


def _layer(x, cvec, P, l, lb, cache):
    m = jnp.einsum('bd,dn->bn', jax.nn.silu(cvec), P['w_ada'][l]) + P['b_ada'][l]
    sh1, sc1, g1, sh2, sc2, g2 = jnp.split(m[:, None, :], 6, axis=-1)
    y, new = _mixer(x * (1.0 + sc1) + sh1, P, l, lb, cache)
    x = _layernorm(ALPHA * x + g1 * y, P['ln1_g'][l], P['ln1_b'][l])
    hf = x * (1.0 + sc2) + sh2
    gate, up = jnp.split(jnp.einsum('bsd,dn->bsn', hf, P['w_ffn_in'][l]), 2, axis=-1)
    f = jnp.einsum('bsn,nd->bsd', jax.nn.silu(gate) * up, P['w_ffn_out'][l])
    x = _layernorm(ALPHA * x + g2 * f, P['ln2_g'][l], P['ln2_b'][l])
    return x, new


def setup_inputs(seed: int = 0) -> dict:
    key = jax.random.key(seed)
    ks = jax.random.split(key, 32)
    nrm = lambda k, shape, sc: jax.random.normal(k, shape, jnp.float32) * sc
    gain = lambda k, shape: 1.0 + nrm(k, shape, 0.02)
    D = D_MODEL
    return {
        'x_prompt': nrm(ks[0], (BATCH, SEQ, D), 1.0),
        'x_sample': nrm(ks[1], (DEC_BATCH, DEC_SEQ, D), 1.0),
        'cache_mla_ckv': nrm(ks[2], (DEC_BATCH, DEPTH, PAST_LEN, MLA_KV_RANK), 1.0),
        'cache_mla_kpe': nrm(ks[3], (DEC_BATCH, DEPTH, PAST_LEN, MLA_ROPE), 1.0),
        'cache_gqa_k': nrm(ks[4], (DEC_BATCH, DEPTH, PAST_LEN, GQA_KV_HEADS, GQA_HD), 1.0),
        'cache_gqa_v': nrm(ks[5], (DEC_BATCH, DEPTH, PAST_LEN, GQA_KV_HEADS, GQA_HD), 1.0),
        'state_hgrn': nrm(ks[6], (DEC_BATCH, DEPTH, 2, HG_HEADS, HG_DK, HG_DV), 0.5),
        'c': nrm(ks[7], (DEC_BATCH, D), 1.0),
        'c_ctx': nrm(ks[8], (D,), 1.0),
        'w_ada': nrm(ks[9], (DEPTH, D, 6 * D), 0.5 * D ** -0.5),
        'b_ada': nrm(ks[10], (DEPTH, 6 * D), 0.02),
        'w_in': nrm(ks[11], (DEPTH, D, N_IN), D ** -0.5),
        'hg_lb': nrm(ks[12], (DEPTH, 2, HG_HEADS * HG_DK), 0.5),
        'hg_norm': gain(ks[13], (DEPTH, HG_DV)),
        'mla_q_norm': gain(ks[14], (DEPTH, MLA_Q_RANK)),
        'mla_w_uq': nrm(ks[15], (DEPTH, MLA_Q_RANK, MLA_HEADS * (MLA_NOPE + MLA_ROPE)), MLA_Q_RANK ** -0.5),
        'mla_kv_norm': gain(ks[16], (DEPTH, MLA_KV_RANK)),
        'mla_w_ukv': nrm(ks[17], (DEPTH, MLA_KV_RANK, MLA_HEADS * (MLA_NOPE + MLA_V)), MLA_KV_RANK ** -0.5),
        'gqa_q_norm': gain(ks[18], (DEPTH, GQA_HD)),
        'gqa_k_norm': gain(ks[19], (DEPTH, GQA_HD)),
        'w_out': nrm(ks[20], (DEPTH, MIX_W, D), BETA * MIX_W ** -0.5),
        'ln1_g': gain(ks[21], (DEPTH, D)),
        'ln1_b': nrm(ks[22], (DEPTH, D), 0.02),
        'w_ffn_in': nrm(ks[23], (DEPTH, D, 2 * D_FF), D ** -0.5),
        'w_ffn_out': nrm(ks[24], (DEPTH, D_FF, D), BETA * D_FF ** -0.5),
        'ln2_g': gain(ks[25], (DEPTH, D)),
        'ln2_b': nrm(ks[26], (DEPTH, D), 0.02),
    }


def reference(x_prompt, x_sample, cache_mla_ckv, cache_mla_kpe, cache_gqa_k, cache_gqa_v, state_hgrn, c, c_ctx,
              w_ada, b_ada, w_in, hg_lb, hg_norm, mla_q_norm, mla_w_uq, mla_kv_norm, mla_w_ukv,
              gqa_q_norm, gqa_k_norm, w_out, ln1_g, ln1_b, w_ffn_in, w_ffn_out, ln2_g, ln2_b):
    P = {'w_ada': w_ada, 'b_ada': b_ada, 'w_in': w_in, 'hg_norm': hg_norm,
         'mla_q_norm': mla_q_norm, 'mla_w_uq': mla_w_uq, 'mla_kv_norm': mla_kv_norm, 'mla_w_ukv': mla_w_ukv,
         'gqa_q_norm': gqa_q_norm, 'gqa_k_norm': gqa_k_norm, 'w_out': w_out,
         'ln1_g': ln1_g, 'ln1_b': ln1_b, 'w_ffn_in': w_ffn_in, 'w_ffn_out': w_ffn_out,
         'ln2_g': ln2_g, 'ln2_b': ln2_b}
    sm = jax.nn.softmax(hg_lb.astype(F32), axis=0)
    lbs = jnp.cumsum(sm, axis=0) - sm[0:1]

    x = x_prompt
    news = []
    for l in range(DEPTH):
        x, new = _layer(x, c_ctx[None, :], P, l, lbs[l], None)
        news.append(new)
    y_prompt = x
    state_mla_ckv = jnp.stack([n[0] for n in news], axis=1)
    state_mla_kpe = jnp.stack([n[1] for n in news], axis=1)
    state_gqa_k = jnp.stack([n[2] for n in news], axis=1)
    state_gqa_v = jnp.stack([n[3] for n in news], axis=1)
    new_state_hgrn = jnp.stack([n[4] for n in news], axis=1)

    x = x_sample
    for l in range(DEPTH):
        cache = (cache_mla_ckv[:, l], cache_mla_kpe[:, l], cache_gqa_k[:, l], cache_gqa_v[:, l], state_hgrn[:, l])
        x, _ = _layer(x, c, P, l, lbs[l], cache)
    y_sample = x
    return (y_prompt, y_sample, state_mla_ckv, state_mla_kpe, state_gqa_k, state_gqa_v, new_state_hgrn)
```

```python
import math
from contextlib import ExitStack
import numpy as np
import concourse.bass as bass
import concourse.mybir as mybir
from concourse.bass_utils import run_bass_kernel_spmd

F32 = mybir.dt.float32
BF16 = mybir.dt.bfloat16
I32 = mybir.dt.int32
AF = mybir.ActivationFunctionType
ALU = mybir.AluOpType
AX = mybir.AxisListType

ENGS = ("pe", "act", "dve", "pool", "sp")

D = 1024
KC = 8
TB = 512
DFF = 2816
NFC = 22
EPS = 1e-6
ALPHA = 4.0 ** 0.25
MLA_SCALE = 96.0 ** -0.5
GQA_SCALE = 64.0 ** -0.5
THETA = 10000.0
O_HQ, O_HFF, O_HFB, O_HI, O_HG, O_MQ, O_MKV, O_MKR, O_GQ, O_GK, O_GV = 0, 256, 512, 768, 1024, 1280, 1536, 1664, 1696, 2080, 2208


class Res:
    __slots__ = ("name", "w", "readers", "psum")

    def __init__(self, name, psum=False):
        self.name = name
        self.w = None
        self.readers = []
        self.psum = psum


class Op:
    __slots__ = ("eng", "fn", "deps", "is_dma", "key", "dma_target", "mile", "needs_inc")

    def __init__(self, eng, fn, is_dma, key):
        self.eng = eng
        self.fn = fn
        self.is_dma = is_dma
        self.key = key
        self.deps = ()
        self.dma_target = 0
        self.mile = 0
        self.needs_inc = False


class Sched:
    def __init__(self, nc):
        self.nc = nc
        self.ops = {e: [] for e in ENGS}
        self.dma_count = {}

    def res(self, name="r"):
        return Res(name)

    def add(self, eng, fn, reads=(), writes=(), dma=False, key=None):
        op = Op(eng, fn, dma, key)
        raw = set()
        deps = set()
        for r in reads:
            if r.w is not None:
                raw.add(r.w)
                deps.add(r.w)
            if r.psum:
                for rd in r.readers:
                    if rd.eng != eng:
                        deps.add(rd)
        for w in writes:
            if w.w is not None:
                deps.add(w.w)
            for rd in w.readers:
                deps.add(rd)
        fdeps = []
        for d in deps:
            if (not d.is_dma) and (not dma) and d.eng == eng:
                if eng == "pe":
                    continue
            fdeps.append(d)
        op.deps = fdeps
        for r in reads:
            r.readers.append(op)
        for w in writes:
            w.w = op
            w.readers = []
        if dma:
            c = self.dma_count.get(key, 0) + 1
            self.dma_count[key] = c
            op.dma_target = 16 * c
        self.ops[eng].append(op)
        return op

    def emit(self, final_wait_eng="sp"):
        nc = self.nc
        for e in ENGS:
            for op in self.ops[e]:
                for d in op.deps:
                    if not d.is_dma:
                        d.needs_inc = True
        for e in ENGS:
            c = 0
            for op in self.ops[e]:
                if (not op.is_dma) and op.needs_inc:
                    c += 1
                    op.mile = c
        with ExitStack() as es:
            esem = {e: es.enter_context(nc.semaphore("s_" + e)) for e in ENGS}
            dsem = {k: es.enter_context(nc.semaphore("d_%d" % i)) for i, k in enumerate(self.dma_count)}
            block = es.enter_context(nc.Block())
            sched = self

            def run_engine(e, eng):
                waited_e = {x: 0 for x in ENGS}
                waited_d = {}
                for op in sched.ops[e]:
                    need_e = {}
                    need_d = {}
                    for d in op.deps:
                        if d.is_dma:
                            if need_d.get(d.key, 0) < d.dma_target:
                                need_d[d.key] = d.dma_target
                        else:
                            if need_e.get(d.eng, 0) < d.mile:
                                need_e[d.eng] = d.mile
                    for x, v in need_e.items():
                        if v > waited_e[x]:
                            eng.wait_ge(esem[x], v)
                            waited_e[x] = v
                    for k, v in need_d.items():
                        if v > waited_d.get(k, 0):
                            eng.wait_ge(dsem[k], v)
                            waited_d[k] = v
                    ins = op.fn(eng)
                    if op.is_dma:
                        ins.then_inc(dsem[op.key], 16)
                    elif op.needs_inc:
                        ins.then_inc(esem[e], 1)
                if e == final_wait_eng:
                    for k, c in sched.dma_count.items():
                        if 16 * c > waited_d.get(k, 0):
                            eng.wait_ge(dsem[k], 16 * c)

            @block.tensor
            def _(eng):
                run_engine("pe", eng)

            @block.scalar
            def _(eng):
                run_engine("act", eng)

            @block.vector
            def _(eng):
                run_engine("dve", eng)

            @block.gpsimd
            def _(eng):
                run_engine("pool", eng)

            @block.sync
            def _(eng):
                run_engine("sp", eng)


class B:
    def __init__(self, t, r):
        self.t = t
        self.r = r


class Builder:
    def __init__(self, do_ctx=True, do_sample=True, nlayers=2, dbg=None):
        self.do_ctx = do_ctx
        self.do_sample = do_sample
        self.nlayers = nlayers
        self.dbg = dbg or {}
        nc = bass.Bass("TRN2", target_bir_lowering=False)
        self.nc = nc
        self.S = Sched(nc)
        self.sb_bytes = 0
        self._ring_i = 0
        self._uid = 0
        self.declare_dram()
        self.alloc()

    def din(self, name, shape, dt=F32):
        return self.nc.dram_tensor(name, list(shape), dt, kind="ExternalInput").ap()

    def dout(self, name, shape, dt=F32):
        return self.nc.dram_tensor(name, list(shape), dt, kind="ExternalOutput").ap()

    def sb(self, name, shape, dt, nres=1):
        t = self.nc.alloc_sbuf_tensor(name, list(shape), dt)
        n = 1
        for s in shape[1:]:
            n *= s
        self.sb_bytes += n * (2 if dt == BF16 else 4)
        if nres == 1:
            return B(t, self.S.res(name))
        return B(t, [self.S.res("%s_%d" % (name, i)) for i in range(nres)])

    def ps(self, name, dt=F32):
        t = self.nc.alloc_psum_tensor(name, [128, 512 if dt == F32 else 1024], dt)
        return B(t, Res(name, psum=True))

    @staticmethod
    def _flat(lst):
        out = []
        for x in lst:
            if x is None:
                continue
            if isinstance(x, (list, tuple)):
                out.extend(Builder._flat(x))
            else:
                out.append(x)
        return out

    def op(self, eng, fn, reads=(), writes=()):
        return self.S.add(eng, fn, self._flat(reads), self._flat(writes))

    def dma(self, q, out, in_, reads=(), writes=(), key=None, nc_ok=False):
        if nc_ok:
            f = lambda e: e.dma_start(out=out, in_=in_, allow_slow_non_contiguous=True)
        else:
            f = lambda e: e.dma_start(out=out, in_=in_)
        return self.S.add(q, f, self._flat(reads), self._flat(writes), dma=True, key=key)

    def mm(self, out, lhsT, rhs, start, stop, reads, writes, tile_position=None):
        if tile_position is None:
            f = lambda e: e.matmul(out, lhsT=lhsT, rhs=rhs, start=start, stop=stop)
        else:
            f = lambda e: e.matmul(out, lhsT=lhsT, rhs=rhs, start=start, stop=stop, tile_position=tile_position)
        return self.op("pe", f, reads, writes)

    def tp(self, out, in_, ident, reads, writes):
        return self.op("pe", lambda e: e.transpose(out, in_, ident), reads, writes)

    def act(self, out, in_, func, reads, writes, scale=1.0, bias=0.0, accum_out=None):
        if accum_out is None:
            f = lambda e: e.activation(out=out, in_=in_, func=func, bias=bias, scale=scale)
        else:
            f = lambda e: e.activation(out=out, in_=in_, func=func, bias=bias, scale=scale, accum_out=accum_out)
        return self.op("act", f, reads, writes)

    def tt(self, out, in0, in1, op, reads, writes, eng="dve"):
        return self.op(eng, lambda e: e.tensor_tensor(out=out, in0=in0, in1=in1, op=op), reads, writes)

    def ts(self, out, in0, s1, s2, op0, op1, reads, writes, eng="dve"):
        if s2 is None:
            f = lambda e: e.tensor_scalar(out=out, in0=in0, scalar1=s1, scalar2=None, op0=op0)
        else:
            f = lambda e: e.tensor_scalar(out=out, in0=in0, scalar1=s1, scalar2=s2, op0=op0, op1=op1)
        return self.op(eng, f, reads, writes)

    def stt(self, out, in0, scalar, in1, op0, op1, reads, writes, eng="dve"):
        return self.op(eng, lambda e: e.scalar_tensor_tensor(out=out, in0=in0, scalar=scalar, in1=in1, op0=op0, op1=op1),
                       reads, writes)

    def cp(self, out, in_, reads, writes, eng="dve"):
        if eng == "act":
            return self.act(out, in_, AF.Copy, reads, writes)
        return self.op(eng, lambda e: e.tensor_copy(out=out, in_=in_), reads, writes)

    def recip(self, out, in_, reads, writes, exact=False):
        if exact:
            return self.op("dve", lambda e: e.reciprocal(out=out, in_=in_), reads, writes)
        self.act(out, in_, AF.Ln, reads, writes)
        return self.act(out, out, AF.Exp, writes, writes, scale=-1.0)

    def memset(self, ap, val, writes, eng="pool", reads=()):
        return self.op(eng, lambda e: e.memset(ap, val), reads, writes)

    def declare_dram(self):
        self.xc = self.din("xc", [1024, D])
        self.xsd = self.din("xs", [2048, D])
        self.c_ckv = self.din("c_ckv", [2, 256, 128])
        self.c_kpe = self.din("c_kpe", [2, 256, 32])
        self.c_gk = self.din("c_gk", [2, 256, 128])
        self.c_gv = self.din("c_gv", [2, 256, 128])
        self.s_hg = self.din("s_hg", [2, 2, 256, 64])
        self.cvec = self.din("cvec", [2, D])
        self.w_ada = self.din("w_ada", [2, D, 6 * D])
        self.b_ada = self.din("b_ada", [2, 6 * D])
        self.w_in = self.din("w_in", [2, D, 2336])
        self.hg_lb = self.din("hg_lb", [2, 2, 256])
        self.hg_norm = self.din("hg_norm", [2, 64])
        self.mla_q_norm = self.din("mla_q_norm", [2, 256])
        self.w_uq = self.din("mla_w_uq", [2, 256, 576])
        self.mla_kv_norm = self.din("mla_kv_norm", [2, 128])
        self.w_ukv = self.din("mla_w_ukv", [2, 128, 768])
        self.gqa_q_norm = self.din("gqa_q_norm", [2, 64])
        self.gqa_k_norm = self.din("gqa_k_norm", [2, 64])
        self.w_out = self.din("w_out", [2, D, D])
        self.ln1_g = self.din("ln1_g", [2, D])
        self.ln1_b = self.din("ln1_b", [2, D])
        self.w_ffn_in = self.din("w_ffn_in", [2, D, 2 * DFF])
        self.w_ffn_out = self.din("w_ffn_out", [2, DFF, D])
        self.ln2_g = self.din("ln2_g", [2, D])
        self.ln2_b = self.din("ln2_b", [2, D])
        self.y_c = self.dout("y_c", [1024, D])
        self.y_own = self.dout("y_own", [TB, D])
        self.gidx_d = self.din("gidx", [128, 1], I32)
        self.o_ckv = self.dout("o_ckv", [4, 2, 256, 128])
        self.o_kpe = self.dout("o_kpe", [4, 2, 256, 32])
        self.o_gk = self.dout("o_gk", [4, 2, 256, 128])
        self.o_gv = self.dout("o_gv", [4, 2, 256, 128])
        self.o_hg = self.dout("o_hg", [4, 2, 2, 256, 64])
        self.dbg_out = {}
        for name, shape in self.dbg.items():
            self.dbg_out[name] = self.dout("dbg_" + name, shape)

    def alloc(self):
        sb = self.sb
        self.xs_dram = self.nc.dram_tensor("xs_scr", [4 * 128, KC * TB], F32)
        self.mh2 = self.nc.dram_tensor("mh_scr", [4 * 128, 2 * TB], BF16)
        self.rp2 = self.nc.dram_tensor("rp_scr", [4 * 128, 4 * 96], F32)
        self.wsc_in = [self.nc.dram_tensor("wsc_in%d" % l, [11 * 128, 4096], BF16) for l in range(2)]
        self.wsc_out = [self.nc.dram_tensor("wsc_out%d" % l, [8 * 128, 2816], BF16) for l in range(2)]
        self.wscr = [self.S.res("wsc%d" % l) for l in range(2)]
        self.ffn_cached = [False, False]
        self.mh2r = self.S.res("mh2")
        self.rp2r = self.S.res("rp2")
        self.gidx = sb("gidx_sb", [128, 1], I32)
        self.xsr = [self.S.res("xsd%d" % i) for i in range(4)]
        self.xb = sb("xb", [128, KC, TB], F32, nres=KC)
        self.hT = sb("hT", [128, KC, TB], BF16)
        self.KmlaT = sb("KmlaT", [128, 6, 2304], BF16)
        self.KgqaT = sb("KgqaT", [128, 2304], BF16)
        self.Vmla = sb("Vmla", [128, 18, 6, 65], BF16)
        self.Vgqa = sb("Vgqa", [128, 18, 2, 65], BF16)
        self.mixHG = sb("mixHG", [128, 2, 2048], BF16, nres=4)
        self.NW = 3
        self.ring = [sb("ring%d" % i, [128, 4096], BF16) for i in range(self.NW)]
        self.wsm = sb("wsm", [128, 1920], BF16)
        self.ident = sb("ident", [128, 128], BF16)
        self.identf = sb("identf", [128, 128], F32)
        self.onesf = sb("onesf", [128, 128], F32)
        self.onesblk = sb("onesblk", [128, 128], F32)
        self.segmask = sb("segmask", [128, TB], F32)
        self.mAf = sb("mAf", [128, 128], BF16)
        self.mAb = sb("mAb", [128, 128], BF16)
        self.pcs = sb("pcs", [128, 4], F32)
        self.rope = sb("rope", [128, 16, 2, 2, 24], F32)
        self.par = sb("par", [128, 224], F32)
        self.mT = sb("mT", [128, 2, 48, 2], F32)
        self.mP = sb("mP", [128, 2, 2, 2, 8], F32)
        self.lbp = sb("lbp", [128, 2, 2, 2, 3], F32)
        self.bc = sb("bc", [128, 512], F32)
        self.hgn = sb("hgn", [128, 2], F32)
        self.V32 = sb("V32", [128, 2, 2, 64], F32, nres=4)
        self.iexp = sb("iexp", [128, 4, 128], BF16)
        self.cmask4 = sb("cmask4", [128, 4], BF16)
        self.NV = 4
        self.Vr32 = sb("Vr32", [128, self.NV, 64], F32, nres=self.NV)
        self.Vrb = sb("Vrb", [128, self.NV, 64], BF16, nres=self.NV)
        self.stg = [sb("stg0", [128, 1024], F32)]
        self.ht = [sb("ht%d" % i, [128, TB], F32) for i in range(5)]
        self.hq32 = sb("hq32", [128, 2, TB], F32)
        self.qh = sb("qh", [128, TB], BF16)
        self.kh = sb("kh", [128, TB], BF16)
        self.qt = sb("qt", [128, TB], BF16)
        self.kd = sb("kd", [128, TB], BF16)
        self.kdtm = sb("kdtm", [128, 4, 128], BF16)
        self.itm = sb("itm", [128, 4, 256], BF16)
        self.ATm = sb("ATm", [128, 2, 128], BF16)
        self.hsm = sb("hsm", [128, 3, 16], F32)
        self.zkf = sb("zkf", [128, 416], F32)
        self.gkf = sb("gkf", [128, 128], F32)
        self.kst = sb("kst", [128, 3, 128], BF16)
        self.ckvT = sb("ckvT", [128, TB], BF16)
        self.sq = sb("sq", [128, 512], F32)
        self.st8 = sb("st8", [128, 8], F32)
        self.QmlaT = sb("QmlaT", [128, 6, TB], BF16)
        self.QgqaT = sb("QgqaT", [128, 3, TB], BF16)
        self.mixA = sb("mixA", [64, 12, TB], BF16, nres=12)
        self.PTb = [self.qh, self.kh, self.qt]
        self.bcs = B(self.hq32.t[0:64, 0, :], self.hq32.r)
        self.rc = self.sq
        self.zq = sb("zq", [128, 640], F32)
        self.qstage = sb("qstage", [128, 6, 96], BF16)
        self.gqb = sb("gqb", [128, 384], BF16)
        self.mqnT = sb("mqnT", [128, 2, 128], BF16)
        self.qpe = sb("qpe", [128, 6, 32], F32)
        self.actT = sb("actT", [128, NFC, TB], BF16)
        self.lnm, self.lnr, self.lnt = self.ht[0], self.ht[1], self.ht[2]
        self.gy = [self.ht[3], self.ht[4]]
        self.P = [self.ps("pb%d" % i) for i in range(6)]
        self.PT = [self.ps("pt%d" % i, BF16) for i in range(2)]
        print("[kernel] SBUF bytes/partition allocated:", self.sb_bytes, "remaining", self.nc.sbuf_bytes_remaining)

    def phase(self, name):
        if not hasattr(self, "phases"):
            self.phases = []
        self.phases.append((name, len(self.S.ops["pe"])))

    def next_ring(self):
        b = self.ring[self._ring_i % self.NW]
        self._ring_i += 1
        return b

    def wload(self, pieces):
        slot = self.next_ring()
        for (dst, src) in pieces:
            self.dma("pool", dst(slot.t), src, writes=[slot.r], key=slot.r.name)
        return slot

    def init_consts(self):
        S = self.S
        pi = math.pi
        for t in (self.ident, self.identf):
            self.memset(t.t[:], 1.0, [t.r])
            self.op("pool", lambda e, t=t: e.affine_select(out=t.t[:], in_=t.t[:], compare_op=ALU.is_equal, fill=0.0,
                                                           base=0, pattern=[[-1, 128]], channel_multiplier=1),
                    [t.r], [t.r])
        self.memset(self.onesf.t[:], 1.0, [self.onesf.r])
        self.memset(self.onesblk.t[:], 0.0, [self.onesblk.r])
        self.memset(self.onesblk.t[0:64, 0:64], 1.0, [self.onesblk.r], reads=[self.onesblk.r])
        self.memset(self.onesblk.t[64:128, 64:128], 1.0, [self.onesblk.r], reads=[self.onesblk.r])
        self.memset(self.segmask.t[:], 1.0, [self.segmask.r])
        self.memset(self.segmask.t[:].rearrange("p (c t) -> p c t", t=32)[:, :, 0:1], 0.0, [self.segmask.r],
                    reads=[self.segmask.r])
        m = self.mAf
        self.memset(m.t[:], 1.0, [m.r])
        self.op("pool", lambda e: e.affine_select(out=m.t[:], in_=m.t[:], compare_op=ALU.is_ge, fill=0.0,
                                                  base=0, pattern=[[1, 128]], channel_multiplier=-1), [m.r], [m.r])
        self.op("pool", lambda e: e.affine_select(out=m.t[:], in_=m.t[:], compare_op=ALU.is_ge, fill=0.0,
                                                  base=0, pattern=[[-32, 4], [0, 32]], channel_multiplier=1), [m.r], [m.r])
        m2 = self.mAb
        self.memset(m2.t[:], 1.0, [m2.r])
        self.op("pool", lambda e: e.affine_select(out=m2.t[:], in_=m2.t[:], compare_op=ALU.is_ge, fill=0.0,
                                                  base=0, pattern=[[-1, 128]], channel_multiplier=1), [m2.r], [m2.r])
        self.op("pool", lambda e: e.affine_select(out=m2.t[:], in_=m2.t[:], compare_op=ALU.is_ge, fill=0.0,
                                                  base=31, pattern=[[32, 4], [0, 32]], channel_multiplier=-1), [m2.r], [m2.r])
        cm = self.cmask4
        self.memset(cm.t[:], 1.0, [cm.r])
        self.op("pool", lambda e: e.affine_select(out=cm.t[:], in_=cm.t[:], compare_op=ALU.is_ge, fill=0.0,
                                                  base=0, pattern=[[-32, 4]], channel_multiplier=1), [cm.r], [cm.r])
        self.op("pool", lambda e: e.affine_select(out=cm.t[:], in_=cm.t[:], compare_op=ALU.is_ge, fill=0.0,
                                                  base=31, pattern=[[32, 4]], channel_multiplier=-1), [cm.r], [cm.r])
        self.memset(self.Vmla.t[:, :, :, 64:65], 1.0, [self.Vmla.r])
        self.memset(self.Vgqa.t[:, :, :, 64:65], 1.0, [self.Vgqa.r])
        pc = self.pcs
        self.op("pool", lambda e: e.iota(pc.t[:, 0:1], pattern=[[0, 1]], base=0, channel_multiplier=1,
                                         allow_small_or_imprecise_dtypes=True), [], [pc.r])
        self.ts(pc.t[:, 2:3], pc.t[:, 0:1], 64.0, None, ALU.is_ge, None, [pc.r], [pc.r])
        self.stt(pc.t[:, 1:2], pc.t[:, 2:3], -64.0, pc.t[:, 0:1], ALU.mult, ALU.add, [pc.r], [pc.r])
        self.dma("sp", self.gidx.t[:], self.gidx_d, writes=[self.gidx.r], key="gidx")
        if self.do_sample:
            self.init_rope()
            for blk in range(4):
                self.dma("sp", self.rp2[blk * 128:(blk + 1) * 128, :],
                         self.rope.t[:, 4 * blk:4 * blk + 4].rearrange("p a b c d -> p (a b c d)"),
                         reads=[self.rope.r], writes=[self.rp2r], key="rope")

    def init_rope(self):
        pi = math.pi
        pc = self.pcs
        ang = self.ht[0]
        a = self.ht[0].t[:, 0:384].rearrange("p (i f) -> p i f", f=24)
        c = self.ht[1].t[:, 0:24]
        fr = self.ht[2].t[:, 0:24]
        rv = self.ht[2].t[:, 32:48]
        r0, r1, r2 = self.ht[0].r, self.ht[1].r, self.ht[2].r
        for j in range(8):
            self.memset(fr[:, j:j + 1], THETA ** (-j / 8.0), [r2], eng="dve", reads=[r2])
        for j in range(16):
            self.memset(fr[:, 8 + j:9 + j], THETA ** (-j / 16.0), [r2], eng="dve", reads=[r2])
        for i in range(16):
            self.memset(rv[:, i:i + 1], 2.0 * i, [r2], eng="dve", reads=[r2])
        self.ts(rv, rv, pc.t[:, 2:3], None, ALU.add, None, [r2, pc.r], [r2])
        self.tt(a, rv[:, :, None].broadcast_to([128, 16, 24]), fr[:, None, :].broadcast_to([128, 16, 24]), ALU.mult, [r2], [r0])
        self.ts(c, fr, pc.t[:, 1:2], None, ALU.mult, None, [r2, pc.r], [r1])
        rp = self.rope
        r3, r4 = self.ht[3].r, self.ht[4].r

        def reduce_sin(dst, src_ap, src_res, shape_n, shift, view):
            y = view(self.ht[3].t[:, 0:shape_n])
            yi = view(self.ht[4].t[:, 0:shape_n].bitcast(I32))
            yf = view(self.ht[4].t[:, 384:384 + shape_n]) if shape_n <= 128 else None
            self.ts(y, src_ap, 1.0 / (2 * pi), shift, ALU.mult, ALU.add, [src_res], [r3])
            self.cp(yi, y, [r3], [r4])
            y2 = view(self.hq32.t[:, 0, 0:shape_n])
            self.cp(y2, yi, [r4], [self.hq32.r])
            self.tt(y, y, y2, ALU.subtract, [r3, self.hq32.r], [r3])
            m = view(self.hq32.t[:, 1, 0:shape_n])
            self.ts(m, y, 0.5, None, ALU.is_gt, None, [r3], [self.hq32.r])
            self.tt(y, y, m, ALU.subtract, [r3, self.hq32.r], [r3])
            self.ts(m, y, -0.5, None, ALU.is_lt, None, [r3], [self.hq32.r])
            self.tt(y, y, m, ALU.add, [r3, self.hq32.r], [r3])
            self.act(dst, y, AF.Sin, [r3], [rp.r], scale=2 * pi)

        v3 = lambda ap: ap.rearrange("p (i f) -> p i f", f=24)
        v2 = lambda ap: ap
        ctmp = self.sq.t[:, 0:48]
        for which, shift in ((0, 0.25), (1, 0.0)):
            reduce_sin(rp.t[:, :, which, 0, :], a, r0, 384, shift, v3)
            y = self.ht[3].t[:, 0:24]
            cdst = ctmp[:, which * 24:(which + 1) * 24]
            old_rp = rp
            def _col(dst=cdst, shift=shift):
                yv = self.ht[3].t[:, 0:24]
                yi = self.ht[4].t[:, 0:24].bitcast(I32)
                y2 = self.hq32.t[:, 0, 0:24]
                m = self.hq32.t[:, 1, 0:24]
                self.ts(yv, c, 1.0 / (2 * pi), shift, ALU.mult, ALU.add, [r1], [r3])
                self.cp(yi, yv, [r3], [r4])
                self.cp(y2, yi, [r4], [self.hq32.r])
                self.tt(yv, yv, y2, ALU.subtract, [r3, self.hq32.r], [r3])
                self.ts(m, yv, 0.5, None, ALU.is_gt, None, [r3], [self.hq32.r])
                self.tt(yv, yv, m, ALU.subtract, [r3, self.hq32.r], [r3])
                self.ts(m, yv, -0.5, None, ALU.is_lt, None, [r3], [self.hq32.r])
                self.tt(yv, yv, m, ALU.add, [r3, self.hq32.r], [r3])
                self.act(dst, yv, AF.Sin, [r3], [self.sq.r], scale=2 * pi)
            _col()
            self.cp(rp.t[:, :, which, 1, :], cdst[:, None, :].broadcast_to([128, 16, 24]), [self.sq.r], [rp.r])

    def load_params(self):
        st = self.stg[0]
        rows = []
        r = 0
        self.par_off = {}

        def put(name, ap2d, n):
            nonlocal r
            self.dma("sp", st.t[r:r + n, 0:128], ap2d, writes=[st.r], key="stg")
            self.par_off[name] = r
            r += n
        put("cvec", self.cvec.rearrange("g (k p) -> (g k) p", p=128), 16)
        put("ln1_g", self.ln1_g.rearrange("l (k p) -> (l k) p", p=128), 16)
        put("ln1_b", self.ln1_b.rearrange("l (k p) -> (l k) p", p=128), 16)
        put("ln2_g", self.ln2_g.rearrange("l (k p) -> (l k) p", p=128), 16)
        put("ln2_b", self.ln2_b.rearrange("l (k p) -> (l k) p", p=128), 16)
        put("lb", self.hg_lb.rearrange("l d (c p) -> (l d c) p", p=128), 8)
        n1 = r
        pt = self.P[0]
        self.tp(pt.t[:, 0:n1], st.t[0:n1, 0:128], self.identf.t[0:n1, 0:n1], [st.r, self.identf.r], [pt.r])
        self.cp(self.par.t[:, 0:n1], pt.t[:, 0:n1], [pt.r], [self.par.r])
        st2r = self.sq
        self.dma("sp", st2r.t[0:96, 0:128], self.b_ada.rearrange("l (j p) -> (l j) p", p=128), writes=[st2r.r], key="sq")
        pt2 = self.P[1]
        self.tp(pt2.t[:, 0:96], st2r.t[0:96, 0:128], self.identf.t[0:96, 0:96], [st2r.r, self.identf.r], [pt2.r])
        self.par_off["b_ada"] = 128
        self.cp(self.par.t[:, 128:224], pt2.t[:, 0:96], [pt2.r], [self.par.r])
        for l in range(2):
            for h in range(2):
                self.dma("sp", self.hgn.t[64 * h:64 * h + 64, l:l + 1], self.hg_norm[l:l + 1, :].rearrange("o d -> d o"),
                         writes=[self.hgn.r], key="hgn", nc_ok=True)
        o = self.par_off["lb"]
        lb0 = self.par.t[:, o:o + 4]
        lb1 = self.par.t[:, o + 4:o + 8]
        sm = self.small = getattr(self, "small", None) or self.sb("small", [128, 32], F32)
        s = sm.t
        self.tt(s[:, 0:4], lb0, lb1, ALU.max, [self.par.r], [sm.r])
        self.tt(s[:, 4:8], lb0, s[:, 0:4], ALU.subtract, [self.par.r, sm.r], [sm.r])
        self.tt(s[:, 8:12], lb1, s[:, 0:4], ALU.subtract, [self.par.r, sm.r], [sm.r])
        self.act(s[:, 4:12], s[:, 4:12], AF.Exp, [sm.r], [sm.r])
        self.tt(s[:, 12:16], s[:, 4:8], s[:, 8:12], ALU.add, [sm.r], [sm.r])
        self.recip(s[:, 12:16], s[:, 12:16], [sm.r], [sm.r])
        self.tt(s[:, 16:20], s[:, 8:12], s[:, 12:16], ALU.mult, [sm.r], [sm.r])
        lbp = self.lbp
        self.memset(lbp.t[:, 0, :, :, 0:1], 0.0, [lbp.r], eng="dve")
        self.cp(lbp.t[:, 1, :, :, 0:1], s[:, 16:20].rearrange("p (d c o) -> p d c o", d=2, c=2), [sm.r], [lbp.r])
        self.ts(lbp.t[:, :, :, :, 1:2], lbp.t[:, :, :, :, 0:1], -1.0, 1.0, ALU.mult, ALU.add, [lbp.r], [lbp.r])
        self.ts(lbp.t[:, :, :, :, 2:3], lbp.t[:, :, :, :, 1:2], -1.0, None, ALU.mult, None, [lbp.r], [lbp.r])

    def pcol(self, name, l, k=None):
        o = self.par_off[name] + l * 8
        if k is None:
            return self.par.t[:, o:o + 8]
        return self.par.t[:, o + k:o + k + 1]

    def ada(self):
        o = self.par_off["cvec"]
        cv = self.par.t[:, o:o + 16]
        sm = self.small
        s = sm.t
        self.act(s[:, 0:16], cv, AF.Exp, [self.par.r], [sm.r], scale=-1.0)
        self.ts(s[:, 0:16], s[:, 0:16], 1.0, None, ALU.add, None, [sm.r], [sm.r])
        self.recip(s[:, 0:16], s[:, 0:16], [sm.r], [sm.r])
        self.tt(s[:, 0:16], s[:, 0:16], cv, ALU.mult, [sm.r, self.par.r], [sm.r])
        scT = self.kst
        scv = scT.t[:, 0, 0:16].rearrange("p (k g) -> p k g", g=2)
        self.cp(scv, s[:, 0:16].rearrange("p (g k) -> p k g", g=2), [sm.r], [scT.r])
        bo = self.par_off["b_ada"]
        for l in range(2):
            for grp in range(12):
                slot = self.wload([(lambda t: t[:, 0:4096].rearrange("p (k n) -> p k n", k=8),
                                    self.w_ada[l][:, grp * 512:(grp + 1) * 512].rearrange("(k p) n -> p k n", p=128))])
                wv = slot.t[:, 0:4096].rearrange("p (k n) -> p k n", k=8)
                pb = self.P[grp % 2]
                for jj in range(4):
                    for k in range(8):
                        self.mm(pb.t[:, jj * 2:jj * 2 + 2], wv[:, k, jj * 128:(jj + 1) * 128], scv[:, k, :],
                                k == 0, k == 7, [slot.r, scT.r], [pb.r])
                j0 = grp * 4
                self.tt(self.mT.t[:, l, j0:j0 + 4, :], pb.t[:, 0:8].rearrange("p (j g) -> p j g", g=2),
                        self.par.t[:, bo + l * 48 + j0: bo + l * 48 + j0 + 4][:, :, None].broadcast_to([128, 4, 2]),
                        ALU.add, [pb.r, self.par.r], [self.mT.r])
        for l in range(2):
            for g in range(2):
                self.ts(self.mP.t[:, l, g, 0, :], self.mT.t[:, l, 8:16, g], 1.0, None, ALU.add, None, [self.mT.r], [self.mP.r])
                self.ts(self.mP.t[:, l, g, 1, :], self.mT.t[:, l, 32:40, g], 1.0, None, ALU.add, None, [self.mT.r], [self.mP.r])

    def mvec(self, l, g, which, k):
        if which == "sc1p":
            return self.mP.t[:, l, g, 0, k:k + 1]
        if which == "sc2p":
            return self.mP.t[:, l, g, 1, k:k + 1]
        base = {"sh1": 0, "g1": 16, "sh2": 24, "g2": 40}[which]
        return self.mT.t[:, l, base + k, g:g + 1]

    def load_layer_small(self, l):
        bc = self.bc
        self.dma("sp", bc.t[:, 0:128], self.mla_kv_norm[l:l + 1, :].partition_broadcast(128), writes=[bc.r], key="bc")
        self.dma("sp", bc.t[:, 128:384], self.mla_q_norm[l:l + 1, :].partition_broadcast(128), writes=[bc.r], key="bc")
        self.dma("sp", bc.t[:, 384:448], self.gqa_q_norm[l:l + 1, :].partition_broadcast(128), writes=[bc.r], key="bc")
        self.dma("sp", bc.t[:, 448:512], self.gqa_k_norm[l:l + 1, :].partition_broadcast(128), writes=[bc.r], key="bc")
        w = self.wsm
        self.dma("pool", w.t[:, 0:1152].rearrange("p (k n) -> p k n", k=2), self.w_uq[l].rearrange("(k p) n -> p k n", p=128),
                 writes=[w.r], key="wsm")
        self.dma("pool", w.t[:, 1152:1920], self.w_ukv[l], writes=[w.r], key="wsm")

    def dump(self, name, ap, res):
        if name not in self.dbg_out:
            return
        rs = res if isinstance(res, (list, tuple)) else [res]
        self.dma("sp", self.dbg_out[name], ap, reads=rs, key="dbg_" + name)

    def load_x_block(self, dram_rows_ap):
        st = self.stg[0]
        for t in range(4):
            self.dma("sp", st.t[:], dram_rows_ap[t * 128:(t + 1) * 128, :], writes=[st.r], key="stg")
            for half in range(2):
                pb = self.P[(2 * t + half) % 6]
                for kk in range(4):
                    k = half * 4 + kk
                    self.tp(pb.t[:, kk * 128:(kk + 1) * 128], st.t[:, k * 128:(k + 1) * 128], self.identf.t[:],
                            [st.r, self.identf.r], [pb.r])
                self.cp(self.xb.t[:, half * 4:half * 4 + 4, t * 128:(t + 1) * 128],
                        pb.t[:].rearrange("p (k n) -> p k n", k=4), [pb.r], [self.xb.r],
                        eng=("act" if half else "dve"))

    def store_y_block(self, dram_rows_ap):
        st = self.stg[0]
        for t in range(4):
            for half in range(2):
                pb = self.P[(2 * t + half) % 6]
                for kk in range(4):
                    k = half * 4 + kk
                    self.tp(pb.t[:, kk * 128:(kk + 1) * 128], self.xb.t[:, k, t * 128:(t + 1) * 128], self.identf.t[:],
                            [self.xb.r, self.identf.r], [pb.r])
                self.cp(st.t[:, half * 512:(half + 1) * 512], pb.t[:], [pb.r], [st.r], eng=("act" if half else "dve"))
            self.dma("sp", dram_rows_ap[t * 128:(t + 1) * 128, :], st.t[:], reads=[st.r], key="stg")

    def xs_dram_view(self, blk):
        return self.xs_dram[blk * 128:(blk + 1) * 128, :].rearrange("p (k n) -> p k n", k=KC)

    def gather(self, out_ap, dram_t, reads, writes, key):
        idx = self.gidx
        f = lambda e: e.indirect_dma_start(out=out_ap, out_offset=None, in_=dram_t,
                                           in_offset=bass.IndirectOffsetOnAxis(ap=idx.t[:, 0:1], axis=0))
        return self.S.add("pool", f, self._flat(list(reads) + [idx.r]), self._flat(writes), dma=True, key=key)

    def load_xs_block(self, blk):
        self.dma("sp", self.xb.t[:], self.xs_dram_view(blk), reads=[self.xsr[blk]], writes=[self.xb.r], key="xb")

    def store_xs_block(self, blk):
        self.dma("sp", self.xs_dram_view(blk), self.xb.t[:], reads=[self.xb.r], writes=[self.xsr[blk]], key="xb")

    def prepH(self, l, g, which="1"):
        scn, shn = ("sc1p", "sh1") if which == "1" else ("sc2p", "sh2")
        for k in range(KC):
            self.ts(self.hT.t[:, k, :], self.xb.t[:, k, :], self.mvec(l, g, scn, k), self.mvec(l, g, shn, k),
                    ALU.mult, ALU.add, [self.xb.r[k], self.mT.r, self.mP.r], [self.hT.r])

    def win_slot(self, l, pieces):
        tot = sum(n for _, n in pieces)
        assert tot * 8 <= 4096
        lst = []
        off = 0
        for (c0, n) in pieces:
            lst.append((lambda t, off=off, n=n, tot=tot: t[:, 0:8 * tot].rearrange("p (k n) -> p k n", k=8)[:, :, off:off + n],
                        self.w_in[l][:, c0:c0 + n].rearrange("(k p) n -> p k n", p=128)))
            off += n
        slot = self.wload(lst)
        return slot, slot.t[:, 0:8 * tot].rearrange("p (k n) -> p k n", k=8)

    def rope_tm(self, out, x, tile_idx, f0, nf, nheads, reads, writes):
        rp = self.rope
        shp = [128, nheads, 2, nf]
        cos = rp.t[:, tile_idx, 0, :, f0:f0 + nf][:, None, :, :].broadcast_to(shp)
        sin = rp.t[:, tile_idx, 1, :, f0:f0 + nf][:, None, :, :].broadcast_to(shp)
        x1 = x[:, :, :, 0, :]
        x2 = x[:, :, :, 1, :]
        t1 = self.sq.t[:, 0:nheads * 2 * nf].rearrange("p (h a f) -> p h a f", h=nheads, a=2)
        t2 = self.sq.t[:, 256:256 + nheads * 2 * nf].rearrange("p (h a f) -> p h a f", h=nheads, a=2)
        rs = list(reads) + [rp.r]
        self.tt(t1, x1, cos, ALU.mult, rs, [self.sq.r])
        self.tt(t2, x2, sin, ALU.mult, rs + [self.sq.r], [self.sq.r])
        self.tt(out[:, :, :, 0, :], t1, t2, ALU.subtract, [self.sq.r], writes)
        self.tt(t1, x1, sin, ALU.mult, rs + [self.sq.r], [self.sq.r])
        self.tt(t2, x2, cos, ALU.mult, rs + [self.sq.r], [self.sq.r])
        self.tt(out[:, :, :, 1, :], t1, t2, ALU.add, [self.sq.r], writes)

    def kv_finish(self, ntiles, kt0):
        w = self.wsm
        wukv = w.t[:, 1152:1920]
        n = ntiles * 128
        for h in range(6):
            pb = self.P[h % 2]
            self.mm(pb.t[0:64, 0:n], wukv[:, h * 128:h * 128 + 64], self.ckvT.t[:, 0:n], True, True, [w.r, self.ckvT.r], [pb.r])
            self.cp(self.KmlaT.t[0:64, h, kt0 * 128:kt0 * 128 + n], pb.t[0:64, 0:n], [pb.r], [self.KmlaT.r],
                    eng=("act" if h % 2 else "dve"))
        wv = wukv.rearrange("p (h x) -> p h x", h=6)[:, :, 64:128]
        for t in range(ntiles):
            pb = self.P[2 + t % 2]
            self.mm(pb.t[:, 0:384].rearrange("p (h x) -> p h x", h=6), self.ckvT.t[:, t * 128:(t + 1) * 128], wv, True, True,
                    [w.r, self.ckvT.r], [pb.r])
            self.cp(self.Vmla.t[:, kt0 + t, :, 0:64], pb.t[:, 0:384].rearrange("p (h x) -> p h x", h=6), [pb.r], [self.Vmla.r],
                    eng=("act" if t % 2 else "dve"))

    def kv_transposes(self, t, kt):
        k = self.kst
        pt = self.PT[t % 2]
        self.tp(pt.t[:, 0:128], k.t[:, 0, :], self.ident.t[:], [k.r, self.ident.r], [pt.r])
        self.tp(pt.t[:, 128:256], k.t[:, 1, :], self.ident.t[:], [k.r, self.ident.r], [pt.r])
        self.tp(pt.t[:, 256:384], k.t[:, 2, :], self.ident.t[:], [k.r, self.ident.r], [pt.r])
        self.cp(self.ckvT.t[:, t * 128:(t + 1) * 128], pt.t[:, 0:128], [pt.r], [self.ckvT.r], eng="act")
        self.cp(self.KmlaT.t[64:96, :, kt * 128:(kt + 1) * 128], pt.t[64:96, 128:256][:, None, :].broadcast_to([32, 6, 128]),
                [pt.r], [self.KmlaT.r], eng="dve")
        self.cp(self.KgqaT.t[:, kt * 128:(kt + 1) * 128], pt.t[:, 256:384], [pt.r], [self.KgqaT.r], eng="act")

    def kv_pass(self, l, ntiles, kt0, rope_tile0=None, ctx_out=None):
        slot, wv = self.win_slot(l, [(O_MKV, 160), (O_GK, 256)])
        bc = self.bc
        for t in range(ntiles):
            pb = self.P[4 + t % 2]
            for k in range(KC):
                self.mm(pb.t[:, 0:416], self.hT.t[:, k, t * 128:(t + 1) * 128], wv[:, k, :], k == 0, k == KC - 1,
                        [self.hT.r, slot.r], [pb.r])
            st = self.st8
            self.act(self.sq.t[:, 0:128], pb.t[:, 0:128], AF.Square, [pb.r], [self.sq.r, st.r], scale=128.0 ** -0.5,
                     accum_out=st.t[:, 0:1])
            for h in range(2):
                self.act(self.sq.t[:, 128:192], pb.t[:, 160 + 64 * h:224 + 64 * h], AF.Square, [pb.r], [self.sq.r, st.r],
                         scale=0.125, accum_out=st.t[:, 1 + h:2 + h])
            self.act(st.t[:, 0:3], st.t[:, 0:3], AF.Ln, [st.r], [st.r], bias=EPS)
            self.act(st.t[:, 0:3], st.t[:, 0:3], AF.Exp, [st.r], [st.r], scale=-0.5)
            k_ = self.kst
            zf = self.zkf
            cut = getattr(self, "cut", 99)
            if cut <= 1:
                continue
            if ctx_out is not None:
                self.stt(zf.t[:, 0:128], pb.t[:, 0:128], st.t[:, 0:1], bc.t[:, 0:128], ALU.mult, ALU.mult, [pb.r, st.r, bc.r], [zf.r])
                self.cp(zf.t[:, 128:160], pb.t[:, 128:160], [pb.r], [zf.r])
                for h in range(2):
                    self.stt(zf.t[:, 160 + 64 * h:224 + 64 * h], pb.t[:, 160 + 64 * h:224 + 64 * h], st.t[:, 1 + h:2 + h],
                             bc.t[:, 448:512], ALU.mult, ALU.mult, [pb.r, st.r, bc.r], [zf.r])
                self.cp(zf.t[:, 288:416], pb.t[:, 288:416], [pb.r], [zf.r], eng="act")
                self.cp(k_.t[:, 0, :], zf.t[:, 0:128], [zf.r], [k_.r], eng="act")
                self.cp(k_.t[:, 1, 64:96], zf.t[:, 128:160], [zf.r], [k_.r])
                self.cp(k_.t[:, 2, :], zf.t[:, 160:288], [zf.r], [k_.r], eng="act")
                seq = ctx_out + t // 2
                r0 = (t % 2) * 128
                self.dma("sp", self.o_ckv[seq, l, r0:r0 + 128, :], zf.t[:, 0:128], reads=[zf.r], key="zkf")
                self.dma("sp", self.o_kpe[seq, l, r0:r0 + 128, :], zf.t[:, 128:160], reads=[zf.r], key="zkf")
                self.dma("sp", self.o_gk[seq, l, r0:r0 + 128, :], zf.t[:, 160:288], reads=[zf.r], key="zkf")
                self.dma("sp", self.o_gv[seq, l, r0:r0 + 128, :], zf.t[:, 288:416], reads=[zf.r], key="zkf")
            else:
                ti = rope_tile0 + t
                self.stt(k_.t[:, 0, :], pb.t[:, 0:128], st.t[:, 0:1], bc.t[:, 0:128], ALU.mult, ALU.mult, [pb.r, st.r, bc.r], [k_.r])
                xin = pb.t[:, 128:160].rearrange("p (h a b f) -> p h a b f", h=1, a=2, b=2)
                xo = k_.t[:, 1, 64:96].rearrange("p (h a b f) -> p h a b f", h=1, a=2, b=2)
                self.rope_tm(xo, xin, ti, 0, 8, 1, [pb.r], [k_.r])
                g = self.gkf
                for h in range(2):
                    self.stt(g.t[:, 64 * h:64 * h + 64], pb.t[:, 160 + 64 * h:224 + 64 * h], st.t[:, 1 + h:2 + h],
                             bc.t[:, 448:512], ALU.mult, ALU.mult, [pb.r, st.r, bc.r], [g.r])
                xin = g.t[:].rearrange("p (h a b f) -> p h a b f", h=2, a=2, b=2)
                xo = k_.t[:, 2, :].rearrange("p (h a b f) -> p h a b f", h=2, a=2, b=2)
                self.rope_tm(xo, xin, ti, 8, 16, 2, [g.r], [k_.r])
            if cut <= 2:
                continue
            self.cp(self.Vgqa.t[:, kt0 + t, :, 0:64], pb.t[:, 288:416].rearrange("p (h x) -> p h x", h=2), [pb.r], [self.Vgqa.r],
                    eng="act")
            if cut <= 3:
                continue
            self.kv_transposes(t, kt0 + t)
        if getattr(self, "cut", 99) <= 4:
            return
        self.kv_finish(ntiles, kt0)

    def cache_kv(self, l):
        k_ = self.kst
        for t in range(2):
            r0 = t * 128
            self.dma("pool", k_.t[:, 0, :], self.c_ckv[l, r0:r0 + 128, :], writes=[k_.r], key="kst")
            self.dma("pool", k_.t[:, 1, 64:96], self.c_kpe[l, r0:r0 + 128, :], writes=[k_.r], key="kst")
            self.dma("pool", k_.t[:, 2, :], self.c_gk[l, r0:r0 + 128, :], writes=[k_.r], key="kst")
            self.dma("pool", self.Vgqa.t[:, t, :, 0:64], self.c_gv[l, r0:r0 + 128, :].rearrange("p (h x) -> p h x", h=2),
                     writes=[self.Vgqa.r], key="Vgqa")
            self.kv_transposes(t, t)
        self.kv_finish(2, 0)

    def hgrn_pass(self, l, d, mcol0, segments, ctx_seq0=None):
        mres = self.mixHG.r[mcol0 // TB]
        mcols = slice(mcol0, mcol0 + TB)
        hf_off = O_HFB if d else O_HFF
        slotA, wA = self.win_slot(l, [(O_HQ, 256), (hf_off, 256)])
        slotB, wB = self.win_slot(l, [(O_HI, 512 if d else 256)])
        ht = self.ht
        P = self.P
        for t in range(4):
            pb = P[2]
            for k in range(KC):
                self.mm(pb.t[:, 0:256], self.hT.t[:, k, t * 128:(t + 1) * 128], wB[:, k, 0:256], k == 0, k == KC - 1,
                        [self.hT.r, slotB.r], [pb.r])
            self.cp(self.itm.t[:, t, :], pb.t[:, 0:256], [pb.r], [self.itm.r], eng=("act" if t % 2 else "dve"))
        mask = self.mAb if d else self.mAf
        for c in range(2):
            lb = self.lbp.t[:, l, d, c, 0:1]
            om = self.lbp.t[:, l, d, c, 1:2]
            vr32 = self.V32.r[d * 2 + c]
            pq, pf = P[0], P[1]
            for k in range(KC):
                self.mm(pq.t[:, :], wA[:, k, c * 128:(c + 1) * 128], self.hT.t[:, k, :], k == 0, k == KC - 1, [slotA.r, self.hT.r], [pq.r])
            for k in range(KC):
                self.mm(pf.t[:, :], wA[:, k, 256 + c * 128:256 + (c + 1) * 128], self.hT.t[:, k, :], k == 0, k == KC - 1,
                        [slotA.r, self.hT.r], [pf.r])
            h0, h1, h2, h3, h4 = [x.t for x in ht]
            r0, r1, r2, r3, r4 = [x.r for x in ht]
            q = self.hq32.t[:, c, :]
            rq = self.hq32.r
            self.act(h0[:], pq.t[:], AF.Exp, [pq.r], [r0], scale=-1.0)
            self.ts(h0[:], h0[:], 1.0, None, ALU.add, None, [r0], [r0])
            self.recip(h0[:], h0[:], [r0], [r0])
            self.stt(q, pq.t[:], 0.125, h0[:], ALU.mult, ALU.mult, [pq.r, r0], [rq])
            self.act(h1[:], pf.t[:], AF.Exp, [pf.r], [r1], scale=-1.0)
            self.ts(h1[:], h1[:], 1.0, None, ALU.add, None, [r1], [r1])
            self.recip(h1[:], h1[:], [r1], [r1])
            self.ts(h2[:], h1[:], om, lb, ALU.mult, ALU.add, [r1, self.lbp.r], [r2])
            self.act(h3[:], h2[:], AF.Identity, [r2], [r3], scale=-1.0, bias=1.0)
            self.act(h2[:], h2[:], AF.Ln, [r2], [r2], bias=1e-30)
            self.op("dve", lambda e: e.tensor_tensor_scan(out=h4[:], data0=self.segmask.t[:], data1=h2[:], initial=0.0,
                                                          op0=ALU.mult, op1=ALU.add), [self.segmask.r, r2], [r4])
            v3 = lambda ap: ap.rearrange("p (c t) -> p c t", t=32)
            if d == 0:
                bT, rb = h4, r4
            else:
                self.tt(h1[:], h2[:], h4[:], ALU.subtract, [r2, r4], [r1])
                self.tt(v3(h2[:]), v3(h1[:]), v3(h4[:])[:, :, 31:32].broadcast_to([128, 16, 32]), ALU.add, [r1, r4], [r2])
                bT, rb = h2, r2
            mid = v3(bT[:])[:, :, 16:17]
            bend = v3(bT[:])[:, :, 31:32] if d == 0 else v3(bT[:])[:, :, 0:1]
            hs = self.hsm
            self.act(hs.t[:, 0, :], mid[:, :, 0], AF.Exp, [rb], [hs.r])
            self.tt(hs.t[:, 1, :], bend[:, :, 0], mid[:, :, 0], ALU.subtract, [rb], [hs.r])
            self.act(hs.t[:, 1, :], hs.t[:, 1, :], AF.Exp, [hs.r], [hs.r])
            self.act(hs.t[:, 2, :], bend[:, :, 0], AF.Exp, [rb], [hs.r])
            self.tt(v3(h1[:]), v3(bT[:]), mid.broadcast_to([128, 16, 32]), ALU.subtract, [rb], [r1])
            self.act(h0[:], h1[:], AF.Exp, [r1], [r0])
            self.act(h1[:], h1[:], AF.Exp, [r1], [r1], scale=-1.0)
            self.tt(self.qh.t[:], q, h0[:], ALU.mult, [rq, r0], [self.qh.r])
            self.tt(self.kh.t[:], h3[:], h1[:], ALU.mult, [r3, r1], [self.kh.r])
            self.tt(v3(h0[:]), v3(h0[:]), hs.t[:, 0, :][:, :, None].broadcast_to([128, 16, 32]), ALU.mult, [r0, hs.r], [r0])
            self.tt(self.qt.t[:], q, h0[:], ALU.mult, [rq, r0], [self.qt.r])
            self.tt(v3(h1[:]), v3(h1[:]), hs.t[:, 1, :][:, :, None].broadcast_to([128, 16, 32]), ALU.mult, [r1, hs.r], [r1])
            self.tt(self.kd.t[:], h3[:], h1[:], ALU.mult, [r3, r1], [self.kd.r])
            for t in range(4):
                pt = self.PT[t % 2]
                self.tp(pt.t[:, 0:128], self.kd.t[:, t * 128:(t + 1) * 128], self.ident.t[:], [self.kd.r, self.ident.r], [pt.r])
                self.cp(self.kdtm.t[:, t, :], pt.t[:, 0:128], [pt.r], [self.kdtm.r], eng=("act" if t % 2 else "dve"))
            psA, psO = [P[2], P[0]], [P[3], P[4]]
            psTl = [(P[5].t, P[5].r), (self.PT[1].t[:].bitcast(F32), self.PT[1].r)]
            NV = self.NV
            step = 0
            tcount = 0
            for (order, reset, out_seq) in segments:
                s0 = step % NV
                if reset:
                    self.memset(self.Vr32.t[:, s0, :], 0.0, [self.Vr32.r[s0]], eng="dve")
                    self.memset(self.Vrb.t[:, s0, :], 0.0, [self.Vrb.r[s0]], eng="dve")
                else:
                    self.cp(self.Vr32.t[:, s0, :], self.V32.t[:, d, c, :], [vr32], [self.Vr32.r[s0]])
                    self.cp(self.Vrb.t[:, s0, :], self.V32.t[:, d, c, :], [vr32], [self.Vrb.r[s0]], eng="act")
                for t in order:
                    tc = slice(t * 128, (t + 1) * 128)
                    psT, psTr = psTl[tcount % 2]
                    tcount += 1
                    for h in range(2):
                        hp = slice(64 * h, 64 * h + 64)
                        self.mm(psA[h].t[:, 0:128], self.kh.t[hp, tc], self.qh.t[hp, tc], True, True,
                                [self.kh.r, self.qh.r], [psA[h].r])
                        self.tt(self.ATm.t[:, h, :], psA[h].t[:, 0:128], mask.t[:], ALU.mult, [psA[h].r, mask.r], [self.ATm.r])
                    for h in range(2):
                        hp = slice(64 * h, 64 * h + 64)
                        self.mm(psO[h].t[hp, tc], self.itm.t[:, t, c * 128 + 64 * h:c * 128 + 64 * h + 64], self.ATm.t[:, h, :],
                                True, False, [self.itm.r, self.ATm.r], [psO[h].r], tile_position=(0, 64 * h))
                    jorder = list(range(4) if d == 0 else range(3, -1, -1))
                    self.tt(self.iexp.t[:], self.itm.t[:, t, c * 128:(c + 1) * 128][:, None, :].broadcast_to([128, 4, 128]),
                            self.cmask4.t[:][:, :, None].broadcast_to([128, 4, 128]), ALU.mult, [self.itm.r, self.cmask4.r], [self.iexp.r])
                    for h in range(2):
                        hp = slice(64 * h, 64 * h + 64)
                        self.mm(psT[hp, 0:256].rearrange("p (j x) -> p j x", j=4), self.kdtm.t[:, t, hp], self.iexp.t[:, :, hp],
                                True, True, [self.kdtm.r, self.iexp.r], [psTr], tile_position=(0, 64 * h))
                    for j in jorder:
                        cc = slice(t * 128 + 32 * j, t * 128 + 32 * j + 32)
                        sv, sn = step % NV, (step + 1) % NV
                        for h in range(2):
                            hp = slice(64 * h, 64 * h + 64)
                            self.mm(psO[h].t[hp, cc], self.Vrb.t[hp, sv, :], self.qt.t[hp, cc], False, j == jorder[-1],
                                    [self.Vrb.r[sv], self.qt.r], [psO[h].r], tile_position=(64 * h, 64 * h))
                        ci = t * 4 + j
                        self.stt(self.Vr32.t[:, sn, :], self.Vr32.t[:, sv, :], self.hsm.t[:, 2, ci:ci + 1], psT[:, j * 64:(j + 1) * 64],
                                 ALU.mult, ALU.add, [self.Vr32.r[sv], self.hsm.r, psTr], [self.Vr32.r[sn]])
                        self.cp(self.Vrb.t[:, sn, :], self.Vr32.t[:, sn, :], [self.Vr32.r[sn]], [self.Vrb.r[sn]], eng="act")
                        step += 1
                se = step % NV
                if out_seq is not None:
                    self.dma("sp", self.o_hg[out_seq, l, d, c * 128:(c + 1) * 128, :], self.Vr32.t[:, se, :], reads=[self.Vr32.r[se]],
                             key="Vr32_%d" % se)
                else:
                    self.cp(self.V32.t[:, d, c, :], self.Vr32.t[:, se, :], [self.Vr32.r[se]], [vr32])
            if d == 0:
                for h in range(2):
                    hp = slice(64 * h, 64 * h + 64)
                    self.cp(self.mixHG.t[hp, c, mcols], psO[h].t[hp, :], [psO[h].r], [mres], eng="act")
            else:
                pg = P[1]
                for k in range(KC):
                    self.mm(pg.t[:, :], wB[:, k, 256 + c * 128:256 + (c + 1) * 128], self.hT.t[:, k, :], k == 0, k == KC - 1,
                            [slotB.r, self.hT.r], [pg.r])
                for h in range(2):
                    hp = slice(64 * h, 64 * h + 64)
                    self.tt(h0[hp, :], psO[h].t[hp, :], self.mixHG.t[hp, c, mcols], ALU.add, [psO[h].r, mres], [r0])
                self.act(h1[:], h0[:], AF.Square, [r0], [r1])
                pss = P[2]
                self.mm(pss.t[:, :], self.onesblk.t[:], h1[:], True, True, [self.onesblk.r, r1], [pss.r])
                self.act(h2[:], pss.t[:], AF.Ln, [pss.r], [r2], scale=1.0 / 64.0, bias=EPS)
                self.act(h2[:], h2[:], AF.Exp, [r2], [r2], scale=-0.5)
                self.tt(h0[:], h0[:], h2[:], ALU.mult, [r0, r2], [r0])
                self.act(h3[:], pg.t[:], AF.Exp, [pg.r], [r3], scale=-1.0)
                self.ts(h3[:], h3[:], 1.0, None, ALU.add, None, [r3], [r3])
                self.recip(h3[:], h3[:], [r3], [r3])
                self.stt(h0[:], h0[:], self.hgn.t[:, l:l + 1], pg.t[:], ALU.mult, ALU.mult, [r0, self.hgn.r, pg.r], [r0])
                self.tt(self.mixHG.t[:, c, mcols], h0[:], h3[:], ALU.mult, [r0, r3], [mres])

    def load_hgrn_state(self, l):
        for d in range(2):
            for c in range(2):
                i = d * 2 + c
                self.dma("sp", self.V32.t[:, d, c, :], self.s_hg[l, d, c * 128:(c + 1) * 128, :], writes=[self.V32.r[i]],
                         key="V32_%d" % i)

    def q_pass(self, l, rope_tile0=None):
        slot1, w1 = self.win_slot(l, [(O_MQ, 256)])
        slot2, w2 = self.win_slot(l, [(O_GQ, 384)])
        bc = self.bc
        w = self.wsm
        wuq = w.t[:, 0:1152].rearrange("p (k n) -> p k n", k=2)
        P = self.P
        st = self.st8
        for t in range(4):
            tcs = slice(t * 128, (t + 1) * 128)
            p0, p1, p2, p3 = P[2 * (t % 2)], P[2 * (t % 2) + 1], P[4], P[5]
            for k in range(KC):
                self.mm(p0.t[:, 0:256], self.hT.t[:, k, tcs], w1[:, k, :], k == 0, k == KC - 1, [self.hT.r, slot1.r], [p0.r])
            for k in range(KC):
                self.mm(p1.t[:, 0:384], self.hT.t[:, k, tcs], w2[:, k, :], k == 0, k == KC - 1, [self.hT.r, slot2.r], [p1.r])
            self.act(self.sq.t[:, 0:256], p0.t[:, 0:256], AF.Square, [p0.r], [self.sq.r, st.r], scale=1.0 / 16.0,
                     accum_out=st.t[:, 0:1])
            self.act(st.t[:, 0:1], st.t[:, 0:1], AF.Ln, [st.r], [st.r], bias=EPS)
            self.act(self.sq.t[:, 0:384], p1.t[:, 0:384], AF.Square, [p1.r], [self.sq.r])
            self.op("dve", lambda e: e.tensor_reduce(out=st.t[:, 1:7], in_=self.sq.t[:, 0:384].rearrange("p (h x) -> p h x", h=6),
                                                     axis=AX.X, op=ALU.add), [self.sq.r], [st.r])
            self.act(st.t[:, 1:7], st.t[:, 1:7], AF.Ln, [st.r], [st.r], scale=1.0 / 64.0, bias=EPS)
            self.act(st.t[:, 0:7], st.t[:, 0:7], AF.Exp, [st.r], [st.r], scale=-0.5)
            k_ = self.kst
            mqn = k_.t[:, 0:2, :]
            self.stt(mqn, p0.t[:, 0:256].rearrange("p (a b) -> p a b", a=2), st.t[:, 0:1],
                     bc.t[:, 128:384].rearrange("p (a b) -> p a b", a=2), ALU.mult, ALU.mult, [p0.r, st.r, bc.r], [k_.r])
            pt0 = self.PT[0]
            self.tp(pt0.t[:, 0:128], k_.t[:, 0, :], self.ident.t[:], [k_.r, self.ident.r], [pt0.r])
            self.tp(pt0.t[:, 128:256], k_.t[:, 1, :], self.ident.t[:], [k_.r, self.ident.r], [pt0.r])
            self.cp(self.mqnT.t[:], pt0.t[:, 0:256].rearrange("p (a b) -> p a b", a=2), [pt0.r], [self.mqnT.r], eng="act")
            for kk in range(2):
                self.mm(p2.t[:, 0:480], self.mqnT.t[:, kk, :], wuq[:, kk, 0:480], kk == 0, kk == 1, [self.mqnT.r, w.r], [p2.r])
            for kk in range(2):
                self.mm(p3.t[:, 0:96], self.mqnT.t[:, kk, :], wuq[:, kk, 480:576], kk == 0, kk == 1, [self.mqnT.r, w.r], [p3.r])
            qs = self.qstage
            v5 = p2.t[:, 0:480].rearrange("p (h x) -> p h x", h=5)
            self.cp(qs.t[:, 0:5, 0:64], v5[:, :, 0:64], [p2.r], [qs.r], eng="act")
            self.cp(qs.t[:, 5, 0:64], p3.t[:, 0:64], [p3.r], [qs.r], eng="act")
            if rope_tile0 is None:
                self.cp(qs.t[:, 0:5, 64:96], v5[:, :, 64:96], [p2.r], [qs.r])
                self.cp(qs.t[:, 5, 64:96], p3.t[:, 64:96], [p3.r], [qs.r])
            else:
                qp = self.qpe
                self.cp(qp.t[:, 0:5, :], v5[:, :, 64:96], [p2.r], [qp.r])
                self.cp(qp.t[:, 5, :], p3.t[:, 64:96], [p3.r], [qp.r])
                self.rope_tm(qs.t[:, :, 64:96].rearrange("p h (a b f) -> p h a b f", a=2, b=2),
                             qp.t[:].rearrange("p h (a b f) -> p h a b f", a=2, b=2), rope_tile0 + t, 0, 8, 6, [qp.r], [qs.r])
            pt1 = self.PT[1]
            for h in range(6):
                self.tp(pt1.t[0:96, h * 128:(h + 1) * 128], qs.t[:, h, :], self.ident.t[:], [qs.r, self.ident.r], [pt1.r])
            self.cp(self.QmlaT.t[0:96, :, tcs], pt1.t[0:96, 0:768].rearrange("p (h x) -> p h x", h=6), [pt1.r], [self.QmlaT.r])
            z = self.zq
            zv = z.t[:, 0:384].rearrange("p (h x) -> p h x", h=6)
            self.tt(zv, p1.t[:, 0:384].rearrange("p (h x) -> p h x", h=6), st.t[:, 1:7][:, :, None].broadcast_to([128, 6, 64]),
                    ALU.mult, [p1.r, st.r], [z.r])
            gq = self.gqb
            gperm = gq.t[:].rearrange("p (j a x) -> p a j x", a=2, x=64)
            znat = z.t[:, 0:384].rearrange("p (a j x) -> p a j x", a=2, x=64)
            ggq4 = bc.t[:, 384:448][:, None, None, :].broadcast_to([128, 2, 3, 64])
            ggq = bc.t[:, 384:448][:, None, :].broadcast_to([128, 6, 64])
            if rope_tile0 is None:
                self.tt(gperm, znat, ggq4, ALU.mult, [z.r, bc.r], [gq.r])
            else:
                self.tt(zv, zv, ggq, ALU.mult, [z.r, bc.r], [z.r])
                self.rope_tm(k_.t[:].rearrange("p c (h2 a b f) -> p (c h2) a b f", h2=2, a=2, b=2),
                             z.t[:, 0:384].rearrange("p (h a b f) -> p h a b f", h=6, a=2, b=2), rope_tile0 + t, 8, 16, 6,
                             [z.r], [k_.r])
                self.cp(gperm, k_.t[:].rearrange("p c x -> p (c x)").rearrange("p (a j x) -> p a j x", a=2, x=64), [k_.r], [gq.r])
            for j in range(3):
                self.tp(pt0.t[:, 256 + j * 128:256 + (j + 1) * 128], gq.t[:, j * 128:(j + 1) * 128], self.ident.t[:],
                        [gq.r, self.ident.r], [pt0.r])
            self.cp(self.QgqaT.t[:, :, tcs], pt0.t[:, 256:640].rearrange("p (j x) -> p j x", j=3), [pt0.r], [self.QgqaT.r], eng="act")

    def attention(self, N, qcol0, keytiles):
        P = self.P
        qc = slice(qcol0, qcol0 + N)
        nk = len(keytiles)

        def head_ops(hh):
            if hh < 6:
                h = hh
                return (lambda kt: self.KmlaT.t[0:96, h, kt * 128:(kt + 1) * 128], self.QmlaT.t[0:96, h, qc],
                        lambda kt: self.Vmla.t[:, kt, h, :], self.KmlaT.r, self.QmlaT.r, self.Vmla.r, MLA_SCALE)
            g = hh - 6
            part, j = g // 3, g % 3
            pp = slice(64 * part, 64 * part + 64)
            return (lambda kt: self.KgqaT.t[pp, kt * 128:(kt + 1) * 128], self.QgqaT.t[pp, j, qc],
                    lambda kt: self.Vgqa.t[:, kt, part, :], self.KgqaT.r, self.QgqaT.r, self.Vgqa.r, GQA_SCALE)

        units = [(hh, i) for hh in range(12) for i in range(nk)]
        hops = {hh: head_ops(hh) for hh in range(12)}

        def issue_qk(idx):
            hh, i = units[idx]
            Kt, Qt, Vv, kr, qr, vr, scale = hops[hh]
            psS = P[idx % 3]
            self.mm(psS.t[:, 0:N], Kt(keytiles[i]), Qt, True, True, [kr, qr], [psS.r])

        def finalize2(hh):
            po = P[4 + hh % 2]
            rc = self.rc
            pb = P[3]
            self.mm(pb.t[0:64, 0:N], self.onesf.t[64:65, 0:64], rc.t[64:65, 0:N], True, True, [self.onesf.r, rc.r], [pb.r])
            bcs = self.bcs
            self.cp(bcs.t[:, 0:N], pb.t[0:64, 0:N], [pb.r], [bcs.r])
            self.tt(self.mixA.t[0:64, hh, qc], po.t[0:64, 0:N], bcs.t[:, 0:N], ALU.mult, [po.r, bcs.r], [self.mixA.r[hh]])

        issue_qk(0)
        if len(units) > 1:
            issue_qk(1)
        pending = []
        for idx, (hh, i) in enumerate(units):
            Kt, Qt, Vv, kr, qr, vr, scale = hops[hh]
            psS = P[idx % 3]
            pt = self.PTb[idx % 3]
            po = P[4 + hh % 2]
            self.act(pt.t[:, 0:N], psS.t[:, 0:N], AF.Exp, [psS.r], [pt.r], scale=scale)
            if idx + 2 < len(units):
                issue_qk(idx + 2)
            self.mm(po.t[0:65, 0:N], Vv(keytiles[i]), pt.t[:, 0:N], i == 0, i == nk - 1, [vr, pt.r], [po.r])
            while pending and pending[0][0] <= idx:
                finalize2(pending.pop(0)[1])
            if i == nk - 1:
                self.recip(self.rc.t[64:65, 0:N], po.t[64:65, 0:N], [po.r], [self.rc.r], exact=True)
                pending.append((idx + min(8, nk), hh))
        while pending:
            finalize2(pending.pop(0)[1])

    def layernorm(self, l, which):
        gname, bname = ("ln1_g", "ln1_b") if which == 1 else ("ln2_g", "ln2_b")
        P = self.P
        pm, pv = P[2], P[3]
        tmps = [self.lnt, self.lnr]
        for k in range(KC):
            self.mm(pm.t[:, :], self.onesf.t[:], self.xb.t[:, k, :], k == 0, k == KC - 1, [self.onesf.r, self.xb.r[k]], [pm.r])
            tq = tmps[k % 2]
            self.act(tq.t[:], self.xb.t[:, k, :], AF.Square, [self.xb.r[k]], [tq.r])
            self.mm(pv.t[:, :], self.onesf.t[:], tq.t[:], k == 0, k == KC - 1, [self.onesf.r, tq.r], [pv.r])
        m, r, t_ = self.lnm, self.lnr, self.lnt
        self.act(m.t[:], pm.t[:], AF.Copy, [pm.r], [m.r], scale=1.0 / D)
        self.tt(t_.t[:], m.t[:], m.t[:], ALU.mult, [m.r], [t_.r])
        self.stt(r.t[:], pv.t[:], 1.0 / D, t_.t[:], ALU.mult, ALU.subtract, [pv.r, t_.r], [r.r])
        self.act(r.t[:], r.t[:], AF.Ln, [r.r], [r.r], bias=EPS)
        self.act(r.t[:], r.t[:], AF.Exp, [r.r], [r.r], scale=-0.5)
        self.stt(m.t[:], m.t[:], -1.0, r.t[:], ALU.mult, ALU.mult, [m.r, r.r], [m.r])
        t2 = [self.lnt, self.gy[0], self.gy[1]]
        for k in range(KC):
            tq = t2[k % 3]
            self.tt(tq.t[:], self.xb.t[:, k, :], r.t[:], ALU.mult, [self.xb.r[k], r.r], [tq.r])
            self.tt(tq.t[:], tq.t[:], m.t[:], ALU.add, [tq.r, m.r], [tq.r], eng="pool")
            self.act(self.xb.t[:, k, :], tq.t[:], AF.Identity, [tq.r, self.par.r], [self.xb.r[k]],
                     scale=self.pcol(gname, l, k), bias=self.pcol(bname, l, k))

    def outproj_ln1(self, l, g, mcol0, mix_own=None):
        P = self.P
        mres = self.mixHG.r[mcol0 // TB]
        for dp in range(4):
            c0 = dp * 256
            slot = self.wload([
                (lambda t: t[:, 0:3584].rearrange("p (k n) -> p k n", k=14)[:, 0:2, :],
                 self.w_out[l][0:256, c0:c0 + 256].rearrange("(k p) n -> p k n", p=128)),
                (lambda t: t[0:64, 0:3584].rearrange("p (k n) -> p k n", k=14)[:, 2:14, :],
                 self.w_out[l][256:1024, c0:c0 + 256].rearrange("(h p) n -> p h n", p=64)),
            ])
            wv = slot.t[:, 0:3584].rearrange("p (k n) -> p k n", k=14)
            for dd in range(2):
                dc = dp * 2 + dd
                py = P[dc % 2]
                for kc in range(2):
                    mrhs = self.mixHG.t[:, kc, mcol0:mcol0 + TB] if mix_own is None else mix_own[:, kc, :]
                    self.mm(py.t[:, :], wv[:, kc, dd * 128:(dd + 1) * 128], mrhs, kc == 0, False,
                            [slot.r, mres] + ([self.mixHG.r[1]] if mix_own is not None else []), [py.r])
                for hh in range(12):
                    self.mm(py.t[:, :], wv[0:64, 2 + hh, dd * 128:(dd + 1) * 128], self.mixA.t[0:64, hh, :], False, hh == 11,
                            [slot.r, self.mixA.r[hh]], [py.r])
                gy = self.gy[dc % 2]
                self.act(gy.t[:], py.t[:], AF.Copy, [py.r, self.mT.r], [gy.r], scale=self.mvec(l, g, "g1", dc))
                self.stt(self.xb.t[:, dc, :], self.xb.t[:, dc, :], ALPHA, gy.t[:], ALU.mult, ALU.add, [self.xb.r[dc], gy.r], [self.xb.r[dc]])
        self.layernorm(l, 1)

    def ffn_ln2(self, l, g):
        P = self.P
        self.prepH(l, g, "2")
        first = not self.ffn_cached[l]
        self.ffn_cached[l] = True
        for fp in range(11):
            if first:
                slot = self.wload([
                    (lambda t: t[:, 0:4096].rearrange("p (k n) -> p k n", k=8)[:, :, 0:256],
                     self.w_ffn_in[l][:, fp * 256:(fp + 1) * 256].rearrange("(k p) n -> p k n", p=128)),
                    (lambda t: t[:, 0:4096].rearrange("p (k n) -> p k n", k=8)[:, :, 256:512],
                     self.w_ffn_in[l][:, DFF + fp * 256:DFF + (fp + 1) * 256].rearrange("(k p) n -> p k n", p=128)),
                ])
                self.dma("sp", self.wsc_in[l][fp * 128:(fp + 1) * 128, :], slot.t[:, 0:4096], reads=[slot.r], writes=[self.wscr[l]],
                         key=slot.r.name + "_st")
            else:
                slot = self.next_ring()
                self.dma("sp", slot.t[:, 0:4096], self.wsc_in[l][fp * 128:(fp + 1) * 128, :], reads=[self.wscr[l]], writes=[slot.r],
                         key=slot.r.name + "_hw")
            wv = slot.t[:, 0:4096].rearrange("p (k n) -> p k n", k=8)
            for ff in range(2):
                f = fp * 2 + ff
                pg, pu = P[f % 2], P[2 + f % 2]
                for k in range(KC):
                    self.mm(pg.t[:, :], wv[:, k, ff * 128:(ff + 1) * 128], self.hT.t[:, k, :], k == 0, k == KC - 1, [slot.r, self.hT.r], [pg.r])
                for k in range(KC):
                    self.mm(pu.t[:, :], wv[:, k, 256 + ff * 128:256 + (ff + 1) * 128], self.hT.t[:, k, :], k == 0, k == KC - 1,
                            [slot.r, self.hT.r], [pu.r])
                sg = self.gy[f % 2]
                self.act(sg.t[:], pg.t[:], AF.Silu, [pg.r], [sg.r])
                self.tt(self.actT.t[:, f, :], sg.t[:], pu.t[:], ALU.mult, [sg.r, pu.r], [self.actT.r])
        for dc in range(KC):
            if first:
                slot = self.wload([(lambda t: t[:, 0:2816].rearrange("p (f n) -> p f n", f=NFC),
                                    self.w_ffn_out[l][:, dc * 128:(dc + 1) * 128].rearrange("(f p) n -> p f n", p=128))])
                self.dma("sp", self.wsc_out[l][dc * 128:(dc + 1) * 128, :], slot.t[:, 0:2816], reads=[slot.r], writes=[self.wscr[l]],
                         key=slot.r.name + "_st")
            else:
                slot = self.next_ring()
                self.dma("sp", slot.t[:, 0:2816], self.wsc_out[l][dc * 128:(dc + 1) * 128, :], reads=[self.wscr[l]], writes=[slot.r],
                         key=slot.r.name + "_hw")
            wv = slot.t[:, 0:2816].rearrange("p (f n) -> p f n", f=NFC)
            pd = P[4 + dc % 2]
            for f in range(NFC):
                self.mm(pd.t[:, :], wv[:, f, :], self.actT.t[:, f, :], f == 0, f == NFC - 1, [slot.r, self.actT.r], [pd.r])
            gy = self.lnt if dc % 2 else self.lnr
            self.act(gy.t[:], pd.t[:], AF.Copy, [pd.r, self.mT.r], [gy.r], scale=self.mvec(l, g, "g2", dc))
            self.stt(self.xb.t[:, dc, :], self.xb.t[:, dc, :], ALPHA, gy.t[:], ALU.mult, ALU.add, [self.xb.r[dc], gy.r], [self.xb.r[dc]])
        self.layernorm(l, 2)

    def run_ctx_block(self, cb):
        self.load_x_block(self.xc[cb * TB:(cb + 1) * TB, :])
        for l in range(self.nlayers):
            self.load_layer_small(l)
            self.prepH(l, 0)
            self.phase("c%d.l%d.kv" % (cb, l))
            self.kv_pass(l, 4, 0, ctx_out=2 * cb)
            self.phase("c%d.l%d.hgf" % (cb, l))
            self.hgrn_pass(l, 0, 0, [([0, 1], True, 2 * cb), ([2, 3], True, 2 * cb + 1)])
            self.phase("c%d.l%d.hgb" % (cb, l))
            self.hgrn_pass(l, 1, 0, [([1, 0], True, 2 * cb), ([3, 2], True, 2 * cb + 1)])
            self.phase("c%d.l%d.q" % (cb, l))
            self.q_pass(l, None)
            self.phase("c%d.l%d.att" % (cb, l))
            self.attention(256, 0, [0, 1])
            self.attention(256, 256, [2, 3])
            self.phase("c%d.l%d.outp" % (cb, l))
            self.outproj_ln1(l, 0, 0)
            self.phase("c%d.l%d.ffn" % (cb, l))
            self.ffn_ln2(l, 0)
        self.store_y_block(self.y_c[cb * TB:(cb + 1) * TB, :])

    def run_sample(self):
        for blk in range(4):
            self.load_x_block(self.xsd[blk * TB:(blk + 1) * TB, :])
            self.store_xs_block(blk)
        for l in range(self.nlayers):
            self.load_layer_small(l)
            self.load_hgrn_state(l)
            self.cache_kv(l)
            for blk in range(4):
                self.load_xs_block(blk)
                self.prepH(l, 1)
                self.phase("s.l%d.b%d.kv" % (l, blk))
                self.kv_pass(l, 4, 2 + 4 * blk, rope_tile0=4 * blk)
                self.phase("s.l%d.b%d.hgf" % (l, blk))
                self.hgrn_pass(l, 0, blk * TB, [([0, 1, 2, 3], False, None)])
            for blk in range(3, -1, -1):
                self.load_xs_block(blk)
                self.prepH(l, 1)
                self.phase("s.l%d.b%d.hgb" % (l, blk))
                self.hgrn_pass(l, 1, blk * TB, [([3, 2, 1, 0], False, None)])
            last = (l == self.nlayers - 1)
            if not last:
                for qb in range(4):
                    self.load_xs_block(qb)
                    self.prepH(l, 1)
                    self.phase("s.l%d.q%d.q" % (l, qb))
                    self.q_pass(l, rope_tile0=4 * qb)
                    self.phase("s.l%d.q%d.att" % (l, qb))
                    self.attention(TB, 0, list(range(18)))
                    self.phase("s.l%d.q%d.outp" % (l, qb))
                    self.outproj_ln1(l, 1, qb * TB)
                    self.phase("s.l%d.q%d.ffn" % (l, qb))
                    self.ffn_ln2(l, 1)
                    self.store_xs_block(qb)
            else:
                for blk in range(4):
                    self.dma("sp", self.mh2[blk * 128:(blk + 1) * 128, :].rearrange("p (c n) -> p c n", c=2),
                             self.mixHG.t[:, :, blk * TB:(blk + 1) * TB], reads=[self.mixHG.r[blk]], writes=[self.mh2r], key="mh2")
                self.gather(self.mixHG.t[:, 0, 0:2 * TB], self.mh2[:, :], [self.mh2r], [self.mixHG.r[0], self.mixHG.r[1]], "mhg")
                self.gather(self.rope.t[:, 0:4].rearrange("p a b c d -> p (a b c d)"), self.rp2[:, :], [self.rp2r], [self.rope.r], "rpg")
                self.gather(self.xb.t[:].rearrange("p k n -> p (k n)"), self.xs_dram[:, :], self.xsr, [self.xb.r], "xbg")
                if "xbg" in self.dbg_out:
                    jj = self.dbg_j
                    self.dump("xbg", self.xb.t[:].rearrange("p k n -> p (k n)"), self.xb.r)
                    self.dma("sp", self.dbg_out["xbs"], self.xs_dram[jj * 128:(jj + 1) * 128, :], reads=self.xsr, key="dbgx")
                    self.dump("rpg", self.rope.t[:, 0:4].rearrange("p a b c d -> p (a b c d)"), self.rope.r)
                    self.dma("sp", self.dbg_out["rps"], self.rp2[jj * 128:(jj + 1) * 128, :], reads=[self.rp2r], key="dbgr")
                    self.dma("sp", self.dbg_out["mhs"], self.mh2[jj * 128:(jj + 1) * 128, :], reads=[self.mh2r], key="dbgm")
                    self.dma("sp", self.dbg_out["mhg"], self.mixHG.t[:, 0, 0:2 * TB], reads=[self.mixHG.r[0]], key="dbgm2")
                self.prepH(l, 1)
                self.phase("s.l%d.own.q" % l)
                self.q_pass(l, rope_tile0=0)
                self.phase("s.l%d.own.att" % l)
                self.attention(TB, 0, list(range(18)))
                self.phase("s.l%d.own.outp" % l)
                self.outproj_ln1(l, 1, 0, mix_own=self.mixHG.t[:, 0, 0:2 * TB].rearrange("p (c n) -> p c n", c=2))
                self.phase("s.l%d.own.ffn" % l)
                self.ffn_ln2(l, 1)
                self.store_y_block(self.y_own[:, :])

    def build(self):
        self.init_consts()
        self.memset(self.kst.t[:], 0.0, [self.kst.r], eng="dve")
        self.load_params()
        self.ada()
        self.phase("start")
        if self.do_ctx:
            for cb in range(2):
                self.run_ctx_block(cb)
        self.phase("sample_load")
        if self.do_sample:
            self.run_sample()
        self.phase("end")
        self.S.emit()
        return self.nc


_WKEYS = ["w_ada", "b_ada", "w_in", "hg_lb", "hg_norm", "mla_q_norm", "mla_w_uq", "mla_kv_norm", "mla_w_ukv",
          "gqa_q_norm", "gqa_k_norm", "w_out", "ln1_g", "ln1_b", "w_ffn_in", "w_ffn_out", "ln2_g", "ln2_b"]


def make_in_maps(inp):
    f = lambda a: np.ascontiguousarray(np.asarray(a), dtype=np.float32)
    shared = {k: f(inp[k]) for k in _WKEYS}
    maps = []
    for c in range(8):
        b = c // 4
        m = dict(shared)
        m["xc"] = f(inp["x_prompt"][4 * c:4 * c + 4]).reshape(1024, D)
        m["xs"] = f(inp["x_sample"][b])
        m["c_ckv"] = f(inp["cache_mla_ckv"][b])
        m["c_kpe"] = f(inp["cache_mla_kpe"][b])
        m["c_gk"] = f(inp["cache_gqa_k"][b]).reshape(2, 256, 128)
        m["c_gv"] = f(inp["cache_gqa_v"][b]).reshape(2, 256, 128)
        m["s_hg"] = f(inp["state_hgrn"][b]).reshape(2, 2, 256, 64)
        m["cvec"] = np.stack([f(inp["c_ctx"]), f(inp["c"][b])])
        m["gidx"] = ((c % 4) * 128 + np.arange(128, dtype=np.int32)).reshape(128, 1)
        maps.append(m)
    return maps


def assemble(results):
    y_prompt = np.concatenate([r["y_c"].reshape(4, 256, D) for r in results], axis=0)
    y_sample = np.stack([np.concatenate([results[4 * b + j]["y_own"] for j in range(4)], axis=0) for b in range(2)], axis=0)
    st_ckv = np.concatenate([r["o_ckv"] for r in results], axis=0)
    st_kpe = np.concatenate([r["o_kpe"] for r in results], axis=0)
    st_gk = np.concatenate([r["o_gk"].reshape(4, 2, 256, 2, 64) for r in results], axis=0)
    st_gv = np.concatenate([r["o_gv"].reshape(4, 2, 256, 2, 64) for r in results], axis=0)
    st_hg = np.concatenate([r["o_hg"].reshape(4, 2, 2, 4, 64, 64) for r in results], axis=0)
    return tuple(np.ascontiguousarray(a, dtype=np.float32) for a in (y_prompt, y_sample, st_ckv, st_kpe, st_gk, st_gv, st_hg))


def kernel(**inputs):
    nc = Builder().build()
    res = run_bass_kernel_spmd(nc, make_in_maps(inputs), core_ids=list(range(8)))
    return assemble(res.results)
```

```python
import math
from contextlib import ExitStack
import numpy as np
import concourse.bass as bass
import concourse.mybir as mybir
from concourse.bass_utils import run_bass_kernel_spmd

F32 = mybir.dt.float32
BF16 = mybir.dt.bfloat16
I32 = mybir.dt.int32
AF = mybir.ActivationFunctionType
ALU = mybir.AluOpType
AX = mybir.AxisListType

ENGS = ("pe", "act", "dve", "pool", "sp")

D = 1024
KC = 8
TB = 512
DFF = 2816
NFC = 22
EPS = 1e-6
ALPHA = 4.0 ** 0.25
MLA_SCALE = 96.0 ** -0.5
GQA_SCALE = 64.0 ** -0.5
THETA = 10000.0
O_HQ, O_HFF, O_HFB, O_HI, O_HG, O_MQ, O_MKV, O_MKR, O_GQ, O_GK, O_GV = 0, 256, 512, 768, 1024, 1280, 1536, 1664, 1696, 2080, 2208


class Res:
    __slots__ = ("name", "w", "readers", "psum")

    def __init__(self, name, psum=False):
        self.name = name
        self.w = None
        self.readers = []
        self.psum = psum


class Op:
    __slots__ = ("eng", "fn", "deps", "is_dma", "key", "dma_target", "mile", "needs_inc")

    def __init__(self, eng, fn, is_dma, key):
        self.eng = eng
        self.fn = fn
        self.is_dma = is_dma
        self.key = key
        self.deps = ()
        self.dma_target = 0
        self.mile = 0
        self.needs_inc = False


class Sched:
    def __init__(self, nc):
        self.nc = nc
        self.ops = {e: [] for e in ENGS}
        self.dma_count = {}

    def res(self, name="r"):
        return Res(name)

    def add(self, eng, fn, reads=(), writes=(), dma=False, key=None):
        op = Op(eng, fn, dma, key)
        raw = set()
        deps = set()
        for r in reads:
            if r.w is not None:
                raw.add(r.w)
                deps.add(r.w)
            if r.psum:
                for rd in r.readers:
                    if rd.eng != eng:
                        deps.add(rd)
        for w in writes:
            if w.w is not None:
                deps.add(w.w)
            for rd in w.readers:
                deps.add(rd)
        fdeps = []
        for d in deps:
            if (not d.is_dma) and (not dma) and d.eng == eng:
                if eng == "pe":
                    continue
            fdeps.append(d)
        op.deps = fdeps
        for r in reads:
            r.readers.append(op)
        for w in writes:
            w.w = op
            w.readers = []
        if dma:
            c = self.dma_count.get(key, 0) + 1
            self.dma_count[key] = c
            op.dma_target = 16 * c
        self.ops[eng].append(op)
        return op

    def emit(self, final_wait_eng="sp"):
        nc = self.nc
        for e in ENGS:
            for op in self.ops[e]:
                for d in op.deps:
                    if not d.is_dma:
                        d.needs_inc = True
        for e in ENGS:
            c = 0
            for op in self.ops[e]:
                if (not op.is_dma) and op.needs_inc:
                    c += 1
                    op.mile = c
        with ExitStack() as es:
            esem = {e: es.enter_context(nc.semaphore("s_" + e)) for e in ENGS}
            dsem = {k: es.enter_context(nc.semaphore("d_%d" % i)) for i, k in enumerate(self.dma_count)}
            block = es.enter_context(nc.Block())
            sched = self

            def run_engine(e, eng):
                waited_e = {x: 0 for x in ENGS}
                waited_d = {}
                for op in sched.ops[e]:
                    need_e = {}
                    need_d = {}
                    for d in op.deps:
                        if d.is_dma:
                            if need_d.get(d.key, 0) < d.dma_target:
                                need_d[d.key] = d.dma_target
                        else:
                            if need_e.get(d.eng, 0) < d.mile:
                                need_e[d.eng] = d.mile
                    for x, v in need_e.items():
                        if v > waited_e[x]:
                            eng.wait_ge(esem[x], v)
                            waited_e[x] = v
                    for k, v in need_d.items():
                        if v > waited_d.get(k, 0):
                            eng.wait_ge(dsem[k], v)
                            waited_d[k] = v
                    ins = op.fn(eng)
                    if op.is_dma:
                        ins.then_inc(dsem[op.key], 16)
                    elif op.needs_inc:
                        ins.then_inc(esem[e], 1)
                if e == final_wait_eng:
                    for k, c in sched.dma_count.items():
                        if 16 * c > waited_d.get(k, 0):
                            eng.wait_ge(dsem[k], 16 * c)

            @block.tensor
            def _(eng):
                run_engine("pe", eng)

            @block.scalar
            def _(eng):
                run_engine("act", eng)

            @block.vector
            def _(eng):
                run_engine("dve", eng)

            @block.gpsimd
            def _(eng):
                run_engine("pool", eng)

            @block.sync
            def _(eng):
                run_engine("sp", eng)


class B:
    def __init__(self, t, r):
        self.t = t
        self.r = r


class Builder:
    def __init__(self, do_ctx=True, do_sample=True, nlayers=2, dbg=None):
        self.do_ctx = do_ctx
        self.do_sample = do_sample
        self.nlayers = nlayers
        self.dbg = dbg or {}
        nc = bass.Bass("TRN2", target_bir_lowering=False)
        self.nc = nc
        self.S = Sched(nc)
        self.sb_bytes = 0
        self._ring_i = 0
        self._uid = 0
        self.declare_dram()
        self.alloc()

    def din(self, name, shape, dt=F32):
        return self.nc.dram_tensor(name, list(shape), dt, kind="ExternalInput").ap()

    def dout(self, name, shape, dt=F32):
        return self.nc.dram_tensor(name, list(shape), dt, kind="ExternalOutput").ap()

    def sb(self, name, shape, dt, nres=1):
        t = self.nc.alloc_sbuf_tensor(name, list(shape), dt)
        n = 1
        for s in shape[1:]:
            n *= s
        self.sb_bytes += n * (2 if dt == BF16 else 4)
        if nres == 1:
            return B(t, self.S.res(name))
        return B(t, [self.S.res("%s_%d" % (name, i)) for i in range(nres)])

    def ps(self, name, dt=F32):
        t = self.nc.alloc_psum_tensor(name, [128, 512 if dt == F32 else 1024], dt)
        return B(t, Res(name, psum=True))

    @staticmethod
    def _flat(lst):
        out = []
        for x in lst:
            if x is None:
                continue
            if isinstance(x, (list, tuple)):
                out.extend(Builder._flat(x))
            else:
                out.append(x)
        return out

    def op(self, eng, fn, reads=(), writes=()):
        return self.S.add(eng, fn, self._flat(reads), self._flat(writes))

    def dma(self, q, out, in_, reads=(), writes=(), key=None, nc_ok=False):
        if nc_ok:
            f = lambda e: e.dma_start(out=out, in_=in_, allow_slow_non_contiguous=True)
        else:
            f = lambda e: e.dma_start(out=out, in_=in_)
        return self.S.add(q, f, self._flat(reads), self._flat(writes), dma=True, key=key)

    def mm(self, out, lhsT, rhs, start, stop, reads, writes, tile_position=None):
        if tile_position is None:
            f = lambda e: e.matmul(out, lhsT=lhsT, rhs=rhs, start=start, stop=stop)
        else:
            f = lambda e: e.matmul(out, lhsT=lhsT, rhs=rhs, start=start, stop=stop, tile_position=tile_position)
        return self.op("pe", f, reads, writes)

    def tp(self, out, in_, ident, reads, writes):
        return self.op("pe", lambda e: e.transpose(out, in_, ident), reads, writes)

    def act(self, out, in_, func, reads, writes, scale=1.0, bias=0.0, accum_out=None):
        if accum_out is None:
            f = lambda e: e.activation(out=out, in_=in_, func=func, bias=bias, scale=scale)
        else:
            f = lambda e: e.activation(out=out, in_=in_, func=func, bias=bias, scale=scale, accum_out=accum_out)
        return self.op("act", f, reads, writes)

    def tt(self, out, in0, in1, op, reads, writes, eng="dve"):
        return self.op(eng, lambda e: e.tensor_tensor(out=out, in0=in0, in1=in1, op=op), reads, writes)

    def ts(self, out, in0, s1, s2, op0, op1, reads, writes, eng="dve"):
        if s2 is None:
            f = lambda e: e.tensor_scalar(out=out, in0=in0, scalar1=s1, scalar2=None, op0=op0)
        else:
            f = lambda e: e.tensor_scalar(out=out, in0=in0, scalar1=s1, scalar2=s2, op0=op0, op1=op1)
        return self.op(eng, f, reads, writes)

    def stt(self, out, in0, scalar, in1, op0, op1, reads, writes, eng="dve"):
        return self.op(eng, lambda e: e.scalar_tensor_tensor(out=out, in0=in0, scalar=scalar, in1=in1, op0=op0, op1=op1),
                       reads, writes)

    def cp(self, out, in_, reads, writes, eng="dve"):
        if eng == "act":
            return self.act(out, in_, AF.Copy, reads, writes)
        return self.op(eng, lambda e: e.tensor_copy(out=out, in_=in_), reads, writes)

    def recip(self, out, in_, reads, writes, exact=False):
        if exact:
            return self.op("dve", lambda e: e.reciprocal(out=out, in_=in_), reads, writes)
        self.act(out, in_, AF.Ln, reads, writes)
        return self.act(out, out, AF.Exp, writes, writes, scale=-1.0)

    def memset(self, ap, val, writes, eng="pool", reads=()):
        return self.op(eng, lambda e: e.memset(ap, val), reads, writes)

    def declare_dram(self):
        self.xc = self.din("xc", [1024, D])
        self.xsd = self.din("xs", [2048, D])
        self.c_ckv = self.din("c_ckv", [2, 256, 128])
        self.c_kpe = self.din("c_kpe", [2, 256, 32])
        self.c_gk = self.din("c_gk", [2, 256, 128])
        self.c_gv = self.din("c_gv", [2, 256, 128])
        self.s_hg = self.din("s_hg", [2, 2, 256, 64])
        self.cvec = self.din("cvec", [2, D])
        self.w_ada = self.din("w_ada", [2, D, 6 * D])
        self.b_ada = self.din("b_ada", [2, 6 * D])
        self.w_in = self.din("w_in", [2, D, 2336])
        self.hg_lb = self.din("hg_lb", [2, 2, 256])
        self.hg_norm = self.din("hg_norm", [2, 64])
        self.mla_q_norm = self.din("mla_q_norm", [2, 256])
        self.w_uq = self.din("mla_w_uq", [2, 256, 576])
        self.mla_kv_norm = self.din("mla_kv_norm", [2, 128])
        self.w_ukv = self.din("mla_w_ukv", [2, 128, 768])
        self.gqa_q_norm = self.din("gqa_q_norm", [2, 64])
        self.gqa_k_norm = self.din("gqa_k_norm", [2, 64])
        self.w_out = self.din("w_out", [2, D, D])
        self.ln1_g = self.din("ln1_g", [2, D])
        self.ln1_b = self.din("ln1_b", [2, D])
        self.w_ffn_in = self.din("w_ffn_in", [2, D, 2 * DFF])
        self.w_ffn_out = self.din("w_ffn_out", [2, DFF, D])
        self.ln2_g = self.din("ln2_g", [2, D])
        self.ln2_b = self.din("ln2_b", [2, D])
        self.y_c = self.dout("y_c", [1024, D])
        self.y_own = self.dout("y_own", [TB, D])
        self.gidx_d = self.din("gidx", [128, 1], I32)
        self.o_ckv = self.dout("o_ckv", [4, 2, 256, 128])
        self.o_kpe = self.dout("o_kpe", [4, 2, 256, 32])
        self.o_gk = self.dout("o_gk", [4, 2, 256, 128])
        self.o_gv = self.dout("o_gv", [4, 2, 256, 128])
        self.o_hg = self.dout("o_hg", [4, 2, 2, 256, 64])
        self.dbg_out = {}
        for name, shape in self.dbg.items():
            self.dbg_out[name] = self.dout("dbg_" + name, shape)

    def alloc(self):
        sb = self.sb
        self.xs_dram = self.nc.dram_tensor("xs_scr", [4 * 128, KC * TB], F32)
        self.mh2 = self.nc.dram_tensor("mh_scr", [4 * 128, 2 * TB], BF16)
        self.rp2 = self.nc.dram_tensor("rp_scr", [4 * 128, 4 * 96], F32)
        self.wsc_in = [self.nc.dram_tensor("wsc_in%d" % l, [11 * 128, 4096], BF16) for l in range(2)]
        self.wsc_out = [self.nc.dram_tensor("wsc_out%d" % l, [8 * 128, 2816], BF16) for l in range(2)]
        self.wscr = [self.S.res("wsc%d" % l) for l in range(2)]
        self.ffn_cached = [False, False]
        self.mh2r = self.S.res("mh2")
        self.rp2r = self.S.res("rp2")
        self.gidx = sb("gidx_sb", [128, 1], I32)
        self.xsr = [self.S.res("xsd%d" % i) for i in range(4)]
        self.xb = sb("xb", [128, KC, TB], F32, nres=KC)
        self.hT = sb("hT", [128, KC, TB], BF16, nres=KC)
        self.KmlaT = sb("KmlaT", [128, 6, 2304], BF16)
        self.KgqaT = sb("KgqaT", [128, 2304], BF16)
        self.Vmla = sb("Vmla", [128, 18, 6, 65], BF16)
        self.Vgqa = sb("Vgqa", [128, 18, 2, 65], BF16)
        self.mixHG = sb("mixHG", [128, 2, 2048], BF16, nres=4)
        self.NW = 3
        self.ring = [sb("ring%d" % i, [128, 4096], BF16) for i in range(self.NW)]
        self.wsm = sb("wsm", [128, 1920], BF16)
        self.ident = sb("ident", [128, 128], BF16)
        self.identf = sb("identf", [128, 128], F32)
        self.onesf = sb("onesf", [128, 128], F32)
        self.onesblk = sb("onesblk", [128, 128], F32)
        self.segmask = sb("segmask", [128, TB], F32)
        self.mAf = sb("mAf", [128, 128], BF16)
        self.mAb = sb("mAb", [128, 128], BF16)
        self.pcs = sb("pcs", [128, 4], F32)
        self.rope = sb("rope", [128, 16, 2, 2, 24], F32)
        self.par = sb("par", [128, 224], F32)
        self.mT = sb("mT", [128, 2, 48, 2], F32)
        self.mP = sb("mP", [128, 2, 2, 2, 8], F32)
        self.lbp = sb("lbp", [128, 2, 2, 2, 3], F32)
        self.bc = sb("bc", [128, 512], F32)
        self.hgn = sb("hgn", [128, 2], F32)
        self.V32 = sb("V32", [128, 2, 2, 64], F32, nres=4)
        self.iexp = sb("iexp", [128, 4, 128], BF16)
        self.cmask4 = sb("cmask4", [128, 4], BF16)
        self.NV = 4
        self.Vr32 = sb("Vr32", [128, self.NV, 64], F32, nres=self.NV)
        self.Vrb = sb("Vrb", [128, self.NV, 64], BF16, nres=self.NV)
        self.stg = [sb("stg0", [128, 1024], F32)]
        self.ht = [sb("ht%d" % i, [128, TB], F32) for i in range(5)]
        self.hq32 = sb("hq32", [128, 2, TB], F32)
        self.qh = sb("qh", [128, TB], BF16)
        self.kh = sb("kh", [128, TB], BF16)
        self.qt = sb("qt", [128, TB], BF16)
        self.kd = sb("kd", [128, TB], BF16)
        self.kdtm = sb("kdtm", [128, 4, 128], BF16)
        self.itm = sb("itm", [128, 4, 256], BF16)
        self.ATm = sb("ATm", [128, 2, 128], BF16)
        self.hsm = sb("hsm", [128, 3, 16], F32)
        self.zkf = sb("zkf", [128, 416], F32)
        self.gkf = sb("gkf", [128, 128], F32)
        self.kst = sb("kst", [128, 3, 128], BF16)
        self.ckvT = sb("ckvT", [128, TB], BF16)
        self.sq = sb("sq", [128, 512], F32)
        self.st8 = sb("st8", [128, 8], F32)
        self.QmlaT = sb("QmlaT", [128, 6, TB], BF16)
        self.QgqaT = sb("QgqaT", [128, 3, TB], BF16)
        self.mixA = sb("mixA", [64, 12, TB], BF16, nres=12)
        self.PTb = [self.qh, self.kh, self.qt]
        self.bcs = B(self.hq32.t[0:64, 0, :], self.hq32.r)
        self.rc = self.sq
        self.zq = sb("zq", [128, 640], F32)
        self.qstage = sb("qstage", [128, 6, 96], BF16)
        self.gqb = sb("gqb", [128, 384], BF16)
        self.mqnT = sb("mqnT", [128, 2, 128], BF16)
        self.qpe = sb("qpe", [128, 6, 32], F32)
        self.actT = sb("actT", [128, NFC, TB], BF16)
        self.lnm, self.lnr, self.lnt = self.ht[0], self.ht[1], self.ht[2]
        self.gy = [self.ht[3], self.ht[4]]
        self.P = [self.ps("pb%d" % i) for i in range(6)]
        self.PT = [self.ps("pt%d" % i, BF16) for i in range(2)]
        print("[kernel] SBUF bytes/partition allocated:", self.sb_bytes, "remaining", self.nc.sbuf_bytes_remaining)

    def phase(self, name):
        if not hasattr(self, "phases"):
            self.phases = []
        self.phases.append((name, len(self.S.ops["pe"])))

    def next_ring(self):
        b = self.ring[self._ring_i % self.NW]
        self._ring_i += 1
        return b

    def wload(self, pieces):
        slot = self.next_ring()
        for (dst, src) in pieces:
            self.dma("pool", dst(slot.t), src, writes=[slot.r], key=slot.r.name)
        return slot

    def init_consts(self):
        S = self.S
        pi = math.pi
        for t in (self.ident, self.identf):
            self.memset(t.t[:], 1.0, [t.r])
            self.op("pool", lambda e, t=t: e.affine_select(out=t.t[:], in_=t.t[:], compare_op=ALU.is_equal, fill=0.0,
                                                           base=0, pattern=[[-1, 128]], channel_multiplier=1),
                    [t.r], [t.r])
        self.memset(self.onesf.t[:], 1.0, [self.onesf.r])
        self.memset(self.onesblk.t[:], 0.0, [self.onesblk.r])
        self.memset(self.onesblk.t[0:64, 0:64], 1.0, [self.onesblk.r], reads=[self.onesblk.r])
        self.memset(self.onesblk.t[64:128, 64:128], 1.0, [self.onesblk.r], reads=[self.onesblk.r])
        self.memset(self.segmask.t[:], 1.0, [self.segmask.r])
        self.memset(self.segmask.t[:].rearrange("p (c t) -> p c t", t=32)[:, :, 0:1], 0.0, [self.segmask.r],
                    reads=[self.segmask.r])
        m = self.mAf
        self.memset(m.t[:], 1.0, [m.r])
        self.op("pool", lambda e: e.affine_select(out=m.t[:], in_=m.t[:], compare_op=ALU.is_ge, fill=0.0,
                                                  base=0, pattern=[[1, 128]], channel_multiplier=-1), [m.r], [m.r])
        self.op("pool", lambda e: e.affine_select(out=m.t[:], in_=m.t[:], compare_op=ALU.is_ge, fill=0.0,
                                                  base=0, pattern=[[-32, 4], [0, 32]], channel_multiplier=1), [m.r], [m.r])
        m2 = self.mAb
        self.memset(m2.t[:], 1.0, [m2.r])
        self.op("pool", lambda e: e.affine_select(out=m2.t[:], in_=m2.t[:], compare_op=ALU.is_ge, fill=0.0,
                                                  base=0, pattern=[[-1, 128]], channel_multiplier=1), [m2.r], [m2.r])
        self.op("pool", lambda e: e.affine_select(out=m2.t[:], in_=m2.t[:], compare_op=ALU.is_ge, fill=0.0,
                                                  base=31, pattern=[[32, 4], [0, 32]], channel_multiplier=-1), [m2.r], [m2.r])
        cm = self.cmask4
        self.memset(cm.t[:], 1.0, [cm.r])
        self.op("pool", lambda e: e.affine_select(out=cm.t[:], in_=cm.t[:], compare_op=ALU.is_ge, fill=0.0,
                                                  base=0, pattern=[[-32, 4]], channel_multiplier=1), [cm.r], [cm.r])
        self.op("pool", lambda e: e.affine_select(out=cm.t[:], in_=cm.t[:], compare_op=ALU.is_ge, fill=0.0,
                                                  base=31, pattern=[[32, 4]], channel_multiplier=-1), [cm.r], [cm.r])
        self.memset(self.Vmla.t[:, :, :, 64:65], 1.0, [self.Vmla.r])
        self.memset(self.Vgqa.t[:, :, :, 64:65], 1.0, [self.Vgqa.r])
        pc = self.pcs
        self.op("pool", lambda e: e.iota(pc.t[:, 0:1], pattern=[[0, 1]], base=0, channel_multiplier=1,
                                         allow_small_or_imprecise_dtypes=True), [], [pc.r])
        self.ts(pc.t[:, 2:3], pc.t[:, 0:1], 64.0, None, ALU.is_ge, None, [pc.r], [pc.r])
        self.stt(pc.t[:, 1:2], pc.t[:, 2:3], -64.0, pc.t[:, 0:1], ALU.mult, ALU.add, [pc.r], [pc.r])
        self.dma("sp", self.gidx.t[:], self.gidx_d, writes=[self.gidx.r], key="gidx")
        if self.do_sample:
            self.init_rope()
            for blk in range(4):
                self.dma("sp", self.rp2[blk * 128:(blk + 1) * 128, :],
                         self.rope.t[:, 4 * blk:4 * blk + 4].rearrange("p a b c d -> p (a b c d)"),
                         reads=[self.rope.r], writes=[self.rp2r], key="rope")

    def init_rope(self):
        pi = math.pi
        pc = self.pcs
        ang = self.ht[0]
        a = self.ht[0].t[:, 0:384].rearrange("p (i f) -> p i f", f=24)
        c = self.ht[1].t[:, 0:24]
        fr = self.ht[2].t[:, 0:24]
        rv = self.ht[2].t[:, 32:48]
        r0, r1, r2 = self.ht[0].r, self.ht[1].r, self.ht[2].r
        for j in range(8):
            self.memset(fr[:, j:j + 1], THETA ** (-j / 8.0), [r2], eng="dve", reads=[r2])
        for j in range(16):
            self.memset(fr[:, 8 + j:9 + j], THETA ** (-j / 16.0), [r2], eng="dve", reads=[r2])
        for i in range(16):
            self.memset(rv[:, i:i + 1], 2.0 * i, [r2], eng="dve", reads=[r2])
        self.ts(rv, rv, pc.t[:, 2:3], None, ALU.add, None, [r2, pc.r], [r2])
        self.tt(a, rv[:, :, None].broadcast_to([128, 16, 24]), fr[:, None, :].broadcast_to([128, 16, 24]), ALU.mult, [r2], [r0])
        self.ts(c, fr, pc.t[:, 1:2], None, ALU.mult, None, [r2, pc.r], [r1])
        rp = self.rope
        r3, r4 = self.ht[3].r, self.ht[4].r

        def reduce_sin(dst, src_ap, src_res, shape_n, shift, view):
            y = view(self.ht[3].t[:, 0:shape_n])
            yi = view(self.ht[4].t[:, 0:shape_n].bitcast(I32))
            yf = view(self.ht[4].t[:, 384:384 + shape_n]) if shape_n <= 128 else None
            self.ts(y, src_ap, 1.0 / (2 * pi), shift, ALU.mult, ALU.add, [src_res], [r3])
            self.cp(yi, y, [r3], [r4])
            y2 = view(self.hq32.t[:, 0, 0:shape_n])
            self.cp(y2, yi, [r4], [self.hq32.r])
            self.tt(y, y, y2, ALU.subtract, [r3, self.hq32.r], [r3])
            m = view(self.hq32.t[:, 1, 0:shape_n])
            self.ts(m, y, 0.5, None, ALU.is_gt, None, [r3], [self.hq32.r])
            self.tt(y, y, m, ALU.subtract, [r3, self.hq32.r], [r3])
            self.ts(m, y, -0.5, None, ALU.is_lt, None, [r3], [self.hq32.r])
            self.tt(y, y, m, ALU.add, [r3, self.hq32.r], [r3])
            self.act(dst, y, AF.Sin, [r3], [rp.r], scale=2 * pi)

        v3 = lambda ap: ap.rearrange("p (i f) -> p i f", f=24)
        v2 = lambda ap: ap
        ctmp = self.sq.t[:, 0:48]
        for which, shift in ((0, 0.25), (1, 0.0)):
            reduce_sin(rp.t[:, :, which, 0, :], a, r0, 384, shift, v3)
            y = self.ht[3].t[:, 0:24]
            cdst = ctmp[:, which * 24:(which + 1) * 24]
            old_rp = rp
            def _col(dst=cdst, shift=shift):
                yv = self.ht[3].t[:, 0:24]
                yi = self.ht[4].t[:, 0:24].bitcast(I32)
                y2 = self.hq32.t[:, 0, 0:24]
                m = self.hq32.t[:, 1, 0:24]
                self.ts(yv, c, 1.0 / (2 * pi), shift, ALU.mult, ALU.add, [r1], [r3])
                self.cp(yi, yv, [r3], [r4])
                self.cp(y2, yi, [r4], [self.hq32.r])
                self.tt(yv, yv, y2, ALU.subtract, [r3, self.hq32.r], [r3])
                self.ts(m, yv, 0.5, None, ALU.is_gt, None, [r3], [self.hq32.r])
                self.tt(yv, yv, m, ALU.subtract, [r3, self.hq32.r], [r3])
                self.ts(m, yv, -0.5, None, ALU.is_lt, None, [r3], [self.hq32.r])
                self.tt(yv, yv, m, ALU.add, [r3, self.hq32.r], [r3])
                self.act(dst, yv, AF.Sin, [r3], [self.sq.r], scale=2 * pi)
            _col()
            self.cp(rp.t[:, :, which, 1, :], cdst[:, None, :].broadcast_to([128, 16, 24]), [self.sq.r], [rp.r])

    def load_params(self):
        st = self.stg[0]
        rows = []
        r = 0
        self.par_off = {}

        def put(name, ap2d, n):
            nonlocal r
            self.dma("sp", st.t[r:r + n, 0:128], ap2d, writes=[st.r], key="stg")
            self.par_off[name] = r
            r += n
        put("cvec", self.cvec.rearrange("g (k p) -> (g k) p", p=128), 16)
        put("ln1_g", self.ln1_g.rearrange("l (k p) -> (l k) p", p=128), 16)
        put("ln1_b", self.ln1_b.rearrange("l (k p) -> (l k) p", p=128), 16)
        put("ln2_g", self.ln2_g.rearrange("l (k p) -> (l k) p", p=128), 16)
        put("ln2_b", self.ln2_b.rearrange("l (k p) -> (l k) p", p=128), 16)
        put("lb", self.hg_lb.rearrange("l d (c p) -> (l d c) p", p=128), 8)
        n1 = r
        pt = self.P[0]
        self.tp(pt.t[:, 0:n1], st.t[0:n1, 0:128], self.identf.t[0:n1, 0:n1], [st.r, self.identf.r], [pt.r])
        self.cp(self.par.t[:, 0:n1], pt.t[:, 0:n1], [pt.r], [self.par.r])
        st2r = self.sq
        self.dma("sp", st2r.t[0:96, 0:128], self.b_ada.rearrange("l (j p) -> (l j) p", p=128), writes=[st2r.r], key="sq")
        pt2 = self.P[1]
        self.tp(pt2.t[:, 0:96], st2r.t[0:96, 0:128], self.identf.t[0:96, 0:96], [st2r.r, self.identf.r], [pt2.r])
        self.par_off["b_ada"] = 128
        self.cp(self.par.t[:, 128:224], pt2.t[:, 0:96], [pt2.r], [self.par.r])
        for l in range(2):
            for h in range(2):
                self.dma("sp", self.hgn.t[64 * h:64 * h + 64, l:l + 1], self.hg_norm[l:l + 1, :].rearrange("o d -> d o"),
                         writes=[self.hgn.r], key="hgn", nc_ok=True)
        o = self.par_off["lb"]
        lb0 = self.par.t[:, o:o + 4]
        lb1 = self.par.t[:, o + 4:o + 8]
        sm = self.small = getattr(self, "small", None) or self.sb("small", [128, 32], F32)
        s = sm.t
        self.tt(s[:, 0:4], lb0, lb1, ALU.max, [self.par.r], [sm.r])
        self.tt(s[:, 4:8], lb0, s[:, 0:4], ALU.subtract, [self.par.r, sm.r], [sm.r])
        self.tt(s[:, 8:12], lb1, s[:, 0:4], ALU.subtract, [self.par.r, sm.r], [sm.r])
        self.act(s[:, 4:12], s[:, 4:12], AF.Exp, [sm.r], [sm.r])
        self.tt(s[:, 12:16], s[:, 4:8], s[:, 8:12], ALU.add, [sm.r], [sm.r])
        self.recip(s[:, 12:16], s[:, 12:16], [sm.r], [sm.r])
        self.tt(s[:, 16:20], s[:, 8:12], s[:, 12:16], ALU.mult, [sm.r], [sm.r])
        lbp = self.lbp
        self.memset(lbp.t[:, 0, :, :, 0:1], 0.0, [lbp.r], eng="dve")
        self.cp(lbp.t[:, 1, :, :, 0:1], s[:, 16:20].rearrange("p (d c o) -> p d c o", d=2, c=2), [sm.r], [lbp.r])
        self.ts(lbp.t[:, :, :, :, 1:2], lbp.t[:, :, :, :, 0:1], -1.0, 1.0, ALU.mult, ALU.add, [lbp.r], [lbp.r])
        self.ts(lbp.t[:, :, :, :, 2:3], lbp.t[:, :, :, :, 1:2], -1.0, None, ALU.mult, None, [lbp.r], [lbp.r])

    def pcol(self, name, l, k=None):
        o = self.par_off[name] + l * 8
        if k is None:
            return self.par.t[:, o:o + 8]
        return self.par.t[:, o + k:o + k + 1]

    def ada(self):
        o = self.par_off["cvec"]
        cv = self.par.t[:, o:o + 16]
        sm = self.small
        s = sm.t
        self.act(s[:, 0:16], cv, AF.Exp, [self.par.r], [sm.r], scale=-1.0)
        self.ts(s[:, 0:16], s[:, 0:16], 1.0, None, ALU.add, None, [sm.r], [sm.r])
        self.recip(s[:, 0:16], s[:, 0:16], [sm.r], [sm.r])
        self.tt(s[:, 0:16], s[:, 0:16], cv, ALU.mult, [sm.r, self.par.r], [sm.r])
        scT = self.kst
        scv = scT.t[:, 0, 0:16].rearrange("p (k g) -> p k g", g=2)
        self.cp(scv, s[:, 0:16].rearrange("p (g k) -> p k g", g=2), [sm.r], [scT.r])
        bo = self.par_off["b_ada"]
        for l in range(2):
            for grp in range(12):
                slot = self.wload([(lambda t: t[:, 0:4096].rearrange("p (k n) -> p k n", k=8),
                                    self.w_ada[l][:, grp * 512:(grp + 1) * 512].rearrange("(k p) n -> p k n", p=128))])
                wv = slot.t[:, 0:4096].rearrange("p (k n) -> p k n", k=8)
                pb = self.P[grp % 2]
                for jj in range(4):
                    for k in range(8):
                        self.mm(pb.t[:, jj * 2:jj * 2 + 2], wv[:, k, jj * 128:(jj + 1) * 128], scv[:, k, :],
                                k == 0, k == 7, [slot.r, scT.r], [pb.r])
                j0 = grp * 4
                self.tt(self.mT.t[:, l, j0:j0 + 4, :], pb.t[:, 0:8].rearrange("p (j g) -> p j g", g=2),
                        self.par.t[:, bo + l * 48 + j0: bo + l * 48 + j0 + 4][:, :, None].broadcast_to([128, 4, 2]),
                        ALU.add, [pb.r, self.par.r], [self.mT.r])
        for l in range(2):
            for g in range(2):
                self.ts(self.mP.t[:, l, g, 0, :], self.mT.t[:, l, 8:16, g], 1.0, None, ALU.add, None, [self.mT.r], [self.mP.r])
                self.ts(self.mP.t[:, l, g, 1, :], self.mT.t[:, l, 32:40, g], 1.0, None, ALU.add, None, [self.mT.r], [self.mP.r])

    def mvec(self, l, g, which, k):
        if which == "sc1p":
            return self.mP.t[:, l, g, 0, k:k + 1]
        if which == "sc2p":
            return self.mP.t[:, l, g, 1, k:k + 1]
        base = {"sh1": 0, "g1": 16, "sh2": 24, "g2": 40}[which]
        return self.mT.t[:, l, base + k, g:g + 1]

    def load_layer_small(self, l):
        bc = self.bc
        self.dma("sp", bc.t[:, 0:128], self.mla_kv_norm[l:l + 1, :].partition_broadcast(128), writes=[bc.r], key="bc")
        self.dma("sp", bc.t[:, 128:384], self.mla_q_norm[l:l + 1, :].partition_broadcast(128), writes=[bc.r], key="bc")
        self.dma("sp", bc.t[:, 384:448], self.gqa_q_norm[l:l + 1, :].partition_broadcast(128), writes=[bc.r], key="bc")
        self.dma("sp", bc.t[:, 448:512], self.gqa_k_norm[l:l + 1, :].partition_broadcast(128), writes=[bc.r], key="bc")
        w = self.wsm
        self.dma("pool", w.t[:, 0:1152].rearrange("p (k n) -> p k n", k=2), self.w_uq[l].rearrange("(k p) n -> p k n", p=128),
                 writes=[w.r], key="wsm")
        self.dma("pool", w.t[:, 1152:1920], self.w_ukv[l], writes=[w.r], key="wsm")

    def dump(self, name, ap, res):
        if name not in self.dbg_out:
            return
        rs = res if isinstance(res, (list, tuple)) else [res]
        self.dma("sp", self.dbg_out[name], ap, reads=rs, key="dbg_" + name)

    def load_x_block(self, dram_rows_ap):
        st = self.stg[0]
        for t in range(4):
            self.dma("sp", st.t[:], dram_rows_ap[t * 128:(t + 1) * 128, :], writes=[st.r], key="stg")
            for half in range(2):
                pb = self.P[(2 * t + half) % 6]
                for kk in range(4):
                    k = half * 4 + kk
                    self.tp(pb.t[:, kk * 128:(kk + 1) * 128], st.t[:, k * 128:(k + 1) * 128], self.identf.t[:],
                            [st.r, self.identf.r], [pb.r])
                self.cp(self.xb.t[:, half * 4:half * 4 + 4, t * 128:(t + 1) * 128],
                        pb.t[:].rearrange("p (k n) -> p k n", k=4), [pb.r], [self.xb.r],
                        eng=("act" if half else "dve"))

    def store_y_block(self, dram_rows_ap):
        st = self.stg[0]
        for t in range(4):
            for half in range(2):
                pb = self.P[(2 * t + half) % 6]
                for kk in range(4):
                    k = half * 4 + kk
                    self.tp(pb.t[:, kk * 128:(kk + 1) * 128], self.xb.t[:, k, t * 128:(t + 1) * 128], self.identf.t[:],
                            [self.xb.r, self.identf.r], [pb.r])
                self.cp(st.t[:, half * 512:(half + 1) * 512], pb.t[:], [pb.r], [st.r], eng=("act" if half else "dve"))
            self.dma("sp", dram_rows_ap[t * 128:(t + 1) * 128, :], st.t[:], reads=[st.r], key="stg")

    def xs_dram_view(self, blk):
        return self.xs_dram[blk * 128:(blk + 1) * 128, :].rearrange("p (k n) -> p k n", k=KC)

    def gather(self, out_ap, dram_t, reads, writes, key):
        idx = self.gidx
        f = lambda e: e.indirect_dma_start(out=out_ap, out_offset=None, in_=dram_t,
                                           in_offset=bass.IndirectOffsetOnAxis(ap=idx.t[:, 0:1], axis=0))
        return self.S.add("pool", f, self._flat(list(reads) + [idx.r]), self._flat(writes), dma=True, key=key)

    def load_xs_block(self, blk):
        self.dma("sp", self.xb.t[:], self.xs_dram_view(blk), reads=[self.xsr[blk]], writes=[self.xb.r], key="xb")

    def store_xs_block(self, blk):
        self.dma("sp", self.xs_dram_view(blk), self.xb.t[:], reads=[self.xb.r], writes=[self.xsr[blk]], key="xb")

    def prepH(self, l, g, which="1"):
        scn, shn = ("sc1p", "sh1") if which == "1" else ("sc2p", "sh2")
        for k in range(KC):
            if k % 2 == 0:
                self.ts(self.hT.t[:, k, :], self.xb.t[:, k, :], self.mvec(l, g, scn, k), self.mvec(l, g, shn, k),
                        ALU.mult, ALU.add, [self.xb.r[k], self.mT.r, self.mP.r], [self.hT.r[k]])
            else:
                self.act(self.hT.t[:, k, :], self.xb.t[:, k, :], AF.Identity, [self.xb.r[k], self.mT.r, self.mP.r], [self.hT.r[k]],
                         scale=self.mvec(l, g, scn, k), bias=self.mvec(l, g, shn, k))

    def win_slot(self, l, pieces):
        tot = sum(n for _, n in pieces)
        assert tot * 8 <= 4096
        lst = []
        off = 0
        for (c0, n) in pieces:
            lst.append((lambda t, off=off, n=n, tot=tot: t[:, 0:8 * tot].rearrange("p (k n) -> p k n", k=8)[:, :, off:off + n],
                        self.w_in[l][:, c0:c0 + n].rearrange("(k p) n -> p k n", p=128)))
            off += n
        slot = self.wload(lst)
        return slot, slot.t[:, 0:8 * tot].rearrange("p (k n) -> p k n", k=8)

    def rope_tm(self, out, x, tile_idx, f0, nf, nheads, reads, writes):
        rp = self.rope
        shp = [128, nheads, 2, nf]
        cos = rp.t[:, tile_idx, 0, :, f0:f0 + nf][:, None, :, :].broadcast_to(shp)
        sin = rp.t[:, tile_idx, 1, :, f0:f0 + nf][:, None, :, :].broadcast_to(shp)
        x1 = x[:, :, :, 0, :]
        x2 = x[:, :, :, 1, :]
        t1 = self.sq.t[:, 0:nheads * 2 * nf].rearrange("p (h a f) -> p h a f", h=nheads, a=2)
        t2 = self.sq.t[:, 256:256 + nheads * 2 * nf].rearrange("p (h a f) -> p h a f", h=nheads, a=2)
        rs = list(reads) + [rp.r]
        self.tt(t1, x1, cos, ALU.mult, rs, [self.sq.r])
        self.tt(t2, x2, sin, ALU.mult, rs + [self.sq.r], [self.sq.r])
        self.tt(out[:, :, :, 0, :], t1, t2, ALU.subtract, [self.sq.r], writes)
        self.tt(t1, x1, sin, ALU.mult, rs + [self.sq.r], [self.sq.r])
        self.tt(t2, x2, cos, ALU.mult, rs + [self.sq.r], [self.sq.r])
        self.tt(out[:, :, :, 1, :], t1, t2, ALU.add, [self.sq.r], writes)

    def kv_finish(self, ntiles, kt0):
        w = self.wsm
        wukv = w.t[:, 1152:1920]
        n = ntiles * 128
        for h in range(6):
            pb = self.P[h % 2]
            self.mm(pb.t[0:64, 0:n], wukv[:, h * 128:h * 128 + 64], self.ckvT.t[:, 0:n], True, True, [w.r, self.ckvT.r], [pb.r])
            self.cp(self.KmlaT.t[0:64, h, kt0 * 128:kt0 * 128 + n], pb.t[0:64, 0:n], [pb.r], [self.KmlaT.r],
                    eng=("act" if h % 2 else "dve"))
        wv = wukv.rearrange("p (h x) -> p h x", h=6)[:, :, 64:128]
        for t in range(ntiles):
            pb = self.P[2 + t % 2]
            self.mm(pb.t[:, 0:384].rearrange("p (h x) -> p h x", h=6), self.ckvT.t[:, t * 128:(t + 1) * 128], wv, True, True,
                    [w.r, self.ckvT.r], [pb.r])
            self.cp(self.Vmla.t[:, kt0 + t, :, 0:64], pb.t[:, 0:384].rearrange("p (h x) -> p h x", h=6), [pb.r], [self.Vmla.r],
                    eng=("act" if t % 2 else "dve"))

    def kv_transposes(self, t, kt):
        k = self.kst
        pt = self.PT[t % 2]
        self.tp(pt.t[:, 0:128], k.t[:, 0, :], self.ident.t[:], [k.r, self.ident.r], [pt.r])
        self.tp(pt.t[:, 128:256], k.t[:, 1, :], self.ident.t[:], [k.r, self.ident.r], [pt.r])
        self.tp(pt.t[:, 256:384], k.t[:, 2, :], self.ident.t[:], [k.r, self.ident.r], [pt.r])
        self.cp(self.ckvT.t[:, t * 128:(t + 1) * 128], pt.t[:, 0:128], [pt.r], [self.ckvT.r], eng="act")
        self.cp(self.KmlaT.t[64:96, :, kt * 128:(kt + 1) * 128], pt.t[64:96, 128:256][:, None, :].broadcast_to([32, 6, 128]),
                [pt.r], [self.KmlaT.r], eng="dve")
        self.cp(self.KgqaT.t[:, kt * 128:(kt + 1) * 128], pt.t[:, 256:384], [pt.r], [self.KgqaT.r], eng="act")

    def kv_pass(self, l, ntiles, kt0, rope_tile0=None, ctx_out=None):
        slot, wv = self.win_slot(l, [(O_MKV, 160), (O_GK, 256)])
        bc = self.bc
        for t in range(ntiles):
            pb = self.P[4 + t % 2]
            for k in range(KC):
                self.mm(pb.t[:, 0:416], self.hT.t[:, k, t * 128:(t + 1) * 128], wv[:, k, :], k == 0, k == KC - 1,
                        [self.hT.r, slot.r], [pb.r])
            st = self.st8
            self.act(self.sq.t[:, 0:128], pb.t[:, 0:128], AF.Square, [pb.r], [self.sq.r, st.r], scale=128.0 ** -0.5,
                     accum_out=st.t[:, 0:1])
            for h in range(2):
                self.act(self.sq.t[:, 128:192], pb.t[:, 160 + 64 * h:224 + 64 * h], AF.Square, [pb.r], [self.sq.r, st.r],
                         scale=0.125, accum_out=st.t[:, 1 + h:2 + h])
            self.act(st.t[:, 0:3], st.t[:, 0:3], AF.Ln, [st.r], [st.r], bias=EPS)
            self.act(st.t[:, 0:3], st.t[:, 0:3], AF.Exp, [st.r], [st.r], scale=-0.5)
            k_ = self.kst
            zf = self.zkf
            cut = getattr(self, "cut", 99)
            if cut <= 1:
                continue
            if ctx_out is not None:
                self.stt(zf.t[:, 0:128], pb.t[:, 0:128], st.t[:, 0:1], bc.t[:, 0:128], ALU.mult, ALU.mult, [pb.r, st.r, bc.r], [zf.r])
                self.cp(zf.t[:, 128:160], pb.t[:, 128:160], [pb.r], [zf.r])
                for h in range(2):
                    self.stt(zf.t[:, 160 + 64 * h:224 + 64 * h], pb.t[:, 160 + 64 * h:224 + 64 * h], st.t[:, 1 + h:2 + h],
                             bc.t[:, 448:512], ALU.mult, ALU.mult, [pb.r, st.r, bc.r], [zf.r])
                self.cp(zf.t[:, 288:416], pb.t[:, 288:416], [pb.r], [zf.r], eng="act")
                self.cp(k_.t[:, 0, :], zf.t[:, 0:128], [zf.r], [k_.r], eng="act")
                self.cp(k_.t[:, 1, 64:96], zf.t[:, 128:160], [zf.r], [k_.r])
                self.cp(k_.t[:, 2, :], zf.t[:, 160:288], [zf.r], [k_.r], eng="act")
                seq = ctx_out + t // 2
                r0 = (t % 2) * 128
                self.dma("sp", self.o_ckv[seq, l, r0:r0 + 128, :], zf.t[:, 0:128], reads=[zf.r], key="zkf")
                self.dma("sp", self.o_kpe[seq, l, r0:r0 + 128, :], zf.t[:, 128:160], reads=[zf.r], key="zkf")
                self.dma("sp", self.o_gk[seq, l, r0:r0 + 128, :], zf.t[:, 160:288], reads=[zf.r], key="zkf")
                self.dma("sp", self.o_gv[seq, l, r0:r0 + 128, :], zf.t[:, 288:416], reads=[zf.r], key="zkf")
            else:
                ti = rope_tile0 + t
                self.stt(k_.t[:, 0, :], pb.t[:, 0:128], st.t[:, 0:1], bc.t[:, 0:128], ALU.mult, ALU.mult, [pb.r, st.r, bc.r], [k_.r])
                xin = pb.t[:, 128:160].rearrange("p (h a b f) -> p h a b f", h=1, a=2, b=2)
                xo = k_.t[:, 1, 64:96].rearrange("p (h a b f) -> p h a b f", h=1, a=2, b=2)
                self.rope_tm(xo, xin, ti, 0, 8, 1, [pb.r], [k_.r])
                g = self.gkf
                for h in range(2):
                    self.stt(g.t[:, 64 * h:64 * h + 64], pb.t[:, 160 + 64 * h:224 + 64 * h], st.t[:, 1 + h:2 + h],
                             bc.t[:, 448:512], ALU.mult, ALU.mult, [pb.r, st.r, bc.r], [g.r])
                xin = g.t[:].rearrange("p (h a b f) -> p h a b f", h=2, a=2, b=2)
                xo = k_.t[:, 2, :].rearrange("p (h a b f) -> p h a b f", h=2, a=2, b=2)
                self.rope_tm(xo, xin, ti, 8, 16, 2, [g.r], [k_.r])
            if cut <= 2:
                continue
            self.cp(self.Vgqa.t[:, kt0 + t, :, 0:64], pb.t[:, 288:416].rearrange("p (h x) -> p h x", h=2), [pb.r], [self.Vgqa.r],
                    eng="act")
            if cut <= 3:
                continue
            self.kv_transposes(t, kt0 + t)
        if getattr(self, "cut", 99) <= 4:
            return
        self.kv_finish(ntiles, kt0)

    def cache_kv(self, l):
        k_ = self.kst
        for t in range(2):
            r0 = t * 128
            self.dma("pool", k_.t[:, 0, :], self.c_ckv[l, r0:r0 + 128, :], writes=[k_.r], key="kst")
            self.dma("pool", k_.t[:, 1, 64:96], self.c_kpe[l, r0:r0 + 128, :], writes=[k_.r], key="kst")
            self.dma("pool", k_.t[:, 2, :], self.c_gk[l, r0:r0 + 128, :], writes=[k_.r], key="kst")
            self.dma("pool", self.Vgqa.t[:, t, :, 0:64], self.c_gv[l, r0:r0 + 128, :].rearrange("p (h x) -> p h x", h=2),
                     writes=[self.Vgqa.r], key="Vgqa")
            self.kv_transposes(t, t)
        self.kv_finish(2, 0)

    def hgrn_pass(self, l, d, mcol0, segments, ctx_seq0=None):
        mres = self.mixHG.r[mcol0 // TB]
        mcols = slice(mcol0, mcol0 + TB)
        hf_off = O_HFB if d else O_HFF
        slotA, wA = self.win_slot(l, [(O_HQ, 256), (hf_off, 256)])
        slotB, wB = self.win_slot(l, [(O_HI, 512 if d else 256)])
        ht = self.ht
        P = self.P
        for t in range(4):
            pb = P[2]
            for k in range(KC):
                self.mm(pb.t[:, 0:256], self.hT.t[:, k, t * 128:(t + 1) * 128], wB[:, k, 0:256], k == 0, k == KC - 1,
                        [self.hT.r, slotB.r], [pb.r])
            self.cp(self.itm.t[:, t, :], pb.t[:, 0:256], [pb.r], [self.itm.r], eng=("act" if t % 2 else "dve"))
        mask = self.mAb if d else self.mAf
        for c in range(2):
            lb = self.lbp.t[:, l, d, c, 0:1]
            om = self.lbp.t[:, l, d, c, 1:2]
            vr32 = self.V32.r[d * 2 + c]
            pq, pf = P[0], P[1]
            for k in range(KC):
                self.mm(pq.t[:, :], wA[:, k, c * 128:(c + 1) * 128], self.hT.t[:, k, :], k == 0, k == KC - 1, [slotA.r, self.hT.r], [pq.r])
            for k in range(KC):
                self.mm(pf.t[:, :], wA[:, k, 256 + c * 128:256 + (c + 1) * 128], self.hT.t[:, k, :], k == 0, k == KC - 1,
                        [slotA.r, self.hT.r], [pf.r])
            h0, h1, h2, h3, h4 = [x.t for x in ht]
            r0, r1, r2, r3, r4 = [x.r for x in ht]
            q = self.hq32.t[:, c, :]
            rq = self.hq32.r
            self.act(h0[:], pq.t[:], AF.Exp, [pq.r], [r0], scale=-1.0)
            self.ts(h0[:], h0[:], 1.0, None, ALU.add, None, [r0], [r0])
            self.recip(h0[:], h0[:], [r0], [r0])
            self.stt(q, pq.t[:], 0.125, h0[:], ALU.mult, ALU.mult, [pq.r, r0], [rq])
            self.act(h1[:], pf.t[:], AF.Exp, [pf.r], [r1], scale=-1.0)
            self.ts(h1[:], h1[:], 1.0, None, ALU.add, None, [r1], [r1])
            self.recip(h1[:], h1[:], [r1], [r1])
            self.ts(h2[:], h1[:], om, lb, ALU.mult, ALU.add, [r1, self.lbp.r], [r2])
            self.act(h3[:], h2[:], AF.Identity, [r2], [r3], scale=-1.0, bias=1.0)
            self.act(h2[:], h2[:], AF.Ln, [r2], [r2], bias=1e-30)
            self.op("dve", lambda e: e.tensor_tensor_scan(out=h4[:], data0=self.segmask.t[:], data1=h2[:], initial=0.0,
                                                          op0=ALU.mult, op1=ALU.add), [self.segmask.r, r2], [r4])
            v3 = lambda ap: ap.rearrange("p (c t) -> p c t", t=32)
            if d == 0:
                bT, rb = h4, r4
            else:
                self.tt(h1[:], h2[:], h4[:], ALU.subtract, [r2, r4], [r1])
                self.tt(v3(h2[:]), v3(h1[:]), v3(h4[:])[:, :, 31:32].broadcast_to([128, 16, 32]), ALU.add, [r1, r4], [r2])
                bT, rb = h2, r2
            mid = v3(bT[:])[:, :, 16:17]
            bend = v3(bT[:])[:, :, 31:32] if d == 0 else v3(bT[:])[:, :, 0:1]
            hs = self.hsm
            self.act(hs.t[:, 0, :], mid[:, :, 0], AF.Exp, [rb], [hs.r])
            self.tt(hs.t[:, 1, :], bend[:, :, 0], mid[:, :, 0], ALU.subtract, [rb], [hs.r])
            self.act(hs.t[:, 1, :], hs.t[:, 1, :], AF.Exp, [hs.r], [hs.r])
            self.act(hs.t[:, 2, :], bend[:, :, 0], AF.Exp, [rb], [hs.r])
            self.tt(v3(h1[:]), v3(bT[:]), mid.broadcast_to([128, 16, 32]), ALU.subtract, [rb], [r1])
            self.act(h0[:], h1[:], AF.Exp, [r1], [r0])
            self.act(h1[:], h1[:], AF.Exp, [r1], [r1], scale=-1.0)
            self.tt(self.qh.t[:], q, h0[:], ALU.mult, [rq, r0], [self.qh.r])
            self.tt(self.kh.t[:], h3[:], h1[:], ALU.mult, [r3, r1], [self.kh.r])
            self.tt(v3(h0[:]), v3(h0[:]), hs.t[:, 0, :][:, :, None].broadcast_to([128, 16, 32]), ALU.mult, [r0, hs.r], [r0])
            self.tt(self.qt.t[:], q, h0[:], ALU.mult, [rq, r0], [self.qt.r])
            self.tt(v3(h1[:]), v3(h1[:]), hs.t[:, 1, :][:, :, None].broadcast_to([128, 16, 32]), ALU.mult, [r1, hs.r], [r1])
            self.tt(self.kd.t[:], h3[:], h1[:], ALU.mult, [r3, r1], [self.kd.r])
            for t in range(4):
                pt = self.PT[t % 2]
                self.tp(pt.t[:, 0:128], self.kd.t[:, t * 128:(t + 1) * 128], self.ident.t[:], [self.kd.r, self.ident.r], [pt.r])
                self.cp(self.kdtm.t[:, t, :], pt.t[:, 0:128], [pt.r], [self.kdtm.r], eng=("act" if t % 2 else "dve"))
            psA, psO = [P[2], P[0]], [P[3], P[4]]
            psTl = [(P[5].t, P[5].r), (self.PT[1].t[:].bitcast(F32), self.PT[1].r)]
            NV = self.NV
            step = 0
            tcount = 0
            for (order, reset, out_seq) in segments:
                s0 = step % NV
                if reset:
                    self.memset(self.Vr32.t[:, s0, :], 0.0, [self.Vr32.r[s0]], eng="dve")
                    self.memset(self.Vrb.t[:, s0, :], 0.0, [self.Vrb.r[s0]], eng="dve")
                else:
                    self.cp(self.Vr32.t[:, s0, :], self.V32.t[:, d, c, :], [vr32], [self.Vr32.r[s0]])
                    self.cp(self.Vrb.t[:, s0, :], self.V32.t[:, d, c, :], [vr32], [self.Vrb.r[s0]], eng="act")
                for t in order:
                    tc = slice(t * 128, (t + 1) * 128)
                    psT, psTr = psTl[tcount % 2]
                    tcount += 1
                    for h in range(2):
                        hp = slice(64 * h, 64 * h + 64)
                        self.mm(psA[h].t[:, 0:128], self.kh.t[hp, tc], self.qh.t[hp, tc], True, True,
                                [self.kh.r, self.qh.r], [psA[h].r])
                        self.tt(self.ATm.t[:, h, :], psA[h].t[:, 0:128], mask.t[:], ALU.mult, [psA[h].r, mask.r], [self.ATm.r])
                    for h in range(2):
                        hp = slice(64 * h, 64 * h + 64)
                        self.mm(psO[h].t[hp, tc], self.itm.t[:, t, c * 128 + 64 * h:c * 128 + 64 * h + 64], self.ATm.t[:, h, :],
                                True, False, [self.itm.r, self.ATm.r], [psO[h].r], tile_position=(0, 64 * h))
                    jorder = list(range(4) if d == 0 else range(3, -1, -1))
                    self.tt(self.iexp.t[:], self.itm.t[:, t, c * 128:(c + 1) * 128][:, None, :].broadcast_to([128, 4, 128]),
                            self.cmask4.t[:][:, :, None].broadcast_to([128, 4, 128]), ALU.mult, [self.itm.r, self.cmask4.r], [self.iexp.r])
                    for h in range(2):
                        hp = slice(64 * h, 64 * h + 64)
                        self.mm(psT[hp, 0:256].rearrange("p (j x) -> p j x", j=4), self.kdtm.t[:, t, hp], self.iexp.t[:, :, hp],
                                True, True, [self.kdtm.r, self.iexp.r], [psTr], tile_position=(0, 64 * h))
                    for j in jorder:
                        cc = slice(t * 128 + 32 * j, t * 128 + 32 * j + 32)
                        sv, sn = step % NV, (step + 1) % NV
                        for h in range(2):
                            hp = slice(64 * h, 64 * h + 64)
                            self.mm(psO[h].t[hp, cc], self.Vrb.t[hp, sv, :], self.qt.t[hp, cc], False, j == jorder[-1],
                                    [self.Vrb.r[sv], self.qt.r], [psO[h].r], tile_position=(64 * h, 64 * h))
                        ci = t * 4 + j
                        self.stt(self.Vr32.t[:, sn, :], self.Vr32.t[:, sv, :], self.hsm.t[:, 2, ci:ci + 1], psT[:, j * 64:(j + 1) * 64],
                                 ALU.mult, ALU.add, [self.Vr32.r[sv], self.hsm.r, psTr], [self.Vr32.r[sn]])
                        self.cp(self.Vrb.t[:, sn, :], self.Vr32.t[:, sn, :], [self.Vr32.r[sn]], [self.Vrb.r[sn]], eng="act")
                        step += 1
                se = step % NV
                if out_seq is not None:
                    self.dma("sp", self.o_hg[out_seq, l, d, c * 128:(c + 1) * 128, :], self.Vr32.t[:, se, :], reads=[self.Vr32.r[se]],
                             key="Vr32_%d" % se)
                else:
                    self.cp(self.V32.t[:, d, c, :], self.Vr32.t[:, se, :], [self.Vr32.r[se]], [vr32])
            if d == 0:
                for h in range(2):
                    hp = slice(64 * h, 64 * h + 64)
                    self.cp(self.mixHG.t[hp, c, mcols], psO[h].t[hp, :], [psO[h].r], [mres], eng="act")
            else:
                pg = P[1]
                for k in range(KC):
                    self.mm(pg.t[:, :], wB[:, k, 256 + c * 128:256 + (c + 1) * 128], self.hT.t[:, k, :], k == 0, k == KC - 1,
                            [slotB.r, self.hT.r], [pg.r])
                for h in range(2):
                    hp = slice(64 * h, 64 * h + 64)
                    self.tt(h0[hp, :], psO[h].t[hp, :], self.mixHG.t[hp, c, mcols], ALU.add, [psO[h].r, mres], [r0])
                self.act(h1[:], h0[:], AF.Square, [r0], [r1])
                pss = P[2]
                self.mm(pss.t[:, :], self.onesblk.t[:], h1[:], True, True, [self.onesblk.r, r1], [pss.r])
                self.act(h2[:], pss.t[:], AF.Ln, [pss.r], [r2], scale=1.0 / 64.0, bias=EPS)
                self.act(h2[:], h2[:], AF.Exp, [r2], [r2], scale=-0.5)
                self.tt(h0[:], h0[:], h2[:], ALU.mult, [r0, r2], [r0])
                self.act(h3[:], pg.t[:], AF.Exp, [pg.r], [r3], scale=-1.0)
                self.ts(h3[:], h3[:], 1.0, None, ALU.add, None, [r3], [r3])
                self.recip(h3[:], h3[:], [r3], [r3])
                self.stt(h0[:], h0[:], self.hgn.t[:, l:l + 1], pg.t[:], ALU.mult, ALU.mult, [r0, self.hgn.r, pg.r], [r0])
                self.tt(self.mixHG.t[:, c, mcols], h0[:], h3[:], ALU.mult, [r0, r3], [mres])

    def load_hgrn_state(self, l):
        for d in range(2):
            for c in range(2):
                i = d * 2 + c
                self.dma("sp", self.V32.t[:, d, c, :], self.s_hg[l, d, c * 128:(c + 1) * 128, :], writes=[self.V32.r[i]],
                         key="V32_%d" % i)

    def q_pass(self, l, rope_tile0=None):
        slot1, w1 = self.win_slot(l, [(O_MQ, 256)])
        slot2, w2 = self.win_slot(l, [(O_GQ, 384)])
        bc = self.bc
        w = self.wsm
        wuq = w.t[:, 0:1152].rearrange("p (k n) -> p k n", k=2)
        P = self.P
        st = self.st8
        for t in range(4):
            tcs = slice(t * 128, (t + 1) * 128)
            p0, p1, p2, p3 = P[2 * (t % 2)], P[2 * (t % 2) + 1], P[4], P[5]
            for k in range(KC):
                self.mm(p0.t[:, 0:256], self.hT.t[:, k, tcs], w1[:, k, :], k == 0, k == KC - 1, [self.hT.r, slot1.r], [p0.r])
            for k in range(KC):
                self.mm(p1.t[:, 0:384], self.hT.t[:, k, tcs], w2[:, k, :], k == 0, k == KC - 1, [self.hT.r, slot2.r], [p1.r])
            self.act(self.sq.t[:, 0:256], p0.t[:, 0:256], AF.Square, [p0.r], [self.sq.r, st.r], scale=1.0 / 16.0,
                     accum_out=st.t[:, 0:1])
            self.act(st.t[:, 0:1], st.t[:, 0:1], AF.Ln, [st.r], [st.r], bias=EPS)
            self.act(self.sq.t[:, 0:384], p1.t[:, 0:384], AF.Square, [p1.r], [self.sq.r])
            self.op("dve", lambda e: e.tensor_reduce(out=st.t[:, 1:7], in_=self.sq.t[:, 0:384].rearrange("p (h x) -> p h x", h=6),
                                                     axis=AX.X, op=ALU.add), [self.sq.r], [st.r])
            self.act(st.t[:, 1:7], st.t[:, 1:7], AF.Ln, [st.r], [st.r], scale=1.0 / 64.0, bias=EPS)
            self.act(st.t[:, 0:7], st.t[:, 0:7], AF.Exp, [st.r], [st.r], scale=-0.5)
            k_ = self.kst
            mqn = k_.t[:, 0:2, :]
            self.stt(mqn, p0.t[:, 0:256].rearrange("p (a b) -> p a b", a=2), st.t[:, 0:1],
                     bc.t[:, 128:384].rearrange("p (a b) -> p a b", a=2), ALU.mult, ALU.mult, [p0.r, st.r, bc.r], [k_.r])
            pt0 = self.PT[0]
            self.tp(pt0.t[:, 0:128], k_.t[:, 0, :], self.ident.t[:], [k_.r, self.ident.r], [pt0.r])
            self.tp(pt0.t[:, 128:256], k_.t[:, 1, :], self.ident.t[:], [k_.r, self.ident.r], [pt0.r])
            self.cp(self.mqnT.t[:], pt0.t[:, 0:256].rearrange("p (a b) -> p a b", a=2), [pt0.r], [self.mqnT.r], eng="act")
            for kk in range(2):
                self.mm(p2.t[:, 0:480], self.mqnT.t[:, kk, :], wuq[:, kk, 0:480], kk == 0, kk == 1, [self.mqnT.r, w.r], [p2.r])
            for kk in range(2):
                self.mm(p3.t[:, 0:96], self.mqnT.t[:, kk, :], wuq[:, kk, 480:576], kk == 0, kk == 1, [self.mqnT.r, w.r], [p3.r])
            qs = self.qstage
            v5 = p2.t[:, 0:480].rearrange("p (h x) -> p h x", h=5)
            self.cp(qs.t[:, 0:5, 0:64], v5[:, :, 0:64], [p2.r], [qs.r], eng="act")
            self.cp(qs.t[:, 5, 0:64], p3.t[:, 0:64], [p3.r], [qs.r], eng="act")
            if rope_tile0 is None:
                self.cp(qs.t[:, 0:5, 64:96], v5[:, :, 64:96], [p2.r], [qs.r])
                self.cp(qs.t[:, 5, 64:96], p3.t[:, 64:96], [p3.r], [qs.r])
            else:
                qp = self.qpe
                self.cp(qp.t[:, 0:5, :], v5[:, :, 64:96], [p2.r], [qp.r])
                self.cp(qp.t[:, 5, :], p3.t[:, 64:96], [p3.r], [qp.r])
                self.rope_tm(qs.t[:, :, 64:96].rearrange("p h (a b f) -> p h a b f", a=2, b=2),
                             qp.t[:].rearrange("p h (a b f) -> p h a b f", a=2, b=2), rope_tile0 + t, 0, 8, 6, [qp.r], [qs.r])
            pt1 = self.PT[1]
            for h in range(6):
                self.tp(pt1.t[0:96, h * 128:(h + 1) * 128], qs.t[:, h, :], self.ident.t[:], [qs.r, self.ident.r], [pt1.r])
            self.cp(self.QmlaT.t[0:96, :, tcs], pt1.t[0:96, 0:768].rearrange("p (h x) -> p h x", h=6), [pt1.r], [self.QmlaT.r])
            z = self.zq
            zv = z.t[:, 0:384].rearrange("p (h x) -> p h x", h=6)
            self.tt(zv, p1.t[:, 0:384].rearrange("p (h x) -> p h x", h=6), st.t[:, 1:7][:, :, None].broadcast_to([128, 6, 64]),
                    ALU.mult, [p1.r, st.r], [z.r])
            gq = self.gqb
            gperm = gq.t[:].rearrange("p (j a x) -> p a j x", a=2, x=64)
            znat = z.t[:, 0:384].rearrange("p (a j x) -> p a j x", a=2, x=64)
            ggq4 = bc.t[:, 384:448][:, None, None, :].broadcast_to([128, 2, 3, 64])
            ggq = bc.t[:, 384:448][:, None, :].broadcast_to([128, 6, 64])
            if rope_tile0 is None:
                self.tt(gperm, znat, ggq4, ALU.mult, [z.r, bc.r], [gq.r])
            else:
                self.tt(zv, zv, ggq, ALU.mult, [z.r, bc.r], [z.r])
                self.rope_tm(k_.t[:].rearrange("p c (h2 a b f) -> p (c h2) a b f", h2=2, a=2, b=2),
                             z.t[:, 0:384].rearrange("p (h a b f) -> p h a b f", h=6, a=2, b=2), rope_tile0 + t, 8, 16, 6,
                             [z.r], [k_.r])
                self.cp(gperm, k_.t[:].rearrange("p c x -> p (c x)").rearrange("p (a j x) -> p a j x", a=2, x=64), [k_.r], [gq.r])
            for j in range(3):
                self.tp(pt0.t[:, 256 + j * 128:256 + (j + 1) * 128], gq.t[:, j * 128:(j + 1) * 128], self.ident.t[:],
                        [gq.r, self.ident.r], [pt0.r])
            self.cp(self.QgqaT.t[:, :, tcs], pt0.t[:, 256:640].rearrange("p (j x) -> p j x", j=3), [pt0.r], [self.QgqaT.r], eng="act")

    def attention(self, N, qcol0, keytiles):
        P = self.P
        qc = slice(qcol0, qcol0 + N)
        nk = len(keytiles)

        def head_ops(hh):
            if hh < 6:
                h = hh
                return (lambda kt: self.KmlaT.t[0:96, h, kt * 128:(kt + 1) * 128], self.QmlaT.t[0:96, h, qc],
                        lambda kt: self.Vmla.t[:, kt, h, :], self.KmlaT.r, self.QmlaT.r, self.Vmla.r, MLA_SCALE)
            g = hh - 6
            part, j = g // 3, g % 3
            pp = slice(64 * part, 64 * part + 64)
            return (lambda kt: self.KgqaT.t[pp, kt * 128:(kt + 1) * 128], self.QgqaT.t[pp, j, qc],
                    lambda kt: self.Vgqa.t[:, kt, part, :], self.KgqaT.r, self.QgqaT.r, self.Vgqa.r, GQA_SCALE)

        units = [(hh, i) for hh in range(12) for i in range(nk)]
        hops = {hh: head_ops(hh) for hh in range(12)}

        def issue_qk(idx):
            hh, i = units[idx]
            Kt, Qt, Vv, kr, qr, vr, scale = hops[hh]
            psS = P[idx % 3]
            self.mm(psS.t[:, 0:N], Kt(keytiles[i]), Qt, True, True, [kr, qr], [psS.r])

        def finalize2(hh):
            po = P[4 + hh % 2]
            rc = self.rc
            pb = P[3]
            self.mm(pb.t[0:64, 0:N], self.onesf.t[64:65, 0:64], rc.t[64:65, 0:N], True, True, [self.onesf.r, rc.r], [pb.r])
            bcs = self.bcs
            self.cp(bcs.t[:, 0:N], pb.t[0:64, 0:N], [pb.r], [bcs.r])
            self.tt(self.mixA.t[0:64, hh, qc], po.t[0:64, 0:N], bcs.t[:, 0:N], ALU.mult, [po.r, bcs.r], [self.mixA.r[hh]])

        issue_qk(0)
        if len(units) > 1:
            issue_qk(1)
        pending = []
        for idx, (hh, i) in enumerate(units):
            Kt, Qt, Vv, kr, qr, vr, scale = hops[hh]
            psS = P[idx % 3]
            pt = self.PTb[idx % 3]
            po = P[4 + hh % 2]
            self.act(pt.t[:, 0:N], psS.t[:, 0:N], AF.Exp, [psS.r], [pt.r], scale=scale)
            if idx + 2 < len(units):
                issue_qk(idx + 2)
            self.mm(po.t[0:65, 0:N], Vv(keytiles[i]), pt.t[:, 0:N], i == 0, i == nk - 1, [vr, pt.r], [po.r])
            while pending and pending[0][0] <= idx:
                finalize2(pending.pop(0)[1])
            if i == nk - 1:
                self.recip(self.rc.t[64:65, 0:N], po.t[64:65, 0:N], [po.r], [self.rc.r], exact=True)
                pending.append((idx + min(8, nk), hh))
        while pending:
            finalize2(pending.pop(0)[1])

    def layernorm(self, l, which):
        gname, bname = ("ln1_g", "ln1_b") if which == 1 else ("ln2_g", "ln2_b")
        P = self.P
        pm, pv = P[2], P[3]
        tmps = [self.lnt, self.lnr]
        for k in range(KC):
            self.mm(pm.t[:, :], self.onesf.t[:], self.xb.t[:, k, :], k == 0, k == KC - 1, [self.onesf.r, self.xb.r[k]], [pm.r])
            tq = tmps[k % 2]
            self.act(tq.t[:], self.xb.t[:, k, :], AF.Square, [self.xb.r[k]], [tq.r])
            self.mm(pv.t[:, :], self.onesf.t[:], tq.t[:], k == 0, k == KC - 1, [self.onesf.r, tq.r], [pv.r])
        m, r, t_ = self.lnm, self.lnr, self.lnt
        self.act(m.t[:], pm.t[:], AF.Copy, [pm.r], [m.r], scale=1.0 / D)
        self.tt(t_.t[:], m.t[:], m.t[:], ALU.mult, [m.r], [t_.r])
        self.stt(r.t[:], pv.t[:], 1.0 / D, t_.t[:], ALU.mult, ALU.subtract, [pv.r, t_.r], [r.r])
        self.act(r.t[:], r.t[:], AF.Ln, [r.r], [r.r], bias=EPS)
        self.act(r.t[:], r.t[:], AF.Exp, [r.r], [r.r], scale=-0.5)
        self.stt(m.t[:], m.t[:], -1.0, r.t[:], ALU.mult, ALU.mult, [m.r, r.r], [m.r])
        t2 = [self.lnt, self.gy[0], self.gy[1]]
        for k in range(KC):
            tq = t2[k % 3]
            self.tt(tq.t[:], self.xb.t[:, k, :], r.t[:], ALU.mult, [self.xb.r[k], r.r], [tq.r])
            self.tt(tq.t[:], tq.t[:], m.t[:], ALU.add, [tq.r, m.r], [tq.r], eng="pool")
            self.act(self.xb.t[:, k, :], tq.t[:], AF.Identity, [tq.r, self.par.r], [self.xb.r[k]],
                     scale=self.pcol(gname, l, k), bias=self.pcol(bname, l, k))

    def outproj_ln1(self, l, g, mcol0, mix_own=None):
        P = self.P
        mres = self.mixHG.r[mcol0 // TB]
        for dp in range(4):
            c0 = dp * 256
            slot = self.wload([
                (lambda t: t[:, 0:3584].rearrange("p (k n) -> p k n", k=14)[:, 0:2, :],
                 self.w_out[l][0:256, c0:c0 + 256].rearrange("(k p) n -> p k n", p=128)),
                (lambda t: t[0:64, 0:3584].rearrange("p (k n) -> p k n", k=14)[:, 2:14, :],
                 self.w_out[l][256:1024, c0:c0 + 256].rearrange("(h p) n -> p h n", p=64)),
            ])
            wv = slot.t[:, 0:3584].rearrange("p (k n) -> p k n", k=14)
            for dd in range(2):
                dc = dp * 2 + dd
                py = P[dc % 2]
                for kc in range(2):
                    mrhs = self.mixHG.t[:, kc, mcol0:mcol0 + TB] if mix_own is None else mix_own[:, kc, :]
                    self.mm(py.t[:, :], wv[:, kc, dd * 128:(dd + 1) * 128], mrhs, kc == 0, False,
                            [slot.r, mres] + ([self.mixHG.r[1]] if mix_own is not None else []), [py.r])
                for hh in range(12):
                    self.mm(py.t[:, :], wv[0:64, 2 + hh, dd * 128:(dd + 1) * 128], self.mixA.t[0:64, hh, :], False, hh == 11,
                            [slot.r, self.mixA.r[hh]], [py.r])
                gy = self.gy[dc % 2]
                self.act(gy.t[:], py.t[:], AF.Copy, [py.r, self.mT.r], [gy.r], scale=self.mvec(l, g, "g1", dc))
                self.stt(self.xb.t[:, dc, :], self.xb.t[:, dc, :], ALPHA, gy.t[:], ALU.mult, ALU.add, [self.xb.r[dc], gy.r], [self.xb.r[dc]])
        self.layernorm(l, 1)

    def ffn_ln2(self, l, g):
        P = self.P
        self.prepH(l, g, "2")
        first = not self.ffn_cached[l]
        self.ffn_cached[l] = True
        for fp in range(11):
            if first:
                slot = self.wload([
                    (lambda t: t[:, 0:4096].rearrange("p (k n) -> p k n", k=8)[:, :, 0:256],
                     self.w_ffn_in[l][:, fp * 256:(fp + 1) * 256].rearrange("(k p) n -> p k n", p=128)),
                    (lambda t: t[:, 0:4096].rearrange("p (k n) -> p k n", k=8)[:, :, 256:512],
                     self.w_ffn_in[l][:, DFF + fp * 256:DFF + (fp + 1) * 256].rearrange("(k p) n -> p k n", p=128)),
                ])
                self.dma("sp", self.wsc_in[l][fp * 128:(fp + 1) * 128, :], slot.t[:, 0:4096], reads=[slot.r], writes=[self.wscr[l]],
                         key=slot.r.name + "_st")
            else:
                slot = self.next_ring()
                self.dma("pool", slot.t[:, 0:4096], self.wsc_in[l][fp * 128:(fp + 1) * 128, :], reads=[self.wscr[l]], writes=[slot.r],
                         key=slot.r.name)
            wv = slot.t[:, 0:4096].rearrange("p (k n) -> p k n", k=8)
            for ff in range(2):
                f = fp * 2 + ff
                pg, pu = P[f % 2], P[2 + f % 2]
                for k in range(KC):
                    self.mm(pg.t[:, :], wv[:, k, ff * 128:(ff + 1) * 128], self.hT.t[:, k, :], k == 0, k == KC - 1, [slot.r, self.hT.r], [pg.r])
                for k in range(KC):
                    self.mm(pu.t[:, :], wv[:, k, 256 + ff * 128:256 + (ff + 1) * 128], self.hT.t[:, k, :], k == 0, k == KC - 1,
                            [slot.r, self.hT.r], [pu.r])
                sg = self.gy[f % 2]
                self.act(sg.t[:], pg.t[:], AF.Silu, [pg.r], [sg.r])
                self.tt(self.actT.t[:, f, :], sg.t[:], pu.t[:], ALU.mult, [sg.r, pu.r], [self.actT.r])
        for dc in range(KC):
            if first:
                slot = self.wload([(lambda t: t[:, 0:2816].rearrange("p (f n) -> p f n", f=NFC),
                                    self.w_ffn_out[l][:, dc * 128:(dc + 1) * 128].rearrange("(f p) n -> p f n", p=128))])
                self.dma("sp", self.wsc_out[l][dc * 128:(dc + 1) * 128, :], slot.t[:, 0:2816], reads=[slot.r], writes=[self.wscr[l]],
                         key=slot.r.name + "_st")
            else:
                slot = self.next_ring()
                self.dma("pool", slot.t[:, 0:2816], self.wsc_out[l][dc * 128:(dc + 1) * 128, :], reads=[self.wscr[l]], writes=[slot.r],
                         key=slot.r.name)
            wv = slot.t[:, 0:2816].rearrange("p (f n) -> p f n", f=NFC)
            pd = P[4 + dc % 2]
            for f in range(NFC):
                self.mm(pd.t[:, :], wv[:, f, :], self.actT.t[:, f, :], f == 0, f == NFC - 1, [slot.r, self.actT.r], [pd.r])
            gy = self.lnt if dc % 2 else self.lnr
            self.act(gy.t[:], pd.t[:], AF.Copy, [pd.r, self.mT.r], [gy.r], scale=self.mvec(l, g, "g2", dc))
            self.stt(self.xb.t[:, dc, :], self.xb.t[:, dc, :], ALPHA, gy.t[:], ALU.mult, ALU.add, [self.xb.r[dc], gy.r], [self.xb.r[dc]])
        self.layernorm(l, 2)

    def run_ctx_block(self, cb):
        self.load_x_block(self.xc[cb * TB:(cb + 1) * TB, :])
        for l in range(self.nlayers):
            self.load_layer_small(l)
            self.prepH(l, 0)
            self.phase("c%d.l%d.kv" % (cb, l))
            self.kv_pass(l, 4, 0, ctx_out=2 * cb)
            self.phase("c%d.l%d.hgf" % (cb, l))
            self.hgrn_pass(l, 0, 0, [([0, 1], True, 2 * cb), ([2, 3], True, 2 * cb + 1)])
            self.phase("c%d.l%d.hgb" % (cb, l))
            self.hgrn_pass(l, 1, 0, [([1, 0], True, 2 * cb), ([3, 2], True, 2 * cb + 1)])
            self.phase("c%d.l%d.q" % (cb, l))
            self.q_pass(l, None)
            self.phase("c%d.l%d.att" % (cb, l))
            self.attention(256, 0, [0, 1])
            self.attention(256, 256, [2, 3])
            self.phase("c%d.l%d.outp" % (cb, l))
            self.outproj_ln1(l, 0, 0)
            self.phase("c%d.l%d.ffn" % (cb, l))
            self.ffn_ln2(l, 0)
        self.store_y_block(self.y_c[cb * TB:(cb + 1) * TB, :])

    def run_sample(self):
        for blk in range(4):
            self.load_x_block(self.xsd[blk * TB:(blk + 1) * TB, :])
            self.store_xs_block(blk)
        for l in range(self.nlayers):
            self.load_layer_small(l)
            self.load_hgrn_state(l)
            self.cache_kv(l)
            for blk in range(4):
                self.load_xs_block(blk)
                self.prepH(l, 1)
                self.phase("s.l%d.b%d.kv" % (l, blk))
                self.kv_pass(l, 4, 2 + 4 * blk, rope_tile0=4 * blk)
                self.phase("s.l%d.b%d.hgf" % (l, blk))
                self.hgrn_pass(l, 0, blk * TB, [([0, 1, 2, 3], False, None)])
            for blk in range(3, -1, -1):
                self.load_xs_block(blk)
                self.prepH(l, 1)
                self.phase("s.l%d.b%d.hgb" % (l, blk))
                self.hgrn_pass(l, 1, blk * TB, [([3, 2, 1, 0], False, None)])
            last = (l == self.nlayers - 1)
            if not last:
                for qb in range(4):
                    self.load_xs_block(qb)
                    self.prepH(l, 1)
                    self.phase("s.l%d.q%d.q" % (l, qb))
                    self.q_pass(l, rope_tile0=4 * qb)
                    self.phase("s.l%d.q%d.att" % (l, qb))
                    self.attention(TB, 0, list(range(18)))
                    self.phase("s.l%d.q%d.outp" % (l, qb))
                    self.outproj_ln1(l, 1, qb * TB)
                    self.phase("s.l%d.q%d.ffn" % (l, qb))
                    self.ffn_ln2(l, 1)
                    self.store_xs_block(qb)
            else:
                for blk in range(4):
                    self.dma("sp", self.mh2[blk * 128:(blk + 1) * 128, :].rearrange("p (c n) -> p c n", c=2),
                             self.mixHG.t[:, :, blk * TB:(blk + 1) * TB], reads=[self.mixHG.r[blk]], writes=[self.mh2r], key="mh2")
                self.gather(self.mixHG.t[:, 0, 0:2 * TB], self.mh2[:, :], [self.mh2r], [self.mixHG.r[0], self.mixHG.r[1]], "mhg")
                self.gather(self.rope.t[:, 0:4].rearrange("p a b c d -> p (a b c d)"), self.rp2[:, :], [self.rp2r], [self.rope.r], "rpg")
                self.gather(self.xb.t[:].rearrange("p k n -> p (k n)"), self.xs_dram[:, :], self.xsr, [self.xb.r], "xbg")
                if "xbg" in self.dbg_out:
                    jj = self.dbg_j
                    self.dump("xbg", self.xb.t[:].rearrange("p k n -> p (k n)"), self.xb.r)
                    self.dma("sp", self.dbg_out["xbs"], self.xs_dram[jj * 128:(jj + 1) * 128, :], reads=self.xsr, key="dbgx")
                    self.dump("rpg", self.rope.t[:, 0:4].rearrange("p a b c d -> p (a b c d)"), self.rope.r)
                    self.dma("sp", self.dbg_out["rps"], self.rp2[jj * 128:(jj + 1) * 128, :], reads=[self.rp2r], key="dbgr")
                    self.dma("sp", self.dbg_out["mhs"], self.mh2[jj * 128:(jj + 1) * 128, :], reads=[self.mh2r], key="dbgm")
                    self.dma("sp", self.dbg_out["mhg"], self.mixHG.t[:, 0, 0:2 * TB], reads=[self.mixHG.r[0]], key="dbgm2")
                self.prepH(l, 1)
                self.phase("s.l%d.own.q" % l)
                self.q_pass(l, rope_tile0=0)
                self.phase("s.l%d.own.att" % l)
                self.attention(TB, 0, list(range(18)))
                self.phase("s.l%d.own.outp" % l)
                self.outproj_ln1(l, 1, 0, mix_own=self.mixHG.t[:, 0, 0:2 * TB].rearrange("p (c n) -> p c n", c=2))
                self.phase("s.l%d.own.ffn" % l)
                self.ffn_ln2(l, 1)
                self.store_y_block(self.y_own[:, :])

    def build(self):
        self.init_consts()
        self.memset(self.kst.t[:], 0.0, [self.kst.r], eng="dve")
        self.load_params()
        self.ada()
        self.phase("start")
        if self.do_ctx:
            for cb in range(2):
                self.run_ctx_block(cb)
        self.phase("sample_load")
        if self.do_sample:
            self.run_sample()
        self.phase("end")
        self.S.emit()
        return self.nc


_WKEYS = ["w_ada", "b_ada", "w_in", "hg_lb", "hg_norm", "mla_q_norm", "mla_w_uq", "mla_kv_norm", "mla_w_ukv",
          "gqa_q_norm", "gqa_k_norm", "w_out", "ln1_g", "ln1_b", "w_ffn_in", "w_ffn_out", "ln2_g", "ln2_b"]


def make_in_maps(inp):
    f = lambda a: np.ascontiguousarray(np.asarray(a), dtype=np.float32)
    shared = {k: f(inp[k]) for k in _WKEYS}
    maps = []
    for c in range(8):
        b = c // 4
        m = dict(shared)
        m["xc"] = f(inp["x_prompt"][4 * c:4 * c + 4]).reshape(1024, D)
        m["xs"] = f(inp["x_sample"][b])
        m["c_ckv"] = f(inp["cache_mla_ckv"][b])
        m["c_kpe"] = f(inp["cache_mla_kpe"][b])
        m["c_gk"] = f(inp["cache_gqa_k"][b]).reshape(2, 256, 128)
        m["c_gv"] = f(inp["cache_gqa_v"][b]).reshape(2, 256, 128)
        m["s_hg"] = f(inp["state_hgrn"][b]).reshape(2, 2, 256, 64)
        m["cvec"] = np.stack([f(inp["c_ctx"]), f(inp["c"][b])])
        m["gidx"] = ((c % 4) * 128 + np.arange(128, dtype=np.int32)).reshape(128, 1)
        maps.append(m)
    return maps


def assemble(results):
    y_prompt = np.concatenate([r["y_c"].reshape(4, 256, D) for r in results], axis=0)
    y_sample = np.stack([np.concatenate([results[4 * b + j]["y_own"] for j in range(4)], axis=0) for b in range(2)], axis=0)
    st_ckv = np.concatenate([r["o_ckv"] for r in results], axis=0)
    st_kpe = np.concatenate([r["o_kpe"] for r in results], axis=0)
    st_gk = np.concatenate([r["o_gk"].reshape(4, 2, 256, 2, 64) for r in results], axis=0)
    st_gv = np.concatenate([r["o_gv"].reshape(4, 2, 256, 2, 64) for r in results], axis=0)
    st_hg = np.concatenate([r["o_hg"].reshape(4, 2, 2, 4, 64, 64) for r in results], axis=0)
    return tuple(np.ascontiguousarray(a, dtype=np.float32) for a in (y_prompt, y_sample, st_ckv, st_kpe, st_gk, st_gv, st_hg))


def kernel(**inputs):
    nc = Builder().build()
    res = run_bass_kernel_spmd(nc, make_in_maps(inputs), core_ids=list(range(8)))
    return assemble(res.results)
```

```python
import math
from contextlib import ExitStack
import numpy as np
import concourse.bass as bass
import concourse.mybir as mybir
from concourse.bass_utils import run_bass_kernel_spmd

F32 = mybir.dt.float32
BF16 = mybir.dt.bfloat16
I32 = mybir.dt.int32
AF = mybir.ActivationFunctionType
ALU = mybir.AluOpType
AX = mybir.AxisListType

ENGS = ("pe", "act", "dve", "pool", "sp")

D = 1024
KC = 8
TB = 512
DFF = 2816
NFC = 22
EPS = 1e-6
ALPHA = 4.0 ** 0.25
MLA_SCALE = 96.0 ** -0.5
GQA_SCALE = 64.0 ** -0.5
THETA = 10000.0
O_HQ, O_HFF, O_HFB, O_HI, O_HG, O_MQ, O_MKV, O_MKR, O_GQ, O_GK, O_GV = 0, 256, 512, 768, 1024, 1280, 1536, 1664, 1696, 2080, 2208


class Res:
    __slots__ = ("name", "w", "readers", "psum")

    def __init__(self, name, psum=False):
        self.name = name
        self.w = None
        self.readers = []
        self.psum = psum


class Op:
    __slots__ = ("eng", "fn", "deps", "is_dma", "key", "dma_target", "mile", "needs_inc")

    def __init__(self, eng, fn, is_dma, key):
        self.eng = eng
        self.fn = fn
        self.is_dma = is_dma
        self.key = key
        self.deps = ()
        self.dma_target = 0
        self.mile = 0
        self.needs_inc = False


class Sched:
    def __init__(self, nc):
        self.nc = nc
        self.ops = {e: [] for e in ENGS}
        self.dma_count = {}

    def res(self, name="r"):
        return Res(name)

    def add(self, eng, fn, reads=(), writes=(), dma=False, key=None):
        op = Op(eng, fn, dma, key)
        raw = set()
        deps = set()
        for r in reads:
            if r.w is not None:
                raw.add(r.w)
                deps.add(r.w)
            if r.psum:
                for rd in r.readers:
                    if rd.eng != eng:
                        deps.add(rd)
        for w in writes:
            if w.w is not None:
                deps.add(w.w)
            for rd in w.readers:
                deps.add(rd)
        fdeps = []
        for d in deps:
            if (not d.is_dma) and (not dma) and d.eng == eng:
                if eng == "pe":
                    continue
            fdeps.append(d)
        op.deps = fdeps
        for r in reads:
            r.readers.append(op)
        for w in writes:
            w.w = op
            w.readers = []
        if dma:
            c = self.dma_count.get(key, 0) + 1
            self.dma_count[key] = c
            op.dma_target = 16 * c
        self.ops[eng].append(op)
        return op

    def emit(self, final_wait_eng="sp"):
        nc = self.nc
        for e in ENGS:
            for op in self.ops[e]:
                for d in op.deps:
                    if not d.is_dma:
                        d.needs_inc = True
        for e in ENGS:
            c = 0
            for op in self.ops[e]:
                if (not op.is_dma) and op.needs_inc:
                    c += 1
                    op.mile = c
        with ExitStack() as es:
            esem = {e: es.enter_context(nc.semaphore("s_" + e)) for e in ENGS}
            dsem = {k: es.enter_context(nc.semaphore("d_%d" % i)) for i, k in enumerate(self.dma_count)}
            block = es.enter_context(nc.Block())
            sched = self

            def run_engine(e, eng):
                waited_e = {x: 0 for x in ENGS}
                waited_d = {}
                for op in sched.ops[e]:
                    need_e = {}
                    need_d = {}
                    for d in op.deps:
                        if d.is_dma:
                            if need_d.get(d.key, 0) < d.dma_target:
                                need_d[d.key] = d.dma_target
                        else:
                            if need_e.get(d.eng, 0) < d.mile:
                                need_e[d.eng] = d.mile
                    for x, v in need_e.items():
                        if v > waited_e[x]:
                            eng.wait_ge(esem[x], v)
                            waited_e[x] = v
                    for k, v in need_d.items():
                        if v > waited_d.get(k, 0):
                            eng.wait_ge(dsem[k], v)
                            waited_d[k] = v
                    ins = op.fn(eng)
                    if op.is_dma:
                        ins.then_inc(dsem[op.key], 16)
                    elif op.needs_inc:
                        ins.then_inc(esem[e], 1)
                if e == final_wait_eng:
                    for k, c in sched.dma_count.items():
                        if 16 * c > waited_d.get(k, 0):
                            eng.wait_ge(dsem[k], 16 * c)

            @block.tensor
            def _(eng):
                run_engine("pe", eng)

            @block.scalar
            def _(eng):
                run_engine("act", eng)

            @block.vector
            def _(eng):
                run_engine("dve", eng)

            @block.gpsimd
            def _(eng):
                run_engine("pool", eng)

            @block.sync
            def _(eng):
                run_engine("sp", eng)


class B:
    def __init__(self, t, r):
        self.t = t
        self.r = r


class Builder:
    def __init__(self, do_ctx=True, do_sample=True, nlayers=2, dbg=None):
        self.do_ctx = do_ctx
        self.do_sample = do_sample
        self.nlayers = nlayers
        self.dbg = dbg or {}
        nc = bass.Bass("TRN2", target_bir_lowering=False)
        self.nc = nc
        self.S = Sched(nc)
        self.sb_bytes = 0
        self._ring_i = 0
        self._uid = 0
        self.declare_dram()
        self.alloc()

    def din(self, name, shape, dt=F32):
        return self.nc.dram_tensor(name, list(shape), dt, kind="ExternalInput").ap()

    def dout(self, name, shape, dt=F32):
        return self.nc.dram_tensor(name, list(shape), dt, kind="ExternalOutput").ap()

    def sb(self, name, shape, dt, nres=1):
        t = self.nc.alloc_sbuf_tensor(name, list(shape), dt)
        n = 1
        for s in shape[1:]:
            n *= s
        self.sb_bytes += n * (2 if dt == BF16 else 4)
        if nres == 1:
            return B(t, self.S.res(name))
        return B(t, [self.S.res("%s_%d" % (name, i)) for i in range(nres)])

    def ps(self, name, dt=F32):
        t = self.nc.alloc_psum_tensor(name, [128, 512 if dt == F32 else 1024], dt)
        return B(t, Res(name, psum=True))

    @staticmethod
    def _flat(lst):
        out = []
        for x in lst:
            if x is None:
                continue
            if isinstance(x, (list, tuple)):
                out.extend(Builder._flat(x))
            else:
                out.append(x)
        return out

    def op(self, eng, fn, reads=(), writes=()):
        return self.S.add(eng, fn, self._flat(reads), self._flat(writes))

    def dma(self, q, out, in_, reads=(), writes=(), key=None, nc_ok=False):
        if nc_ok:
            f = lambda e: e.dma_start(out=out, in_=in_, allow_slow_non_contiguous=True)
        else:
            f = lambda e: e.dma_start(out=out, in_=in_)
        return self.S.add(q, f, self._flat(reads), self._flat(writes), dma=True, key=key)

    def mm(self, out, lhsT, rhs, start, stop, reads, writes, tile_position=None):
        if tile_position is None:
            f = lambda e: e.matmul(out, lhsT=lhsT, rhs=rhs, start=start, stop=stop)
        else:
            f = lambda e: e.matmul(out, lhsT=lhsT, rhs=rhs, start=start, stop=stop, tile_position=tile_position)
        return self.op("pe", f, reads, writes)

    def tp(self, out, in_, ident, reads, writes):
        return self.op("pe", lambda e: e.transpose(out, in_, ident), reads, writes)

    def act(self, out, in_, func, reads, writes, scale=1.0, bias=0.0, accum_out=None):
        if accum_out is None:
            f = lambda e: e.activation(out=out, in_=in_, func=func, bias=bias, scale=scale)
        else:
            f = lambda e: e.activation(out=out, in_=in_, func=func, bias=bias, scale=scale, accum_out=accum_out)
        return self.op("act", f, reads, writes)

    def tt(self, out, in0, in1, op, reads, writes, eng="dve"):
        return self.op(eng, lambda e: e.tensor_tensor(out=out, in0=in0, in1=in1, op=op), reads, writes)

    def ts(self, out, in0, s1, s2, op0, op1, reads, writes, eng="dve"):
        if s2 is None:
            f = lambda e: e.tensor_scalar(out=out, in0=in0, scalar1=s1, scalar2=None, op0=op0)
        else:
            f = lambda e: e.tensor_scalar(out=out, in0=in0, scalar1=s1, scalar2=s2, op0=op0, op1=op1)
        return self.op(eng, f, reads, writes)

    def stt(self, out, in0, scalar, in1, op0, op1, reads, writes, eng="dve"):
        return self.op(eng, lambda e: e.scalar_tensor_tensor(out=out, in0=in0, scalar=scalar, in1=in1, op0=op0, op1=op1),
                       reads, writes)

    def cp(self, out, in_, reads, writes, eng="dve"):
        if eng == "act":
            return self.act(out, in_, AF.Copy, reads, writes)
        return self.op(eng, lambda e: e.tensor_copy(out=out, in_=in_), reads, writes)

    def recip(self, out, in_, reads, writes, exact=False):
        if exact:
            return self.op("dve", lambda e: e.reciprocal(out=out, in_=in_), reads, writes)
        self.act(out, in_, AF.Ln, reads, writes)
        return self.act(out, out, AF.Exp, writes, writes, scale=-1.0)

    def memset(self, ap, val, writes, eng="pool", reads=()):
        return self.op(eng, lambda e: e.memset(ap, val), reads, writes)

    def declare_dram(self):
        self.xc = self.din("xc", [1024, D])
        self.xsd = self.din("xs", [2048, D])
        self.c_ckv = self.din("c_ckv", [2, 256, 128])
        self.c_kpe = self.din("c_kpe", [2, 256, 32])
        self.c_gk = self.din("c_gk", [2, 256, 128])
        self.c_gv = self.din("c_gv", [2, 256, 128])
        self.s_hg = self.din("s_hg", [2, 2, 256, 64])
        self.cvec = self.din("cvec", [2, D])
        self.w_ada = self.din("w_ada", [2, D, 6 * D])
        self.b_ada = self.din("b_ada", [2, 6 * D])
        self.w_in = self.din("w_in", [2, D, 2336])
        self.hg_lb = self.din("hg_lb", [2, 2, 256])
        self.hg_norm = self.din("hg_norm", [2, 64])
        self.mla_q_norm = self.din("mla_q_norm", [2, 256])
        self.w_uq = self.din("mla_w_uq", [2, 256, 576])
        self.mla_kv_norm = self.din("mla_kv_norm", [2, 128])
        self.w_ukv = self.din("mla_w_ukv", [2, 128, 768])
        self.gqa_q_norm = self.din("gqa_q_norm", [2, 64])
        self.gqa_k_norm = self.din("gqa_k_norm", [2, 64])
        self.w_out = self.din("w_out", [2, D, D])
        self.ln1_g = self.din("ln1_g", [2, D])
        self.ln1_b = self.din("ln1_b", [2, D])
        self.w_ffn_in = self.din("w_ffn_in", [2, D, 2 * DFF])
        self.w_ffn_out = self.din("w_ffn_out", [2, DFF, D])
        self.ln2_g = self.din("ln2_g", [2, D])
        self.ln2_b = self.din("ln2_b", [2, D])
        self.y_c = self.dout("y_c", [1024, D])
        self.y_own = self.dout("y_own", [TB, D])
        self.gidx_d = self.din("gidx", [128, 1], I32)
        self.o_ckv = self.dout("o_ckv", [4, 2, 256, 128])
        self.o_kpe = self.dout("o_kpe", [4, 2, 256, 32])
        self.o_gk = self.dout("o_gk", [4, 2, 256, 128])
        self.o_gv = self.dout("o_gv", [4, 2, 256, 128])
        self.o_hg = self.dout("o_hg", [4, 2, 2, 256, 64])
        self.dbg_out = {}
        for name, shape in self.dbg.items():
            self.dbg_out[name] = self.dout("dbg_" + name, shape)

    def alloc(self):
        sb = self.sb
        self.xs_dram = self.nc.dram_tensor("xs_scr", [4 * 128, KC * TB], F32)
        self.mh2 = self.nc.dram_tensor("mh_scr", [4 * 128, 2 * TB], BF16)
        self.rp2 = self.nc.dram_tensor("rp_scr", [4 * 128, 4 * 96], F32)
        self.wsc_in = [self.nc.dram_tensor("wsc_in%d" % l, [11 * 128, 4096], BF16) for l in range(2)]
        self.wsc_out = [self.nc.dram_tensor("wsc_out%d" % l, [8 * 128, 2816], BF16) for l in range(2)]
        self.wscr = [self.S.res("wsc%d" % l) for l in range(2)]
        self.ffn_cached = [False, False]
        self.mh2r = self.S.res("mh2")
        self.rp2r = self.S.res("rp2")
        self.gidx = sb("gidx_sb", [128, 1], I32)
        self.xsr = [self.S.res("xsd%d" % i) for i in range(4)]
        self.xb = sb("xb", [128, KC, TB], F32, nres=KC)
        self.hT = sb("hT", [128, KC, TB], BF16, nres=KC)
        self.KmlaT = sb("KmlaT", [128, 6, 2304], BF16)
        self.KgqaT = sb("KgqaT", [128, 2304], BF16)
        self.Vmla = sb("Vmla", [128, 18, 6, 65], BF16)
        self.Vgqa = sb("Vgqa", [128, 18, 2, 65], BF16)
        self.mixHG = sb("mixHG", [128, 2, 2048], BF16, nres=4)
        self.NW = 3
        self.ring = [sb("ring%d" % i, [128, 4096], BF16) for i in range(self.NW)]
        self.wsm = sb("wsm", [128, 1920], BF16)
        self.ident = sb("ident", [128, 128], BF16)
        self.identf = sb("identf", [128, 128], F32)
        self.onesf = sb("onesf", [128, 128], F32)
        self.onesblk = sb("onesblk", [128, 128], F32)
        self.segmask = sb("segmask", [128, TB], F32)
        self.mAf = sb("mAf", [128, 128], BF16)
        self.mAb = sb("mAb", [128, 128], BF16)
        self.pcs = sb("pcs", [128, 4], F32)
        self.rope = sb("rope", [128, 16, 2, 2, 24], F32)
        self.par = sb("par", [128, 224], F32)
        self.mT = sb("mT", [128, 2, 48, 2], F32)
        self.mP = sb("mP", [128, 2, 2, 2, 8], F32)
        self.lbp = sb("lbp", [128, 2, 2, 2, 3], F32)
        self.bc = sb("bc", [128, 512], F32)
        self.hgn = sb("hgn", [128, 2], F32)
        self.V32 = sb("V32", [128, 2, 2, 64], F32, nres=4)
        self.iexp = sb("iexp", [128, 4, 128], BF16)
        self.cmask4 = sb("cmask4", [128, 4], BF16)
        self.NV = 4
        self.Vr32 = sb("Vr32", [128, self.NV, 64], F32, nres=self.NV)
        self.Vrb = sb("Vrb", [128, self.NV, 64], BF16, nres=self.NV)
        self.stg = [sb("stg0", [128, 1024], F32)]
        self.ht = [sb("ht%d" % i, [128, TB], F32) for i in range(5)]
        self.hq32 = sb("hq32", [128, 2, TB], F32)
        self.qh = sb("qh", [128, TB], BF16)
        self.kh = sb("kh", [128, TB], BF16)
        self.qt = sb("qt", [128, TB], BF16)
        self.kd = sb("kd", [128, TB], BF16)
        self.kdtm = sb("kdtm", [128, 4, 128], BF16)
        self.itm = sb("itm", [128, 4, 256], BF16)
        self.ATm = sb("ATm", [128, 2, 128], BF16)
        self.hsm = sb("hsm", [128, 3, 16], F32)
        self.zkf = sb("zkf", [128, 416], F32)
        self.gkf = sb("gkf", [128, 128], F32)
        self.kst = sb("kst", [128, 3, 128], BF16)
        self.ckvT = sb("ckvT", [128, TB], BF16)
        self.sq = sb("sq", [128, 512], F32)
        self.st8 = sb("st8", [128, 8], F32)
        self.QmlaT = sb("QmlaT", [128, 6, TB], BF16)
        self.QgqaT = sb("QgqaT", [128, 3, TB], BF16)
        self.mixA = sb("mixA", [64, 12, TB], BF16, nres=12)
        self.PTb = [self.qh, self.kh, self.qt]
        self.bcs = B(self.hq32.t[0:64, 0, :], self.hq32.r)
        self.rc = self.sq
        self.zq = sb("zq", [128, 640], F32)
        self.qstage = sb("qstage", [128, 6, 96], BF16)
        self.gqb = sb("gqb", [128, 384], BF16)
        self.mqnT = sb("mqnT", [128, 2, 128], BF16)
        self.qpe = sb("qpe", [128, 6, 32], F32)
        self.actT = sb("actT", [128, NFC, TB], BF16)
        self.lnm, self.lnr, self.lnt = self.ht[0], self.ht[1], self.ht[2]
        self.gy = [self.ht[3], self.ht[4]]
        self.P = [self.ps("pb%d" % i) for i in range(6)]
        self.PT = [self.ps("pt%d" % i, BF16) for i in range(2)]
        print("[kernel] SBUF bytes/partition allocated:", self.sb_bytes, "remaining", self.nc.sbuf_bytes_remaining)

    def phase(self, name):
        if not hasattr(self, "phases"):
            self.phases = []
        self.phases.append((name, len(self.S.ops["pe"])))

    def next_ring(self):
        b = self.ring[self._ring_i % self.NW]
        self._ring_i += 1
        return b

    def wload(self, pieces):
        slot = self.next_ring()
        for (dst, src) in pieces:
            self.dma("pool", dst(slot.t), src, writes=[slot.r], key=slot.r.name)
        return slot

    def init_consts(self):
        S = self.S
        pi = math.pi
        for t in (self.ident, self.identf):
            self.memset(t.t[:], 1.0, [t.r])
            self.op("pool", lambda e, t=t: e.affine_select(out=t.t[:], in_=t.t[:], compare_op=ALU.is_equal, fill=0.0,
                                                           base=0, pattern=[[-1, 128]], channel_multiplier=1),
                    [t.r], [t.r])
        self.memset(self.onesf.t[:], 1.0, [self.onesf.r])
        self.memset(self.onesblk.t[:], 0.0, [self.onesblk.r])
        self.memset(self.onesblk.t[0:64, 0:64], 1.0, [self.onesblk.r], reads=[self.onesblk.r])
        self.memset(self.onesblk.t[64:128, 64:128], 1.0, [self.onesblk.r], reads=[self.onesblk.r])
        self.memset(self.segmask.t[:], 1.0, [self.segmask.r])
        self.memset(self.segmask.t[:].rearrange("p (c t) -> p c t", t=32)[:, :, 0:1], 0.0, [self.segmask.r],
                    reads=[self.segmask.r])
        m = self.mAf
        self.memset(m.t[:], 1.0, [m.r])
        self.op("pool", lambda e: e.affine_select(out=m.t[:], in_=m.t[:], compare_op=ALU.is_ge, fill=0.0,
                                                  base=0, pattern=[[1, 128]], channel_multiplier=-1), [m.r], [m.r])
        self.op("pool", lambda e: e.affine_select(out=m.t[:], in_=m.t[:], compare_op=ALU.is_ge, fill=0.0,
                                                  base=0, pattern=[[-32, 4], [0, 32]], channel_multiplier=1), [m.r], [m.r])
        m2 = self.mAb
        self.memset(m2.t[:], 1.0, [m2.r])
        self.op("pool", lambda e: e.affine_select(out=m2.t[:], in_=m2.t[:], compare_op=ALU.is_ge, fill=0.0,
                                                  base=0, pattern=[[-1, 128]], channel_multiplier=1), [m2.r], [m2.r])
        self.op("pool", lambda e: e.affine_select(out=m2.t[:], in_=m2.t[:], compare_op=ALU.is_ge, fill=0.0,
                                                  base=31, pattern=[[32, 4], [0, 32]], channel_multiplier=-1), [m2.r], [m2.r])
        cm = self.cmask4
        self.memset(cm.t[:], 1.0, [cm.r])
        self.op("pool", lambda e: e.affine_select(out=cm.t[:], in_=cm.t[:], compare_op=ALU.is_ge, fill=0.0,
                                                  base=0, pattern=[[-32, 4]], channel_multiplier=1), [cm.r], [cm.r])
        self.op("pool", lambda e: e.affine_select(out=cm.t[:], in_=cm.t[:], compare_op=ALU.is_ge, fill=0.0,
                                                  base=31, pattern=[[32, 4]], channel_multiplier=-1), [cm.r], [cm.r])
        self.memset(self.Vmla.t[:, :, :, 64:65], 1.0, [self.Vmla.r])
        self.memset(self.Vgqa.t[:, :, :, 64:65], 1.0, [self.Vgqa.r])
        pc = self.pcs
        self.op("pool", lambda e: e.iota(pc.t[:, 0:1], pattern=[[0, 1]], base=0, channel_multiplier=1,
                                         allow_small_or_imprecise_dtypes=True), [], [pc.r])
        self.ts(pc.t[:, 2:3], pc.t[:, 0:1], 64.0, None, ALU.is_ge, None, [pc.r], [pc.r])
        self.stt(pc.t[:, 1:2], pc.t[:, 2:3], -64.0, pc.t[:, 0:1], ALU.mult, ALU.add, [pc.r], [pc.r])
        self.dma("sp", self.gidx.t[:], self.gidx_d, writes=[self.gidx.r], key="gidx")
        if self.do_sample:
            self.init_rope()
            for blk in range(4):
                self.dma("sp", self.rp2[blk * 128:(blk + 1) * 128, :],
                         self.rope.t[:, 4 * blk:4 * blk + 4].rearrange("p a b c d -> p (a b c d)"),
                         reads=[self.rope.r], writes=[self.rp2r], key="rope")

    def init_rope(self):
        pi = math.pi
        pc = self.pcs
        ang = self.ht[0]
        a = self.ht[0].t[:, 0:384].rearrange("p (i f) -> p i f", f=24)
        c = self.ht[1].t[:, 0:24]
        fr = self.ht[2].t[:, 0:24]
        rv = self.ht[2].t[:, 32:48]
        r0, r1, r2 = self.ht[0].r, self.ht[1].r, self.ht[2].r
        for j in range(8):
            self.memset(fr[:, j:j + 1], THETA ** (-j / 8.0), [r2], eng="dve", reads=[r2])
        for j in range(16):
            self.memset(fr[:, 8 + j:9 + j], THETA ** (-j / 16.0), [r2], eng="dve", reads=[r2])
        for i in range(16):
            self.memset(rv[:, i:i + 1], 2.0 * i, [r2], eng="dve", reads=[r2])
        self.ts(rv, rv, pc.t[:, 2:3], None, ALU.add, None, [r2, pc.r], [r2])
        self.tt(a, rv[:, :, None].broadcast_to([128, 16, 24]), fr[:, None, :].broadcast_to([128, 16, 24]), ALU.mult, [r2], [r0])
        self.ts(c, fr, pc.t[:, 1:2], None, ALU.mult, None, [r2, pc.r], [r1])
        rp = self.rope
        r3, r4 = self.ht[3].r, self.ht[4].r

        def reduce_sin(dst, src_ap, src_res, shape_n, shift, view):
            y = view(self.ht[3].t[:, 0:shape_n])
            yi = view(self.ht[4].t[:, 0:shape_n].bitcast(I32))
            yf = view(self.ht[4].t[:, 384:384 + shape_n]) if shape_n <= 128 else None
            self.ts(y, src_ap, 1.0 / (2 * pi), shift, ALU.mult, ALU.add, [src_res], [r3])
            self.cp(yi, y, [r3], [r4])
            y2 = view(self.hq32.t[:, 0, 0:shape_n])
            self.cp(y2, yi, [r4], [self.hq32.r])
            self.tt(y, y, y2, ALU.subtract, [r3, self.hq32.r], [r3])
            m = view(self.hq32.t[:, 1, 0:shape_n])
            self.ts(m, y, 0.5, None, ALU.is_gt, None, [r3], [self.hq32.r])
            self.tt(y, y, m, ALU.subtract, [r3, self.hq32.r], [r3])
            self.ts(m, y, -0.5, None, ALU.is_lt, None, [r3], [self.hq32.r])
            self.tt(y, y, m, ALU.add, [r3, self.hq32.r], [r3])
            self.act(dst, y, AF.Sin, [r3], [rp.r], scale=2 * pi)

        v3 = lambda ap: ap.rearrange("p (i f) -> p i f", f=24)
        v2 = lambda ap: ap
        ctmp = self.sq.t[:, 0:48]
        for which, shift in ((0, 0.25), (1, 0.0)):
            reduce_sin(rp.t[:, :, which, 0, :], a, r0, 384, shift, v3)
            y = self.ht[3].t[:, 0:24]
            cdst = ctmp[:, which * 24:(which + 1) * 24]
            old_rp = rp
            def _col(dst=cdst, shift=shift):
                yv = self.ht[3].t[:, 0:24]
                yi = self.ht[4].t[:, 0:24].bitcast(I32)
                y2 = self.hq32.t[:, 0, 0:24]
                m = self.hq32.t[:, 1, 0:24]
                self.ts(yv, c, 1.0 / (2 * pi), shift, ALU.mult, ALU.add, [r1], [r3])
                self.cp(yi, yv, [r3], [r4])
                self.cp(y2, yi, [r4], [self.hq32.r])
                self.tt(yv, yv, y2, ALU.subtract, [r3, self.hq32.r], [r3])
                self.ts(m, yv, 0.5, None, ALU.is_gt, None, [r3], [self.hq32.r])
                self.tt(yv, yv, m, ALU.subtract, [r3, self.hq32.r], [r3])
                self.ts(m, yv, -0.5, None, ALU.is_lt, None, [r3], [self.hq32.r])
                self.tt(yv, yv, m, ALU.add, [r3, self.hq32.r], [r3])
                self.act(dst, yv, AF.Sin, [r3], [self.sq.r], scale=2 * pi)
            _col()
            self.cp(rp.t[:, :, which, 1, :], cdst[:, None, :].broadcast_to([128, 16, 24]), [self.sq.r], [rp.r])

    def load_params(self):
        st = self.stg[0]
        rows = []
        r = 0
        self.par_off = {}

        def put(name, ap2d, n):
            nonlocal r
            self.dma("sp", st.t[r:r + n, 0:128], ap2d, writes=[st.r], key="stg")
            self.par_off[name] = r
            r += n
        put("cvec", self.cvec.rearrange("g (k p) -> (g k) p", p=128), 16)
        put("ln1_g", self.ln1_g.rearrange("l (k p) -> (l k) p", p=128), 16)
        put("ln1_b", self.ln1_b.rearrange("l (k p) -> (l k) p", p=128), 16)
        put("ln2_g", self.ln2_g.rearrange("l (k p) -> (l k) p", p=128), 16)
        put("ln2_b", self.ln2_b.rearrange("l (k p) -> (l k) p", p=128), 16)
        put("lb", self.hg_lb.rearrange("l d (c p) -> (l d c) p", p=128), 8)
        n1 = r
        pt = self.P[0]
        self.tp(pt.t[:, 0:n1], st.t[0:n1, 0:128], self.identf.t[0:n1, 0:n1], [st.r, self.identf.r], [pt.r])
        self.cp(self.par.t[:, 0:n1], pt.t[:, 0:n1], [pt.r], [self.par.r])
        st2r = self.sq
        self.dma("sp", st2r.t[0:96, 0:128], self.b_ada.rearrange("l (j p) -> (l j) p", p=128), writes=[st2r.r], key="sq")
        pt2 = self.P[1]
        self.tp(pt2.t[:, 0:96], st2r.t[0:96, 0:128], self.identf.t[0:96, 0:96], [st2r.r, self.identf.r], [pt2.r])
        self.par_off["b_ada"] = 128
        self.cp(self.par.t[:, 128:224], pt2.t[:, 0:96], [pt2.r], [self.par.r])
        for l in range(2):
            for h in range(2):
                self.dma("sp", self.hgn.t[64 * h:64 * h + 64, l:l + 1], self.hg_norm[l:l + 1, :].rearrange("o d -> d o"),
                         writes=[self.hgn.r], key="hgn", nc_ok=True)
        o = self.par_off["lb"]
        lb0 = self.par.t[:, o:o + 4]
        lb1 = self.par.t[:, o + 4:o + 8]
        sm = self.small = getattr(self, "small", None) or self.sb("small", [128, 32], F32)
        s = sm.t
        self.tt(s[:, 0:4], lb0, lb1, ALU.max, [self.par.r], [sm.r])
        self.tt(s[:, 4:8], lb0, s[:, 0:4], ALU.subtract, [self.par.r, sm.r], [sm.r])
        self.tt(s[:, 8:12], lb1, s[:, 0:4], ALU.subtract, [self.par.r, sm.r], [sm.r])
        self.act(s[:, 4:12], s[:, 4:12], AF.Exp, [sm.r], [sm.r])
        self.tt(s[:, 12:16], s[:, 4:8], s[:, 8:12], ALU.add, [sm.r], [sm.r])
        self.recip(s[:, 12:16], s[:, 12:16], [sm.r], [sm.r])
        self.tt(s[:, 16:20], s[:, 8:12], s[:, 12:16], ALU.mult, [sm.r], [sm.r])
        lbp = self.lbp
        self.memset(lbp.t[:, 0, :, :, 0:1], 0.0, [lbp.r], eng="dve")
        self.cp(lbp.t[:, 1, :, :, 0:1], s[:, 16:20].rearrange("p (d c o) -> p d c o", d=2, c=2), [sm.r], [lbp.r])
        self.ts(lbp.t[:, :, :, :, 1:2], lbp.t[:, :, :, :, 0:1], -1.0, 1.0, ALU.mult, ALU.add, [lbp.r], [lbp.r])
        self.ts(lbp.t[:, :, :, :, 2:3], lbp.t[:, :, :, :, 1:2], -1.0, None, ALU.mult, None, [lbp.r], [lbp.r])

    def pcol(self, name, l, k=None):
        o = self.par_off[name] + l * 8
        if k is None:
            return self.par.t[:, o:o + 8]
        return self.par.t[:, o + k:o + k + 1]

    def ada(self):
        o = self.par_off["cvec"]
        cv = self.par.t[:, o:o + 16]
        sm = self.small
        s = sm.t
        self.act(s[:, 0:16], cv, AF.Exp, [self.par.r], [sm.r], scale=-1.0)
        self.ts(s[:, 0:16], s[:, 0:16], 1.0, None, ALU.add, None, [sm.r], [sm.r])
        self.recip(s[:, 0:16], s[:, 0:16], [sm.r], [sm.r])
        self.tt(s[:, 0:16], s[:, 0:16], cv, ALU.mult, [sm.r, self.par.r], [sm.r])
        scT = self.kst
        scv = scT.t[:, 0, 0:16].rearrange("p (k g) -> p k g", g=2)
        self.cp(scv, s[:, 0:16].rearrange("p (g k) -> p k g", g=2), [sm.r], [scT.r])
        bo = self.par_off["b_ada"]
        for l in range(2):
            for grp in range(12):
                slot = self.wload([(lambda t: t[:, 0:4096].rearrange("p (k n) -> p k n", k=8),
                                    self.w_ada[l][:, grp * 512:(grp + 1) * 512].rearrange("(k p) n -> p k n", p=128))])
                wv = slot.t[:, 0:4096].rearrange("p (k n) -> p k n", k=8)
                pb = self.P[grp % 2]
                for jj in range(4):
                    for k in range(8):
                        self.mm(pb.t[:, jj * 2:jj * 2 + 2], wv[:, k, jj * 128:(jj + 1) * 128], scv[:, k, :],
                                k == 0, k == 7, [slot.r, scT.r], [pb.r])
                j0 = grp * 4
                self.tt(self.mT.t[:, l, j0:j0 + 4, :], pb.t[:, 0:8].rearrange("p (j g) -> p j g", g=2),
                        self.par.t[:, bo + l * 48 + j0: bo + l * 48 + j0 + 4][:, :, None].broadcast_to([128, 4, 2]),
                        ALU.add, [pb.r, self.par.r], [self.mT.r])
        for l in range(2):
            for g in range(2):
                self.ts(self.mP.t[:, l, g, 0, :], self.mT.t[:, l, 8:16, g], 1.0, None, ALU.add, None, [self.mT.r], [self.mP.r])
                self.ts(self.mP.t[:, l, g, 1, :], self.mT.t[:, l, 32:40, g], 1.0, None, ALU.add, None, [self.mT.r], [self.mP.r])

    def mvec(self, l, g, which, k):
        if which == "sc1p":
            return self.mP.t[:, l, g, 0, k:k + 1]
        if which == "sc2p":
            return self.mP.t[:, l, g, 1, k:k + 1]
        base = {"sh1": 0, "g1": 16, "sh2": 24, "g2": 40}[which]
        return self.mT.t[:, l, base + k, g:g + 1]

    def load_layer_small(self, l):
        bc = self.bc
        self.dma("sp", bc.t[:, 0:128], self.mla_kv_norm[l:l + 1, :].partition_broadcast(128), writes=[bc.r], key="bc")
        self.dma("sp", bc.t[:, 128:384], self.mla_q_norm[l:l + 1, :].partition_broadcast(128), writes=[bc.r], key="bc")
        self.dma("sp", bc.t[:, 384:448], self.gqa_q_norm[l:l + 1, :].partition_broadcast(128), writes=[bc.r], key="bc")
        self.dma("sp", bc.t[:, 448:512], self.gqa_k_norm[l:l + 1, :].partition_broadcast(128), writes=[bc.r], key="bc")
        w = self.wsm
        self.dma("pool", w.t[:, 0:1152].rearrange("p (k n) -> p k n", k=2), self.w_uq[l].rearrange("(k p) n -> p k n", p=128),
                 writes=[w.r], key="wsm")
        self.dma("pool", w.t[:, 1152:1920], self.w_ukv[l], writes=[w.r], key="wsm")

    def dump(self, name, ap, res):
        if name not in self.dbg_out:
            return
        rs = res if isinstance(res, (list, tuple)) else [res]
        self.dma("sp", self.dbg_out[name], ap, reads=rs, key="dbg_" + name)

    def load_x_block(self, dram_rows_ap):
        st = self.stg[0]
        for t in range(4):
            self.dma("sp", st.t[:], dram_rows_ap[t * 128:(t + 1) * 128, :], writes=[st.r], key="stg")
            for half in range(2):
                pb = self.P[(2 * t + half) % 6]
                for kk in range(4):
                    k = half * 4 + kk
                    self.tp(pb.t[:, kk * 128:(kk + 1) * 128], st.t[:, k * 128:(k + 1) * 128], self.identf.t[:],
                            [st.r, self.identf.r], [pb.r])
                self.cp(self.xb.t[:, half * 4:half * 4 + 4, t * 128:(t + 1) * 128],
                        pb.t[:].rearrange("p (k n) -> p k n", k=4), [pb.r], [self.xb.r],
                        eng=("act" if half else "dve"))

    def store_y_block(self, dram_rows_ap):
        st = self.stg[0]
        for t in range(4):
            for half in range(2):
                pb = self.P[(2 * t + half) % 6]
                for kk in range(4):
                    k = half * 4 + kk
                    self.tp(pb.t[:, kk * 128:(kk + 1) * 128], self.xb.t[:, k, t * 128:(t + 1) * 128], self.identf.t[:],
                            [self.xb.r, self.identf.r], [pb.r])
                self.cp(st.t[:, half * 512:(half + 1) * 512], pb.t[:], [pb.r], [st.r], eng=("act" if half else "dve"))
            self.dma("sp", dram_rows_ap[t * 128:(t + 1) * 128, :], st.t[:], reads=[st.r], key="stg")

    def xs_dram_view(self, blk):
        return self.xs_dram[blk * 128:(blk + 1) * 128, :].rearrange("p (k n) -> p k n", k=KC)

    def gather(self, out_ap, dram_t, reads, writes, key):
        idx = self.gidx
        f = lambda e: e.indirect_dma_start(out=out_ap, out_offset=None, in_=dram_t,
                                           in_offset=bass.IndirectOffsetOnAxis(ap=idx.t[:, 0:1], axis=0))
        return self.S.add("pool", f, self._flat(list(reads) + [idx.r]), self._flat(writes), dma=True, key=key)

    def load_xs_block(self, blk):
        self.dma("sp", self.xb.t[:], self.xs_dram_view(blk), reads=[self.xsr[blk]], writes=[self.xb.r], key="xb")

    def store_xs_block(self, blk):
        self.dma("sp", self.xs_dram_view(blk), self.xb.t[:], reads=[self.xb.r], writes=[self.xsr[blk]], key="xb")

    def prepH(self, l, g, which="1"):
        scn, shn = ("sc1p", "sh1") if which == "1" else ("sc2p", "sh2")
        for k in range(KC):
            if k % 2 == 0:
                self.ts(self.hT.t[:, k, :], self.xb.t[:, k, :], self.mvec(l, g, scn, k), self.mvec(l, g, shn, k),
                        ALU.mult, ALU.add, [self.xb.r[k], self.mT.r, self.mP.r], [self.hT.r[k]])
            else:
                self.act(self.hT.t[:, k, :], self.xb.t[:, k, :], AF.Identity, [self.xb.r[k], self.mT.r, self.mP.r], [self.hT.r[k]],
                         scale=self.mvec(l, g, scn, k), bias=self.mvec(l, g, shn, k))

    def win_slot(self, l, pieces):
        tot = sum(n for _, n in pieces)
        assert tot * 8 <= 4096
        lst = []
        off = 0
        for (c0, n) in pieces:
            lst.append((lambda t, off=off, n=n, tot=tot: t[:, 0:8 * tot].rearrange("p (k n) -> p k n", k=8)[:, :, off:off + n],
                        self.w_in[l][:, c0:c0 + n].rearrange("(k p) n -> p k n", p=128)))
            off += n
        slot = self.wload(lst)
        return slot, slot.t[:, 0:8 * tot].rearrange("p (k n) -> p k n", k=8)

    def rope_tm(self, out, x, tile_idx, f0, nf, nheads, reads, writes):
        rp = self.rope
        shp = [128, nheads, 2, nf]
        cos = rp.t[:, tile_idx, 0, :, f0:f0 + nf][:, None, :, :].broadcast_to(shp)
        sin = rp.t[:, tile_idx, 1, :, f0:f0 + nf][:, None, :, :].broadcast_to(shp)
        x1 = x[:, :, :, 0, :]
        x2 = x[:, :, :, 1, :]
        t1 = self.sq.t[:, 0:nheads * 2 * nf].rearrange("p (h a f) -> p h a f", h=nheads, a=2)
        t2 = self.sq.t[:, 256:256 + nheads * 2 * nf].rearrange("p (h a f) -> p h a f", h=nheads, a=2)
        rs = list(reads) + [rp.r]
        self.tt(t1, x1, cos, ALU.mult, rs, [self.sq.r])
        self.tt(t2, x2, sin, ALU.mult, rs + [self.sq.r], [self.sq.r])
        self.tt(out[:, :, :, 0, :], t1, t2, ALU.subtract, [self.sq.r], writes)
        self.tt(t1, x1, sin, ALU.mult, rs + [self.sq.r], [self.sq.r])
        self.tt(t2, x2, cos, ALU.mult, rs + [self.sq.r], [self.sq.r])
        self.tt(out[:, :, :, 1, :], t1, t2, ALU.add, [self.sq.r], writes)

    def kv_finish(self, ntiles, kt0):
        w = self.wsm
        wukv = w.t[:, 1152:1920]
        n = ntiles * 128
        for h in range(6):
            pb = self.P[h % 2]
            self.mm(pb.t[0:64, 0:n], wukv[:, h * 128:h * 128 + 64], self.ckvT.t[:, 0:n], True, True, [w.r, self.ckvT.r], [pb.r])
            self.cp(self.KmlaT.t[0:64, h, kt0 * 128:kt0 * 128 + n], pb.t[0:64, 0:n], [pb.r], [self.KmlaT.r],
                    eng=("act" if h % 2 else "dve"))
        wv = wukv.rearrange("p (h x) -> p h x", h=6)[:, :, 64:128]
        for t in range(ntiles):
            pb = self.P[2 + t % 2]
            self.mm(pb.t[:, 0:384].rearrange("p (h x) -> p h x", h=6), self.ckvT.t[:, t * 128:(t + 1) * 128], wv, True, True,
                    [w.r, self.ckvT.r], [pb.r])
            self.cp(self.Vmla.t[:, kt0 + t, :, 0:64], pb.t[:, 0:384].rearrange("p (h x) -> p h x", h=6), [pb.r], [self.Vmla.r],
                    eng=("act" if t % 2 else "dve"))

    def kv_transposes(self, t, kt):
        k = self.kst
        pt = self.PT[t % 2]
        self.tp(pt.t[:, 0:128], k.t[:, 0, :], self.ident.t[:], [k.r, self.ident.r], [pt.r])
        self.tp(pt.t[:, 128:256], k.t[:, 1, :], self.ident.t[:], [k.r, self.ident.r], [pt.r])
        self.tp(pt.t[:, 256:384], k.t[:, 2, :], self.ident.t[:], [k.r, self.ident.r], [pt.r])
        self.cp(self.ckvT.t[:, t * 128:(t + 1) * 128], pt.t[:, 0:128], [pt.r], [self.ckvT.r], eng="act")
        self.cp(self.KmlaT.t[64:96, :, kt * 128:(kt + 1) * 128], pt.t[64:96, 128:256][:, None, :].broadcast_to([32, 6, 128]),
                [pt.r], [self.KmlaT.r], eng="dve")
        self.cp(self.KgqaT.t[:, kt * 128:(kt + 1) * 128], pt.t[:, 256:384], [pt.r], [self.KgqaT.r], eng="act")

    def kv_pass(self, l, ntiles, kt0, rope_tile0=None, ctx_out=None):
        slot, wv = self.win_slot(l, [(O_MKV, 160), (O_GK, 256)])
        bc = self.bc
        for t in range(ntiles):
            pb = self.P[4 + t % 2]
            for k in range(KC):
                self.mm(pb.t[:, 0:416], self.hT.t[:, k, t * 128:(t + 1) * 128], wv[:, k, :], k == 0, k == KC - 1,
                        [self.hT.r, slot.r], [pb.r])
            st = self.st8
            self.act(self.sq.t[:, 0:128], pb.t[:, 0:128], AF.Square, [pb.r], [self.sq.r, st.r], scale=128.0 ** -0.5,
                     accum_out=st.t[:, 0:1])
            for h in range(2):
                self.act(self.sq.t[:, 128:192], pb.t[:, 160 + 64 * h:224 + 64 * h], AF.Square, [pb.r], [self.sq.r, st.r],
                         scale=0.125, accum_out=st.t[:, 1 + h:2 + h])
            self.act(st.t[:, 0:3], st.t[:, 0:3], AF.Ln, [st.r], [st.r], bias=EPS)
            self.act(st.t[:, 0:3], st.t[:, 0:3], AF.Exp, [st.r], [st.r], scale=-0.5)
            k_ = self.kst
            zf = self.zkf
            cut = getattr(self, "cut", 99)
            if cut <= 1:
                continue
            if ctx_out is not None:
                self.stt(zf.t[:, 0:128], pb.t[:, 0:128], st.t[:, 0:1], bc.t[:, 0:128], ALU.mult, ALU.mult, [pb.r, st.r, bc.r], [zf.r])
                self.cp(zf.t[:, 128:160], pb.t[:, 128:160], [pb.r], [zf.r])
                for h in range(2):
                    self.stt(zf.t[:, 160 + 64 * h:224 + 64 * h], pb.t[:, 160 + 64 * h:224 + 64 * h], st.t[:, 1 + h:2 + h],
                             bc.t[:, 448:512], ALU.mult, ALU.mult, [pb.r, st.r, bc.r], [zf.r])
                self.cp(zf.t[:, 288:416], pb.t[:, 288:416], [pb.r], [zf.r], eng="act")
                self.cp(k_.t[:, 0, :], zf.t[:, 0:128], [zf.r], [k_.r], eng="act")
                self.cp(k_.t[:, 1, 64:96], zf.t[:, 128:160], [zf.r], [k_.r])
                self.cp(k_.t[:, 2, :], zf.t[:, 160:288], [zf.r], [k_.r], eng="act")
                seq = ctx_out + t // 2
                r0 = (t % 2) * 128
                self.dma("sp", self.o_ckv[seq, l, r0:r0 + 128, :], zf.t[:, 0:128], reads=[zf.r], key="zkf")
                self.dma("sp", self.o_kpe[seq, l, r0:r0 + 128, :], zf.t[:, 128:160], reads=[zf.r], key="zkf")
                self.dma("sp", self.o_gk[seq, l, r0:r0 + 128, :], zf.t[:, 160:288], reads=[zf.r], key="zkf")
                self.dma("sp", self.o_gv[seq, l, r0:r0 + 128, :], zf.t[:, 288:416], reads=[zf.r], key="zkf")
            else:
                ti = rope_tile0 + t
                self.stt(k_.t[:, 0, :], pb.t[:, 0:128], st.t[:, 0:1], bc.t[:, 0:128], ALU.mult, ALU.mult, [pb.r, st.r, bc.r], [k_.r])
                xin = pb.t[:, 128:160].rearrange("p (h a b f) -> p h a b f", h=1, a=2, b=2)
                xo = k_.t[:, 1, 64:96].rearrange("p (h a b f) -> p h a b f", h=1, a=2, b=2)
                self.rope_tm(xo, xin, ti, 0, 8, 1, [pb.r], [k_.r])
                g = self.gkf
                for h in range(2):
                    self.stt(g.t[:, 64 * h:64 * h + 64], pb.t[:, 160 + 64 * h:224 + 64 * h], st.t[:, 1 + h:2 + h],
                             bc.t[:, 448:512], ALU.mult, ALU.mult, [pb.r, st.r, bc.r], [g.r])
                xin = g.t[:].rearrange("p (h a b f) -> p h a b f", h=2, a=2, b=2)
                xo = k_.t[:, 2, :].rearrange("p (h a b f) -> p h a b f", h=2, a=2, b=2)
                self.rope_tm(xo, xin, ti, 8, 16, 2, [g.r], [k_.r])
            if cut <= 2:
                continue
            self.cp(self.Vgqa.t[:, kt0 + t, :, 0:64], pb.t[:, 288:416].rearrange("p (h x) -> p h x", h=2), [pb.r], [self.Vgqa.r],
                    eng="act")
            if cut <= 3:
                continue
            self.kv_transposes(t, kt0 + t)
        if getattr(self, "cut", 99) <= 4:
            return
        self.kv_finish(ntiles, kt0)

    def cache_kv(self, l):
        k_ = self.kst
        for t in range(2):
            r0 = t * 128
            self.dma("pool", k_.t[:, 0, :], self.c_ckv[l, r0:r0 + 128, :], writes=[k_.r], key="kst")
            self.dma("pool", k_.t[:, 1, 64:96], self.c_kpe[l, r0:r0 + 128, :], writes=[k_.r], key="kst")
            self.dma("pool", k_.t[:, 2, :], self.c_gk[l, r0:r0 + 128, :], writes=[k_.r], key="kst")
            self.dma("pool", self.Vgqa.t[:, t, :, 0:64], self.c_gv[l, r0:r0 + 128, :].rearrange("p (h x) -> p h x", h=2),
                     writes=[self.Vgqa.r], key="Vgqa")
            self.kv_transposes(t, t)
        self.kv_finish(2, 0)

    def hgrn_pass(self, l, d, mcol0, segments, ctx_seq0=None):
        mres = self.mixHG.r[mcol0 // TB]
        mcols = slice(mcol0, mcol0 + TB)
        hf_off = O_HFB if d else O_HFF
        slotA, wA = self.win_slot(l, [(O_HQ, 256), (hf_off, 256)])
        slotB, wB = self.win_slot(l, [(O_HI, 512 if d else 256)])
        ht = self.ht
        P = self.P
        for t in range(4):
            pb = P[2] if t % 2 == 0 else P[5]
            for k in range(KC):
                self.mm(pb.t[:, 0:256], self.hT.t[:, k, t * 128:(t + 1) * 128], wB[:, k, 0:256], k == 0, k == KC - 1,
                        [self.hT.r, slotB.r], [pb.r])
            self.cp(self.itm.t[:, t, :], pb.t[:, 0:256], [pb.r], [self.itm.r], eng=("act" if t % 2 else "dve"))
        mask = self.mAb if d else self.mAf
        for c in range(2):
            lb = self.lbp.t[:, l, d, c, 0:1]
            om = self.lbp.t[:, l, d, c, 1:2]
            vr32 = self.V32.r[d * 2 + c]
            pq, pf = P[0], P[1]
            for k in range(KC):
                self.mm(pq.t[:, :], wA[:, k, c * 128:(c + 1) * 128], self.hT.t[:, k, :], k == 0, k == KC - 1, [slotA.r, self.hT.r], [pq.r])
            for k in range(KC):
                self.mm(pf.t[:, :], wA[:, k, 256 + c * 128:256 + (c + 1) * 128], self.hT.t[:, k, :], k == 0, k == KC - 1,
                        [slotA.r, self.hT.r], [pf.r])
            h0, h1, h2, h3, h4 = [x.t for x in ht]
            r0, r1, r2, r3, r4 = [x.r for x in ht]
            q = self.hq32.t[:, c, :]
            rq = self.hq32.r
            self.act(h0[:], pq.t[:], AF.Exp, [pq.r], [r0], scale=-1.0)
            self.ts(h0[:], h0[:], 1.0, None, ALU.add, None, [r0], [r0])
            self.recip(h0[:], h0[:], [r0], [r0])
            self.stt(q, pq.t[:], 0.125, h0[:], ALU.mult, ALU.mult, [pq.r, r0], [rq])
            self.act(h1[:], pf.t[:], AF.Exp, [pf.r], [r1], scale=-1.0)
            self.ts(h1[:], h1[:], 1.0, None, ALU.add, None, [r1], [r1])
            self.recip(h1[:], h1[:], [r1], [r1])
            self.ts(h2[:], h1[:], om, lb, ALU.mult, ALU.add, [r1, self.lbp.r], [r2])
            self.act(h3[:], h2[:], AF.Identity, [r2], [r3], scale=-1.0, bias=1.0)
            self.act(h2[:], h2[:], AF.Ln, [r2], [r2], bias=1e-30)
            self.op("dve", lambda e: e.tensor_tensor_scan(out=h4[:], data0=self.segmask.t[:], data1=h2[:], initial=0.0,
                                                          op0=ALU.mult, op1=ALU.add), [self.segmask.r, r2], [r4])
            v3 = lambda ap: ap.rearrange("p (c t) -> p c t", t=32)
            if d == 0:
                bT, rb = h4, r4
            else:
                self.tt(h1[:], h2[:], h4[:], ALU.subtract, [r2, r4], [r1])
                self.tt(v3(h2[:]), v3(h1[:]), v3(h4[:])[:, :, 31:32].broadcast_to([128, 16, 32]), ALU.add, [r1, r4], [r2])
                bT, rb = h2, r2
            mid = v3(bT[:])[:, :, 16:17]
            bend = v3(bT[:])[:, :, 31:32] if d == 0 else v3(bT[:])[:, :, 0:1]
            hs = self.hsm
            self.act(hs.t[:, 0, :], mid[:, :, 0], AF.Exp, [rb], [hs.r])
            self.tt(hs.t[:, 1, :], bend[:, :, 0], mid[:, :, 0], ALU.subtract, [rb], [hs.r])
            self.act(hs.t[:, 1, :], hs.t[:, 1, :], AF.Exp, [hs.r], [hs.r])
            self.act(hs.t[:, 2, :], bend[:, :, 0], AF.Exp, [rb], [hs.r])
            self.tt(v3(h1[:]), v3(bT[:]), mid.broadcast_to([128, 16, 32]), ALU.subtract, [rb], [r1])
            self.act(h0[:], h1[:], AF.Exp, [r1], [r0])
            self.act(h1[:], h1[:], AF.Exp, [r1], [r1], scale=-1.0)
            self.tt(self.qh.t[:], q, h0[:], ALU.mult, [rq, r0], [self.qh.r])
            self.tt(self.kh.t[:], h3[:], h1[:], ALU.mult, [r3, r1], [self.kh.r])
            self.tt(v3(h0[:]), v3(h0[:]), hs.t[:, 0, :][:, :, None].broadcast_to([128, 16, 32]), ALU.mult, [r0, hs.r], [r0])
            self.tt(self.qt.t[:], q, h0[:], ALU.mult, [rq, r0], [self.qt.r])
            self.tt(v3(h1[:]), v3(h1[:]), hs.t[:, 1, :][:, :, None].broadcast_to([128, 16, 32]), ALU.mult, [r1, hs.r], [r1])
            self.tt(self.kd.t[:], h3[:], h1[:], ALU.mult, [r3, r1], [self.kd.r])
            for t in range(4):
                pt = self.PT[t % 2]
                self.tp(pt.t[:, 0:128], self.kd.t[:, t * 128:(t + 1) * 128], self.ident.t[:], [self.kd.r, self.ident.r], [pt.r])
                self.cp(self.kdtm.t[:, t, :], pt.t[:, 0:128], [pt.r], [self.kdtm.r], eng=("act" if t % 2 else "dve"))
            psA, psO = [P[2], P[0]], [P[3], P[4]]
            psTl = [(P[5].t, P[5].r), (self.PT[1].t[:].bitcast(F32), self.PT[1].r)]
            NV = self.NV
            step = 0
            tcount = 0
            for (order, reset, out_seq) in segments:
                s0 = step % NV
                if reset:
                    self.memset(self.Vr32.t[:, s0, :], 0.0, [self.Vr32.r[s0]], eng="dve")
                    self.memset(self.Vrb.t[:, s0, :], 0.0, [self.Vrb.r[s0]], eng="dve")
                else:
                    self.cp(self.Vr32.t[:, s0, :], self.V32.t[:, d, c, :], [vr32], [self.Vr32.r[s0]])
                    self.cp(self.Vrb.t[:, s0, :], self.V32.t[:, d, c, :], [vr32], [self.Vrb.r[s0]], eng="act")
                for t in order:
                    tc = slice(t * 128, (t + 1) * 128)
                    psT, psTr = psTl[tcount % 2]
                    tcount += 1
                    for h in range(2):
                        hp = slice(64 * h, 64 * h + 64)
                        self.mm(psA[h].t[:, 0:128], self.kh.t[hp, tc], self.qh.t[hp, tc], True, True,
                                [self.kh.r, self.qh.r], [psA[h].r])
                        self.tt(self.ATm.t[:, h, :], psA[h].t[:, 0:128], mask.t[:], ALU.mult, [psA[h].r, mask.r], [self.ATm.r])
                    for h in range(2):
                        hp = slice(64 * h, 64 * h + 64)
                        self.mm(psO[h].t[hp, tc], self.itm.t[:, t, c * 128 + 64 * h:c * 128 + 64 * h + 64], self.ATm.t[:, h, :],
                                True, False, [self.itm.r, self.ATm.r], [psO[h].r], tile_position=(0, 64 * h))
                    jorder = list(range(4) if d == 0 else range(3, -1, -1))
                    self.tt(self.iexp.t[:], self.itm.t[:, t, c * 128:(c + 1) * 128][:, None, :].broadcast_to([128, 4, 128]),
                            self.cmask4.t[:][:, :, None].broadcast_to([128, 4, 128]), ALU.mult, [self.itm.r, self.cmask4.r], [self.iexp.r])
                    for h in range(2):
                        hp = slice(64 * h, 64 * h + 64)
                        self.mm(psT[hp, 0:256].rearrange("p (j x) -> p j x", j=4), self.kdtm.t[:, t, hp], self.iexp.t[:, :, hp],
                                True, True, [self.kdtm.r, self.iexp.r], [psTr], tile_position=(0, 64 * h))
                    for j in jorder:
                        cc = slice(t * 128 + 32 * j, t * 128 + 32 * j + 32)
                        sv, sn = step % NV, (step + 1) % NV
                        for h in range(2):
                            hp = slice(64 * h, 64 * h + 64)
                            self.mm(psO[h].t[hp, cc], self.Vrb.t[hp, sv, :], self.qt.t[hp, cc], False, j == jorder[-1],
                                    [self.Vrb.r[sv], self.qt.r], [psO[h].r], tile_position=(64 * h, 64 * h))
                        ci = t * 4 + j
                        self.stt(self.Vr32.t[:, sn, :], self.Vr32.t[:, sv, :], self.hsm.t[:, 2, ci:ci + 1], psT[:, j * 64:(j + 1) * 64],
                                 ALU.mult, ALU.add, [self.Vr32.r[sv], self.hsm.r, psTr], [self.Vr32.r[sn]])
                        self.cp(self.Vrb.t[:, sn, :], self.Vr32.t[:, sn, :], [self.Vr32.r[sn]], [self.Vrb.r[sn]], eng="act")
                        step += 1
                se = step % NV
                if out_seq is not None:
                    self.dma("sp", self.o_hg[out_seq, l, d, c * 128:(c + 1) * 128, :], self.Vr32.t[:, se, :], reads=[self.Vr32.r[se]],
                             key="Vr32_%d" % se)
                else:
                    self.cp(self.V32.t[:, d, c, :], self.Vr32.t[:, se, :], [self.Vr32.r[se]], [vr32])
            if d == 0:
                for h in range(2):
                    hp = slice(64 * h, 64 * h + 64)
                    self.cp(self.mixHG.t[hp, c, mcols], psO[h].t[hp, :], [psO[h].r], [mres], eng="act")
            else:
                pg = P[1]
                for k in range(KC):
                    self.mm(pg.t[:, :], wB[:, k, 256 + c * 128:256 + (c + 1) * 128], self.hT.t[:, k, :], k == 0, k == KC - 1,
                            [slotB.r, self.hT.r], [pg.r])
                for h in range(2):
                    hp = slice(64 * h, 64 * h + 64)
                    self.tt(h0[hp, :], psO[h].t[hp, :], self.mixHG.t[hp, c, mcols], ALU.add, [psO[h].r, mres], [r0])
                self.act(h1[:], h0[:], AF.Square, [r0], [r1])
                pss = P[2]
                self.mm(pss.t[:, :], self.onesblk.t[:], h1[:], True, True, [self.onesblk.r, r1], [pss.r])
                self.act(h2[:], pss.t[:], AF.Ln, [pss.r], [r2], scale=1.0 / 64.0, bias=EPS)
                self.act(h2[:], h2[:], AF.Exp, [r2], [r2], scale=-0.5)
                self.tt(h0[:], h0[:], h2[:], ALU.mult, [r0, r2], [r0])
                self.act(h3[:], pg.t[:], AF.Exp, [pg.r], [r3], scale=-1.0)
                self.ts(h3[:], h3[:], 1.0, None, ALU.add, None, [r3], [r3])
                self.recip(h3[:], h3[:], [r3], [r3])
                self.stt(h0[:], h0[:], self.hgn.t[:, l:l + 1], pg.t[:], ALU.mult, ALU.mult, [r0, self.hgn.r, pg.r], [r0])
                self.tt(self.mixHG.t[:, c, mcols], h0[:], h3[:], ALU.mult, [r0, r3], [mres])

    def load_hgrn_state(self, l):
        for d in range(2):
            for c in range(2):
                i = d * 2 + c
                self.dma("sp", self.V32.t[:, d, c, :], self.s_hg[l, d, c * 128:(c + 1) * 128, :], writes=[self.V32.r[i]],
                         key="V32_%d" % i)

    def q_pass(self, l, rope_tile0=None):
        slot1, w1 = self.win_slot(l, [(O_MQ, 256)])
        slot2, w2 = self.win_slot(l, [(O_GQ, 384)])
        bc = self.bc
        w = self.wsm
        wuq = w.t[:, 0:1152].rearrange("p (k n) -> p k n", k=2)
        P = self.P
        st = self.st8
        for t in range(4):
            tcs = slice(t * 128, (t + 1) * 128)
            p0, p1, p2, p3 = P[2 * (t % 2)], P[2 * (t % 2) + 1], P[4], P[5]
            for k in range(KC):
                self.mm(p0.t[:, 0:256], self.hT.t[:, k, tcs], w1[:, k, :], k == 0, k == KC - 1, [self.hT.r, slot1.r], [p0.r])
            for k in range(KC):
                self.mm(p1.t[:, 0:384], self.hT.t[:, k, tcs], w2[:, k, :], k == 0, k == KC - 1, [self.hT.r, slot2.r], [p1.r])
            self.act(self.sq.t[:, 0:256], p0.t[:, 0:256], AF.Square, [p0.r], [self.sq.r, st.r], scale=1.0 / 16.0,
                     accum_out=st.t[:, 0:1])
            self.act(st.t[:, 0:1], st.t[:, 0:1], AF.Ln, [st.r], [st.r], bias=EPS)
            self.act(self.sq.t[:, 0:384], p1.t[:, 0:384], AF.Square, [p1.r], [self.sq.r])
            self.op("dve", lambda e: e.tensor_reduce(out=st.t[:, 1:7], in_=self.sq.t[:, 0:384].rearrange("p (h x) -> p h x", h=6),
                                                     axis=AX.X, op=ALU.add), [self.sq.r], [st.r])
            self.act(st.t[:, 1:7], st.t[:, 1:7], AF.Ln, [st.r], [st.r], scale=1.0 / 64.0, bias=EPS)
            self.act(st.t[:, 0:7], st.t[:, 0:7], AF.Exp, [st.r], [st.r], scale=-0.5)
            k_ = self.kst
            mqn = k_.t[:, 0:2, :]
            self.stt(mqn, p0.t[:, 0:256].rearrange("p (a b) -> p a b", a=2), st.t[:, 0:1],
                     bc.t[:, 128:384].rearrange("p (a b) -> p a b", a=2), ALU.mult, ALU.mult, [p0.r, st.r, bc.r], [k_.r])
            pt0 = self.PT[0]
            self.tp(pt0.t[:, 0:128], k_.t[:, 0, :], self.ident.t[:], [k_.r, self.ident.r], [pt0.r])
            self.tp(pt0.t[:, 128:256], k_.t[:, 1, :], self.ident.t[:], [k_.r, self.ident.r], [pt0.r])
            self.cp(self.mqnT.t[:], pt0.t[:, 0:256].rearrange("p (a b) -> p a b", a=2), [pt0.r], [self.mqnT.r], eng="act")
            for kk in range(2):
                self.mm(p2.t[:, 0:480], self.mqnT.t[:, kk, :], wuq[:, kk, 0:480], kk == 0, kk == 1, [self.mqnT.r, w.r], [p2.r])
            for kk in range(2):
                self.mm(p3.t[:, 0:96], self.mqnT.t[:, kk, :], wuq[:, kk, 480:576], kk == 0, kk == 1, [self.mqnT.r, w.r], [p3.r])
            qs = self.qstage
            v5 = p2.t[:, 0:480].rearrange("p (h x) -> p h x", h=5)
            self.cp(qs.t[:, 0:5, 0:64], v5[:, :, 0:64], [p2.r], [qs.r], eng="act")
            self.cp(qs.t[:, 5, 0:64], p3.t[:, 0:64], [p3.r], [qs.r], eng="act")
            if rope_tile0 is None:
                self.cp(qs.t[:, 0:5, 64:96], v5[:, :, 64:96], [p2.r], [qs.r])
                self.cp(qs.t[:, 5, 64:96], p3.t[:, 64:96], [p3.r], [qs.r])
            else:
                qp = self.qpe
                self.cp(qp.t[:, 0:5, :], v5[:, :, 64:96], [p2.r], [qp.r])
                self.cp(qp.t[:, 5, :], p3.t[:, 64:96], [p3.r], [qp.r])
                self.rope_tm(qs.t[:, :, 64:96].rearrange("p h (a b f) -> p h a b f", a=2, b=2),
                             qp.t[:].rearrange("p h (a b f) -> p h a b f", a=2, b=2), rope_tile0 + t, 0, 8, 6, [qp.r], [qs.r])
            pt1 = self.PT[1]
            for h in range(6):
                self.tp(pt1.t[0:96, h * 128:(h + 1) * 128], qs.t[:, h, :], self.ident.t[:], [qs.r, self.ident.r], [pt1.r])
            self.cp(self.QmlaT.t[0:96, :, tcs], pt1.t[0:96, 0:768].rearrange("p (h x) -> p h x", h=6), [pt1.r], [self.QmlaT.r])
            z = self.zq
            zv = z.t[:, 0:384].rearrange("p (h x) -> p h x", h=6)
            self.tt(zv, p1.t[:, 0:384].rearrange("p (h x) -> p h x", h=6), st.t[:, 1:7][:, :, None].broadcast_to([128, 6, 64]),
                    ALU.mult, [p1.r, st.r], [z.r])
            gq = self.gqb
            gperm = gq.t[:].rearrange("p (j a x) -> p a j x", a=2, x=64)
            znat = z.t[:, 0:384].rearrange("p (a j x) -> p a j x", a=2, x=64)
            ggq4 = bc.t[:, 384:448][:, None, None, :].broadcast_to([128, 2, 3, 64])
            ggq = bc.t[:, 384:448][:, None, :].broadcast_to([128, 6, 64])
            if rope_tile0 is None:
                self.tt(gperm, znat, ggq4, ALU.mult, [z.r, bc.r], [gq.r])
            else:
                self.tt(zv, zv, ggq, ALU.mult, [z.r, bc.r], [z.r])
                self.rope_tm(k_.t[:].rearrange("p c (h2 a b f) -> p (c h2) a b f", h2=2, a=2, b=2),
                             z.t[:, 0:384].rearrange("p (h a b f) -> p h a b f", h=6, a=2, b=2), rope_tile0 + t, 8, 16, 6,
                             [z.r], [k_.r])
                self.cp(gperm, k_.t[:].rearrange("p c x -> p (c x)").rearrange("p (a j x) -> p a j x", a=2, x=64), [k_.r], [gq.r])
            for j in range(3):
                self.tp(pt0.t[:, 256 + j * 128:256 + (j + 1) * 128], gq.t[:, j * 128:(j + 1) * 128], self.ident.t[:],
                        [gq.r, self.ident.r], [pt0.r])
            self.cp(self.QgqaT.t[:, :, tcs], pt0.t[:, 256:640].rearrange("p (j x) -> p j x", j=3), [pt0.r], [self.QgqaT.r], eng="act")

    def attention(self, N, qcol0, keytiles):
        P = self.P
        qc = slice(qcol0, qcol0 + N)
        nk = len(keytiles)

        def head_ops(hh):
            if hh < 6:
                h = hh
                return (lambda kt: self.KmlaT.t[0:96, h, kt * 128:(kt + 1) * 128], self.QmlaT.t[0:96, h, qc],
                        lambda kt: self.Vmla.t[:, kt, h, :], self.KmlaT.r, self.QmlaT.r, self.Vmla.r, MLA_SCALE)
            g = hh - 6
            part, j = g // 3, g % 3
            pp = slice(64 * part, 64 * part + 64)
            return (lambda kt: self.KgqaT.t[pp, kt * 128:(kt + 1) * 128], self.QgqaT.t[pp, j, qc],
                    lambda kt: self.Vgqa.t[:, kt, part, :], self.KgqaT.r, self.QgqaT.r, self.Vgqa.r, GQA_SCALE)

        units = [(hh, i) for hh in range(12) for i in range(nk)]
        hops = {hh: head_ops(hh) for hh in range(12)}

        def issue_qk(idx):
            hh, i = units[idx]
            Kt, Qt, Vv, kr, qr, vr, scale = hops[hh]
            psS = P[idx % 3]
            self.mm(psS.t[:, 0:N], Kt(keytiles[i]), Qt, True, True, [kr, qr], [psS.r])

        def finalize2(hh):
            po = P[4 + hh % 2]
            rc = self.rc
            pb = P[3]
            self.mm(pb.t[0:64, 0:N], self.onesf.t[64:65, 0:64], rc.t[64:65, 0:N], True, True, [self.onesf.r, rc.r], [pb.r])
            bcs = self.bcs
            self.cp(bcs.t[:, 0:N], pb.t[0:64, 0:N], [pb.r], [bcs.r])
            self.tt(self.mixA.t[0:64, hh, qc], po.t[0:64, 0:N], bcs.t[:, 0:N], ALU.mult, [po.r, bcs.r], [self.mixA.r[hh]])

        issue_qk(0)
        if len(units) > 1:
            issue_qk(1)
        pending = []
        for idx, (hh, i) in enumerate(units):
            Kt, Qt, Vv, kr, qr, vr, scale = hops[hh]
            psS = P[idx % 3]
            pt = self.PTb[idx % 3]
            po = P[4 + hh % 2]
            self.act(pt.t[:, 0:N], psS.t[:, 0:N], AF.Exp, [psS.r], [pt.r], scale=scale)
            if idx + 2 < len(units):
                issue_qk(idx + 2)
            self.mm(po.t[0:65, 0:N], Vv(keytiles[i]), pt.t[:, 0:N], i == 0, i == nk - 1, [vr, pt.r], [po.r])
            while pending and pending[0][0] <= idx:
                finalize2(pending.pop(0)[1])
            if i == nk - 1:
                self.recip(self.rc.t[64:65, 0:N], po.t[64:65, 0:N], [po.r], [self.rc.r], exact=True)
                pending.append((idx + min(8, nk), hh))
        while pending:
            finalize2(pending.pop(0)[1])

    def layernorm(self, l, which):
        gname, bname = ("ln1_g", "ln1_b") if which == 1 else ("ln2_g", "ln2_b")
        P = self.P
        pm, pv = P[2], P[3]
        tmps = [self.lnt, self.lnr]
        for k in range(KC):
            self.mm(pm.t[:, :], self.onesf.t[:], self.xb.t[:, k, :], k == 0, k == KC - 1, [self.onesf.r, self.xb.r[k]], [pm.r])
            tq = tmps[k % 2]
            self.act(tq.t[:], self.xb.t[:, k, :], AF.Square, [self.xb.r[k]], [tq.r])
            self.mm(pv.t[:, :], self.onesf.t[:], tq.t[:], k == 0, k == KC - 1, [self.onesf.r, tq.r], [pv.r])
        m, r, t_ = self.lnm, self.lnr, self.lnt
        self.act(m.t[:], pm.t[:], AF.Copy, [pm.r], [m.r], scale=1.0 / D)
        self.tt(t_.t[:], m.t[:], m.t[:], ALU.mult, [m.r], [t_.r])
        self.stt(r.t[:], pv.t[:], 1.0 / D, t_.t[:], ALU.mult, ALU.subtract, [pv.r, t_.r], [r.r])
        self.act(r.t[:], r.t[:], AF.Ln, [r.r], [r.r], bias=EPS)
        self.act(r.t[:], r.t[:], AF.Exp, [r.r], [r.r], scale=-0.5)
        self.stt(m.t[:], m.t[:], -1.0, r.t[:], ALU.mult, ALU.mult, [m.r, r.r], [m.r])
        t2 = [self.lnt, self.gy[0], self.gy[1]]
        for k in range(KC):
            tq = t2[k % 3]
            self.tt(tq.t[:], self.xb.t[:, k, :], r.t[:], ALU.mult, [self.xb.r[k], r.r], [tq.r])
            self.tt(tq.t[:], tq.t[:], m.t[:], ALU.add, [tq.r, m.r], [tq.r], eng="pool")
            self.act(self.xb.t[:, k, :], tq.t[:], AF.Identity, [tq.r, self.par.r], [self.xb.r[k]],
                     scale=self.pcol(gname, l, k), bias=self.pcol(bname, l, k))

    def outproj_ln1(self, l, g, mcol0, mix_own=None):
        P = self.P
        mres = self.mixHG.r[mcol0 // TB]
        for dp in range(4):
            c0 = dp * 256
            slot = self.wload([
                (lambda t: t[:, 0:3584].rearrange("p (k n) -> p k n", k=14)[:, 0:2, :],
                 self.w_out[l][0:256, c0:c0 + 256].rearrange("(k p) n -> p k n", p=128)),
                (lambda t: t[0:64, 0:3584].rearrange("p (k n) -> p k n", k=14)[:, 2:14, :],
                 self.w_out[l][256:1024, c0:c0 + 256].rearrange("(h p) n -> p h n", p=64)),
            ])
            wv = slot.t[:, 0:3584].rearrange("p (k n) -> p k n", k=14)
            for dd in range(2):
                dc = dp * 2 + dd
                py = P[dc % 2]
                for kc in range(2):
                    mrhs = self.mixHG.t[:, kc, mcol0:mcol0 + TB] if mix_own is None else mix_own[:, kc, :]
                    self.mm(py.t[:, :], wv[:, kc, dd * 128:(dd + 1) * 128], mrhs, kc == 0, False,
                            [slot.r, mres] + ([self.mixHG.r[1]] if mix_own is not None else []), [py.r])
                for hh in range(12):
                    self.mm(py.t[:, :], wv[0:64, 2 + hh, dd * 128:(dd + 1) * 128], self.mixA.t[0:64, hh, :], False, hh == 11,
                            [slot.r, self.mixA.r[hh]], [py.r])
                gy = self.gy[dc % 2]
                self.act(gy.t[:], py.t[:], AF.Copy, [py.r, self.mT.r], [gy.r], scale=self.mvec(l, g, "g1", dc))
                self.stt(self.xb.t[:, dc, :], self.xb.t[:, dc, :], ALPHA, gy.t[:], ALU.mult, ALU.add, [self.xb.r[dc], gy.r], [self.xb.r[dc]])
        self.layernorm(l, 1)

    def ffn_ln2(self, l, g):
        P = self.P
        self.prepH(l, g, "2")
        first = not self.ffn_cached[l]
        self.ffn_cached[l] = True
        for fp in range(11):
            if first:
                slot = self.wload([
                    (lambda t: t[:, 0:4096].rearrange("p (k n) -> p k n", k=8)[:, :, 0:256],
                     self.w_ffn_in[l][:, fp * 256:(fp + 1) * 256].rearrange("(k p) n -> p k n", p=128)),
                    (lambda t: t[:, 0:4096].rearrange("p (k n) -> p k n", k=8)[:, :, 256:512],
                     self.w_ffn_in[l][:, DFF + fp * 256:DFF + (fp + 1) * 256].rearrange("(k p) n -> p k n", p=128)),
                ])
                self.dma("sp", self.wsc_in[l][fp * 128:(fp + 1) * 128, :], slot.t[:, 0:4096], reads=[slot.r], writes=[self.wscr[l]],
                         key=slot.r.name + "_st")
            else:
                slot = self.next_ring()
                self.dma("pool", slot.t[:, 0:4096], self.wsc_in[l][fp * 128:(fp + 1) * 128, :], reads=[self.wscr[l]], writes=[slot.r],
                         key=slot.r.name)
            wv = slot.t[:, 0:4096].rearrange("p (k n) -> p k n", k=8)
            for ff in range(2):
                f = fp * 2 + ff
                pg, pu = P[f % 2], P[2 + f % 2]
                for k in range(KC):
                    self.mm(pg.t[:, :], wv[:, k, ff * 128:(ff + 1) * 128], self.hT.t[:, k, :], k == 0, k == KC - 1, [slot.r, self.hT.r], [pg.r])
                for k in range(KC):
                    self.mm(pu.t[:, :], wv[:, k, 256 + ff * 128:256 + (ff + 1) * 128], self.hT.t[:, k, :], k == 0, k == KC - 1,
                            [slot.r, self.hT.r], [pu.r])
                sg = self.gy[f % 2]
                self.act(sg.t[:], pg.t[:], AF.Silu, [pg.r], [sg.r])
                self.tt(self.actT.t[:, f, :], sg.t[:], pu.t[:], ALU.mult, [sg.r, pu.r], [self.actT.r])
        for dc in range(KC):
            if first:
                slot = self.wload([(lambda t: t[:, 0:2816].rearrange("p (f n) -> p f n", f=NFC),
                                    self.w_ffn_out[l][:, dc * 128:(dc + 1) * 128].rearrange("(f p) n -> p f n", p=128))])
                self.dma("sp", self.wsc_out[l][dc * 128:(dc + 1) * 128, :], slot.t[:, 0:2816], reads=[slot.r], writes=[self.wscr[l]],
                         key=slot.r.name + "_st")
            else:
                slot = self.next_ring()
                self.dma("pool", slot.t[:, 0:2816], self.wsc_out[l][dc * 128:(dc + 1) * 128, :], reads=[self.wscr[l]], writes=[slot.r],
                         key=slot.r.name)
            wv = slot.t[:, 0:2816].rearrange("p (f n) -> p f n", f=NFC)
            pd = P[4 + dc % 2]
            for f in range(NFC):
                self.mm(pd.t[:, :], wv[:, f, :], self.actT.t[:, f, :], f == 0, f == NFC - 1, [slot.r, self.actT.r], [pd.r])
            gy = self.lnt if dc % 2 else self.lnr
            self.act(gy.t[:], pd.t[:], AF.Copy, [pd.r, self.mT.r], [gy.r], scale=self.mvec(l, g, "g2", dc))
            self.stt(self.xb.t[:, dc, :], self.xb.t[:, dc, :], ALPHA, gy.t[:], ALU.mult, ALU.add, [self.xb.r[dc], gy.r], [self.xb.r[dc]])
        self.layernorm(l, 2)

    def run_ctx_block(self, cb):
        self.load_x_block(self.xc[cb * TB:(cb + 1) * TB, :])
        for l in range(self.nlayers):
            self.load_layer_small(l)
            self.prepH(l, 0)
            self.phase("c%d.l%d.kv" % (cb, l))
            self.kv_pass(l, 4, 0, ctx_out=2 * cb)
            self.phase("c%d.l%d.hgf" % (cb, l))
            self.hgrn_pass(l, 0, 0, [([0, 1], True, 2 * cb), ([2, 3], True, 2 * cb + 1)])
            self.phase("c%d.l%d.hgb" % (cb, l))
            self.hgrn_pass(l, 1, 0, [([1, 0], True, 2 * cb), ([3, 2], True, 2 * cb + 1)])
            self.phase("c%d.l%d.q" % (cb, l))
            self.q_pass(l, None)
            self.phase("c%d.l%d.att" % (cb, l))
            self.attention(256, 0, [0, 1])
            self.attention(256, 256, [2, 3])
            self.phase("c%d.l%d.outp" % (cb, l))
            self.outproj_ln1(l, 0, 0)
            self.phase("c%d.l%d.ffn" % (cb, l))
            self.ffn_ln2(l, 0)
        self.store_y_block(self.y_c[cb * TB:(cb + 1) * TB, :])

    def run_sample(self):
        for blk in range(4):
            self.load_x_block(self.xsd[blk * TB:(blk + 1) * TB, :])
            self.store_xs_block(blk)
        for l in range(self.nlayers):
            self.load_layer_small(l)
            self.load_hgrn_state(l)
            self.cache_kv(l)
            for blk in range(4):
                self.load_xs_block(blk)
                self.prepH(l, 1)
                self.phase("s.l%d.b%d.kv" % (l, blk))
                self.kv_pass(l, 4, 2 + 4 * blk, rope_tile0=4 * blk)
                self.phase("s.l%d.b%d.hgf" % (l, blk))
                self.hgrn_pass(l, 0, blk * TB, [([0, 1, 2, 3], False, None)])
            for blk in range(3, -1, -1):
                self.load_xs_block(blk)
                self.prepH(l, 1)
                self.phase("s.l%d.b%d.hgb" % (l, blk))
                self.hgrn_pass(l, 1, blk * TB, [([3, 2, 1, 0], False, None)])
            last = (l == self.nlayers - 1)
            if not last:
                for qb in range(4):
                    self.load_xs_block(qb)
                    self.prepH(l, 1)
                    self.phase("s.l%d.q%d.q" % (l, qb))
                    self.q_pass(l, rope_tile0=4 * qb)
                    self.phase("s.l%d.q%d.att" % (l, qb))
                    self.attention(TB, 0, list(range(18)))
                    self.phase("s.l%d.q%d.outp" % (l, qb))
                    self.outproj_ln1(l, 1, qb * TB)
                    self.phase("s.l%d.q%d.ffn" % (l, qb))
                    self.ffn_ln2(l, 1)
                    self.store_xs_block(qb)
            else:
                for blk in range(4):
                    self.dma("sp", self.mh2[blk * 128:(blk + 1) * 128, :].rearrange("p (c n) -> p c n", c=2),
                             self.mixHG.t[:, :, blk * TB:(blk + 1) * TB], reads=[self.mixHG.r[blk]], writes=[self.mh2r], key="mh2")
                self.gather(self.mixHG.t[:, 0, 0:2 * TB], self.mh2[:, :], [self.mh2r], [self.mixHG.r[0], self.mixHG.r[1]], "mhg")
                self.gather(self.rope.t[:, 0:4].rearrange("p a b c d -> p (a b c d)"), self.rp2[:, :], [self.rp2r], [self.rope.r], "rpg")
                self.gather(self.xb.t[:].rearrange("p k n -> p (k n)"), self.xs_dram[:, :], self.xsr, [self.xb.r], "xbg")
                if "xbg" in self.dbg_out:
                    jj = self.dbg_j
                    self.dump("xbg", self.xb.t[:].rearrange("p k n -> p (k n)"), self.xb.r)
                    self.dma("sp", self.dbg_out["xbs"], self.xs_dram[jj * 128:(jj + 1) * 128, :], reads=self.xsr, key="dbgx")
                    self.dump("rpg", self.rope.t[:, 0:4].rearrange("p a b c d -> p (a b c d)"), self.rope.r)
                    self.dma("sp", self.dbg_out["rps"], self.rp2[jj * 128:(jj + 1) * 128, :], reads=[self.rp2r], key="dbgr")
                    self.dma("sp", self.dbg_out["mhs"], self.mh2[jj * 128:(jj + 1) * 128, :], reads=[self.mh2r], key="dbgm")
                    self.dma("sp", self.dbg_out["mhg"], self.mixHG.t[:, 0, 0:2 * TB], reads=[self.mixHG.r[0]], key="dbgm2")
                self.prepH(l, 1)
                self.phase("s.l%d.own.q" % l)
                self.q_pass(l, rope_tile0=0)
                self.phase("s.l%d.own.att" % l)
                self.attention(TB, 0, list(range(18)))
                self.phase("s.l%d.own.outp" % l)
                self.outproj_ln1(l, 1, 0, mix_own=self.mixHG.t[:, 0, 0:2 * TB].rearrange("p (c n) -> p c n", c=2))
                self.phase("s.l%d.own.ffn" % l)
                self.ffn_ln2(l, 1)
                self.store_y_block(self.y_own[:, :])

    def build(self):
        self.init_consts()
        self.memset(self.kst.t[:], 0.0, [self.kst.r], eng="dve")
        self.load_params()
        self.ada()
        self.phase("start")
        if self.do_ctx:
            for cb in range(2):
                self.run_ctx_block(cb)
        self.phase("sample_load")
        if self.do_sample:
            self.run_sample()
        self.phase("end")
        self.S.emit()
        return self.nc


_WKEYS = ["w_ada", "b_ada", "w_in", "hg_lb", "hg_norm", "mla_q_norm", "mla_w_uq", "mla_kv_norm", "mla_w_ukv",
          "gqa_q_norm", "gqa_k_norm", "w_out", "ln1_g", "ln1_b", "w_ffn_in", "w_ffn_out", "ln2_g", "ln2_b"]


def make_in_maps(inp):
    f = lambda a: np.ascontiguousarray(np.asarray(a), dtype=np.float32)
    shared = {k: f(inp[k]) for k in _WKEYS}
    maps = []
    for c in range(8):
        b = c // 4
        m = dict(shared)
        m["xc"] = f(inp["x_prompt"][4 * c:4 * c + 4]).reshape(1024, D)
        m["xs"] = f(inp["x_sample"][b])
        m["c_ckv"] = f(inp["cache_mla_ckv"][b])
        m["c_kpe"] = f(inp["cache_mla_kpe"][b])
        m["c_gk"] = f(inp["cache_gqa_k"][b]).reshape(2, 256, 128)
        m["c_gv"] = f(inp["cache_gqa_v"][b]).reshape(2, 256, 128)
        m["s_hg"] = f(inp["state_hgrn"][b]).reshape(2, 2, 256, 64)
        m["cvec"] = np.stack([f(inp["c_ctx"]), f(inp["c"][b])])
        m["gidx"] = ((c % 4) * 128 + np.arange(128, dtype=np.int32)).reshape(128, 1)
        maps.append(m)
    return maps


def assemble(results):
    y_prompt = np.concatenate([r["y_c"].reshape(4, 256, D) for r in results], axis=0)
    y_sample = np.stack([np.concatenate([results[4 * b + j]["y_own"] for j in range(4)], axis=0) for b in range(2)], axis=0)
    st_ckv = np.concatenate([r["o_ckv"] for r in results], axis=0)
    st_kpe = np.concatenate([r["o_kpe"] for r in results], axis=0)
    st_gk = np.concatenate([r["o_gk"].reshape(4, 2, 256, 2, 64) for r in results], axis=0)
    st_gv = np.concatenate([r["o_gv"].reshape(4, 2, 256, 2, 64) for r in results], axis=0)
    st_hg = np.concatenate([r["o_hg"].reshape(4, 2, 2, 4, 64, 64) for r in results], axis=0)
    return tuple(np.ascontiguousarray(a, dtype=np.float32) for a in (y_prompt, y_sample, st_ckv, st_kpe, st_gk, st_gv, st_hg))


def kernel(**inputs):
    nc = Builder().build()
    res = run_bass_kernel_spmd(nc, make_in_maps(inputs), core_ids=list(range(8)))
    return assemble(res.results)
```
